# Optimizing a Trainium2 kernel written in Bass

```python
import math
import jax, jax.numpy as jnp
from jax import lax
import numpy as np

D_MODEL = 1024
BATCH = 4
SEQ = 8192
DEPTH = 2

HEAD_DIM = 64
FOX_HEADS = 8
RET_HEADS = 8
FOX_WIDTH = FOX_HEADS * HEAD_DIM
RET_WIDTH = RET_HEADS * HEAD_DIM
EVEN_MIX_WIDTH = FOX_WIDTH + RET_WIDTH
FOX_Q_BLOCK = 128
RET_CHUNK = 128
EVEN_SIZES = (FOX_WIDTH, FOX_WIDTH, FOX_WIDTH, FOX_HEADS, RET_WIDTH, RET_WIDTH, RET_WIDTH, EVEN_MIX_WIDTH)
EVEN_IN = sum(EVEN_SIZES)
EVEN_SPLITS = tuple(int(v) for v in np.cumsum(EVEN_SIZES)[:-1])
NSA_HEADS = 16
NSA_KV_GROUPS = 4
NSA_HEADS_PER_GROUP = NSA_HEADS // NSA_KV_GROUPS
NSA_WIDTH = NSA_HEADS * HEAD_DIM
NSA_KV_WIDTH = NSA_KV_GROUPS * HEAD_DIM
N_BRANCH = 3
CMP_BLOCK = 32
CMP_STRIDE = 16
CMP_HIDDEN = 256
SLC_BLOCK = 64
SLC_TOPK = 16
WINDOW = 512
NSA_Q_BLOCK = 64
ODD_SIZES = (NSA_WIDTH,) + (NSA_KV_WIDTH,) * 6 + (NSA_HEADS * N_BRANCH, NSA_WIDTH)
ODD_IN = sum(ODD_SIZES)
ODD_SPLITS = tuple(int(v) for v in np.cumsum(ODD_SIZES)[:-1])

RMS_EPS = 1e-6
GN_EPS = 1e-5
NEG = -1e30
FORCE_BONUS = 1e6

kernel_name = "fox_retnet_nsa_hybrid"


def rms_norm(x, g):
    xf = x.astype(jnp.float32)
    y = xf * lax.rsqrt(jnp.mean(xf * xf, axis=-1, keepdims=True) + RMS_EPS)
    return (y * g.astype(jnp.float32)).astype(x.dtype)


def masked_softmax(s, mask):
    s = jnp.where(mask, s, NEG)
    p = jax.nn.softmax(s, axis=-1)
    return jnp.where(mask, p, 0.0)


def alibi_slopes(n):
    return jnp.asarray(2.0 ** (-8.0 * (np.arange(n) + 1) / n), jnp.float32)


def fox_attention(q, k, v, f_logit):
    B, S, H, d = q.shape
    nb = S // FOX_Q_BLOCK
    scale = d ** -0.5
    c = jnp.cumsum(jax.nn.log_sigmoid(f_logit.astype(jnp.float32)), axis=1)
    c_h = c.transpose(0, 2, 1)
    kh = k.transpose(0, 2, 1, 3)
    vh = v.transpose(0, 2, 1, 3)
    qb = q.reshape(B, nb, FOX_Q_BLOCK, H, d).transpose(1, 0, 3, 2, 4)
    cb = c.reshape(B, nb, FOX_Q_BLOCK, H).transpose(1, 0, 3, 2)
    key_pos = jnp.arange(S)

    def block(args):
        qi, ci, bi = args
        t = bi * FOX_Q_BLOCK + jnp.arange(FOX_Q_BLOCK)
        s = jnp.einsum('bhqd,bhkd->bhqk', qi, kh).astype(jnp.float32) * scale
        s = s + ci[..., None] - c_h[:, :, None, :]
        p = masked_softmax(s, key_pos[None, :] <= t[:, None])
        return jnp.einsum('bhqk,bhkd->bhqd', p.astype(vh.dtype), vh)

    o = lax.map(block, (qb, cb, jnp.arange(nb)))
    return o.transpose(1, 0, 3, 2, 4).reshape(B, S, H, d)


def retention_decays(n_heads, chunk):
    lg = np.log(1.0 - 2.0 ** (-5.0 - np.arange(n_heads)))
    i = np.arange(chunk)
    diff = i[:, None] - i[None, :]
    inner = np.where(diff[None] >= 0, np.exp(lg[:, None, None] * np.maximum(diff, 0)[None]), 0.0)
    cross = np.exp(lg[:, None] * (i[None, :] + 1))
    kdec = np.exp(lg[:, None] * (chunk - 1 - i)[None, :])
    cdec = np.exp(lg * chunk)
    return (jnp.asarray(inner, jnp.float32), jnp.asarray(cross, jnp.float32),
            jnp.asarray(kdec, jnp.float32), jnp.asarray(cdec, jnp.float32))


def retention(q, k, v):
    B, S, H, d = q.shape
    n = S // RET_CHUNK
    inner, cross, kdec, cdec = retention_decays(H, RET_CHUNK)

    def chunks(a):
        return a.astype(jnp.float32).reshape(B, n, RET_CHUNK, H, d).transpose(1, 0, 3, 2, 4)

    qc, kc, vc = chunks(q), chunks(k) * (d ** -0.5), chunks(v)

    def step(state, xs):
        qi, ki, vi = xs
        a = jnp.einsum('bhid,bhjd->bhij', qi, ki) * inner
        o = jnp.einsum('bhij,bhje->bhie', a, vi) + jnp.einsum('bhid,bhde->bhie', qi, state) * cross[None, :, :, None]
        state = state * cdec[None, :, None, None] + jnp.einsum('bhjd,bhje->bhde', ki * kdec[None, :, :, None], vi)
        return state, o

    _, o = lax.scan(step, jnp.zeros((B, H, d, d), jnp.float32), (qc, kc, vc))
    o = o.transpose(1, 0, 3, 2, 4).reshape(B, S, H, d)
    mu = jnp.mean(o, axis=-1, keepdims=True)
    var = jnp.mean(jnp.square(o - mu), axis=-1, keepdims=True)
    return ((o - mu) * lax.rsqrt(var + GN_EPS)).reshape(B, S, H * d)


def even_layer(x, norm_g, w_in, b_f, gn_g, w_out):
    B, S, _ = x.shape
    h = rms_norm(x, norm_g)
    u = h @ w_in
    q_f, k_f, v_f, f_l, q_r, k_r, v_r, z = jnp.split(u, EVEN_SPLITS, axis=-1)
    hd = lambda a, n: a.reshape(B, S, n, HEAD_DIM)
    o_f = fox_attention(hd(q_f, FOX_HEADS), hd(k_f, FOX_HEADS), hd(v_f, FOX_HEADS), f_l + b_f)
    o_f = o_f.reshape(B, S, FOX_WIDTH)
    o_r = (retention(hd(q_r, RET_HEADS), hd(k_r, RET_HEADS), hd(v_r, RET_HEADS)) * gn_g).astype(x.dtype)
    y = jnp.concatenate([o_f, o_r], axis=-1) * jax.nn.silu(z)
    return x + y @ w_out


def compress_blocks(a, idx, pe, w1, w2):
    B = a.shape[0]
    nc, l = idx.shape
    ab = a[:, idx] + pe[None, None, :, None, :]
    ab = ab.transpose(0, 1, 3, 2, 4).reshape(B, nc, NSA_KV_GROUPS, l * HEAD_DIM)
    return jax.nn.silu(ab @ w1) @ w2


def nsa_attention(q, kc, vc, ks, vs, kw, vw, gate_logit, pe_k, pe_v, wk1, wk2, wv1, wv2):
    B, S, H, d = q.shape
    G, Hg, QB = NSA_KV_GROUPS, NSA_HEADS_PER_GROUP, NSA_Q_BLOCK
    scale = d ** -0.5
    dtype = q.dtype
    nc = (S - CMP_BLOCK) // CMP_STRIDE + 1
    idx = np.arange(nc)[:, None] * CMP_STRIDE + np.arange(CMP_BLOCK)[None, :]
    cmp_end = jnp.asarray(idx[:, -1], jnp.int32)
    k_cmp = compress_blocks(kc, idx, pe_k, wk1, wk2)
    v_cmp = compress_blocks(vc, idx, pe_v, wv1, wv2)
    ns = S // SLC_BLOCK
    topk = min(SLC_TOPK, ns)
    c0 = np.arange(nc)[:, None] * CMP_STRIDE
    s0 = np.arange(ns)[None, :] * SLC_BLOCK
    overlap = np.clip(np.minimum(c0 + CMP_BLOCK, s0 + SLC_BLOCK) - np.maximum(c0, s0), 0, None)
    cmp_to_slc = jnp.asarray(overlap / CMP_STRIDE, jnp.float32)
    slopes = alibi_slopes(H).reshape(G, Hg)[None, :, :, None, None]
    ks_g = ks.transpose(0, 2, 1, 3)
    vs_g = vs.transpose(0, 2, 1, 3)
    kw_pad = jnp.pad(kw, ((0, 0), (WINDOW, 0), (0, 0), (0, 0)))
    vw_pad = jnp.pad(vw, ((0, 0), (WINDOW, 0), (0, 0), (0, 0)))
    nqb = S // QB
    qb = q.reshape(B, nqb, QB, G, Hg, d).transpose(1, 0, 2, 3, 4, 5)
    gb = gate_logit.reshape(B, nqb, QB, G, Hg, N_BRANCH).transpose(1, 0, 2, 3, 4, 5)
    b_idx = jnp.arange(B)[:, None, None, None]
    g_idx = jnp.arange(G)[None, :, None, None]
    blk = jnp.arange(ns)

    def block(args):
        qi, gi, bi = args
        t = bi * QB + jnp.arange(QB)
        s = jnp.einsum('bqghd,bcgd->bghqc', qi, k_cmp).astype(jnp.float32) * scale
        s = s - slopes * (t[:, None] - cmp_end[None, :]).astype(jnp.float32)
        p_cmp = masked_softmax(s, cmp_end[None, :] <= t[:, None])
        o_cmp = jnp.einsum('bghqc,bcgd->bqghd', p_cmp.astype(dtype), v_cmp)
        imp = p_cmp.sum(axis=2) @ cmp_to_slc
        cur = t // SLC_BLOCK
        valid = blk[None, :] * SLC_BLOCK <= t[:, None]
        forced = (blk[None, :] == 0) | (blk[None, :] == cur[:, None]) | (blk[None, :] == cur[:, None] - 1)
        score = jnp.where(valid, imp + jnp.where(forced, FORCE_BONUS, 0.0), NEG)
        _, sel = lax.top_k(score, topk)
        tok = (sel[..., None] * SLC_BLOCK + jnp.arange(SLC_BLOCK)).reshape(B, G, QB, topk * SLC_BLOCK)
        k_sel = ks_g[b_idx, g_idx, tok]
        v_sel = vs_g[b_idx, g_idx, tok]
        s = jnp.einsum('bqghd,bgqld->bghql', qi, k_sel).astype(jnp.float32) * scale
        s = s - slopes * (t[None, None, :, None] - tok).astype(jnp.float32)[:, :, None]
        p = masked_softmax(s, (tok <= t[None, None, :, None])[:, :, None])
        o_slc = jnp.einsum('bghql,bgqld->bqghd', p.astype(dtype), v_sel)
        start = bi * QB
        k_win = lax.dynamic_slice_in_dim(kw_pad, start, QB + WINDOW, axis=1)
        v_win = lax.dynamic_slice_in_dim(vw_pad, start, QB + WINDOW, axis=1)
        sp = start - WINDOW + jnp.arange(QB + WINDOW)
        wmask = (sp[None, :] <= t[:, None]) & (sp[None, :] > t[:, None] - WINDOW) & (sp[None, :] >= 0)
        s = jnp.einsum('bqghd,bkgd->bghqk', qi, k_win).astype(jnp.float32) * scale
        s = s - slopes * (t[:, None] - sp[None, :]).astype(jnp.float32)
        p = masked_softmax(s, wmask)
        o_win = jnp.einsum('bghqk,bkgd->bqghd', p.astype(dtype), v_win)
        g = jax.nn.sigmoid(gi.astype(jnp.float32)).astype(dtype)
        return g[..., 0:1] * o_cmp + g[..., 1:2] * o_slc + g[..., 2:3] * o_win

    o = lax.map(block, (qb, gb, jnp.arange(nqb)))
    return o.transpose(1, 0, 2, 3, 4, 5).reshape(B, S, H * d)


def odd_layer(x, norm_g, w_in, b_gate, pe_k, pe_v, wk1, wk2, wv1, wv2, w_out):
    B, S, _ = x.shape
    h = rms_norm(x, norm_g)
    u = h @ w_in
    q, kc, vc, ks, vs, kw, vw, gl, z = jnp.split(u, ODD_SPLITS, axis=-1)
    kv = lambda a: a.reshape(B, S, NSA_KV_GROUPS, HEAD_DIM)
    gl = (gl + b_gate).reshape(B, S, NSA_HEADS, N_BRANCH)
    o = nsa_attention(q.reshape(B, S, NSA_HEADS, HEAD_DIM), kv(kc), kv(vc), kv(ks), kv(vs), kv(kw), kv(vw),
                      gl, pe_k, pe_v, wk1, wk2, wv1, wv2)
    return x + (o * jax.nn.silu(z)) @ w_out


def setup_inputs(seed: int = 0) -> dict:
    key = jax.random.key(seed)
    ks = jax.random.split(key, 24)
    ne = (DEPTH + 1) // 2
    no = DEPTH // 2
    nrm = lambda k, shape, sc: jax.random.normal(k, shape, jnp.float32) * sc
    fan_cmp = CMP_BLOCK * HEAD_DIM
    return {
        "x": nrm(ks[0], (BATCH, SEQ, D_MODEL), 1.0),
        "even_norm_g": 1.0 + nrm(ks[1], (ne, D_MODEL), 0.05),
        "even_w_in": nrm(ks[2], (ne, D_MODEL, EVEN_IN), D_MODEL ** -0.5),
        "even_b_f": 1.0 + nrm(ks[3], (ne, FOX_HEADS), 0.1),
        "even_gn_g": 1.0 + nrm(ks[4], (ne, RET_WIDTH), 0.05),
        "even_w_out": nrm(ks[5], (ne, EVEN_MIX_WIDTH, D_MODEL), EVEN_MIX_WIDTH ** -0.5),
        "odd_norm_g": 1.0 + nrm(ks[6], (no, D_MODEL), 0.05),
        "odd_w_in": nrm(ks[7], (no, D_MODEL, ODD_IN), D_MODEL ** -0.5),
        "odd_b_gate": nrm(ks[8], (no, NSA_HEADS * N_BRANCH), 0.1),
        "odd_pe_k": nrm(ks[9], (no, CMP_BLOCK, HEAD_DIM), 0.1),
        "odd_pe_v": nrm(ks[10], (no, CMP_BLOCK, HEAD_DIM), 0.1),
        "odd_wk1": nrm(ks[11], (no, fan_cmp, CMP_HIDDEN), fan_cmp ** -0.5),
        "odd_wk2": nrm(ks[12], (no, CMP_HIDDEN, HEAD_DIM), CMP_HIDDEN ** -0.5),
        "odd_wv1": nrm(ks[13], (no, fan_cmp, CMP_HIDDEN), fan_cmp ** -0.5),
        "odd_wv2": nrm(ks[14], (no, CMP_HIDDEN, HEAD_DIM), CMP_HIDDEN ** -0.5),
        "odd_w_out": nrm(ks[15], (no, NSA_WIDTH, D_MODEL), NSA_WIDTH ** -0.5),
        "final_g": 1.0 + nrm(ks[16], (D_MODEL,), 0.05),
    }


def reference(x, even_norm_g, even_w_in, even_b_f, even_gn_g, even_w_out,
              odd_norm_g, odd_w_in, odd_b_gate, odd_pe_k, odd_pe_v, odd_wk1, odd_wk2, odd_wv1, odd_wv2, odd_w_out,
              final_g):
    for layer in range(DEPTH):
        i = layer // 2
        if layer % 2 == 0:
            x = even_layer(x, even_norm_g[i], even_w_in[i], even_b_f[i], even_gn_g[i], even_w_out[i])
        else:
            x = odd_layer(x, odd_norm_g[i], odd_w_in[i], odd_b_gate[i], odd_pe_k[i], odd_pe_v[i],
                          odd_wk1[i], odd_wk2[i], odd_wv1[i], odd_wv2[i], odd_w_out[i])
    return rms_norm(x, final_g)
```

```python
import numpy as np
import ml_dtypes
from contextlib import ExitStack
import concourse.bass as bass
import concourse.mybir as mybir
from concourse.bass_utils import run_bass_kernel_spmd

F32 = mybir.dt.float32
BF16 = mybir.dt.bfloat16
AF = mybir.ActivationFunctionType
ALU = mybir.AluOpType
AX = mybir.AxisListType
NPBF = ml_dtypes.bfloat16

SEQ = 8192
BATCH = 4
DM = 1024
NEGM = -30000.0
FLAT = False
DBG = {}


class Sub(tuple):
    pass


def sub(base, idx):
    return Sub((base, idx))


class Prog:
    ENG = ("pe", "act", "dve", "pool", "sp")

    def __init__(self, nc, stack):
        self.nc = nc
        self.stack = stack
        self.top = stack
        self.streams = {e: [] for e in self.ENG}
        self.esem = {e: stack.enter_context(nc.semaphore("es_" + e)) for e in self.ENG}
        self.ecnt = {e: 0 for e in self.ENG}
        self.waited = {e: {} for e in self.ENG}
        self.lastw = {}
        self.readers = {}
        self.slots = {}
        self.nops = 0

    def _uniq(self, name):
        self._names = getattr(self, "_names", {})
        n = self._names.get(name, 0)
        self._names[name] = n + 1
        return name if n == 0 else "%s_v%d" % (name, n)

    def sb(self, name, shape, dt):
        return self.stack.enter_context(self.nc.sbuf_tensor(self._uniq(name), list(shape), dt))

    def ps(self, name, shape, dt):
        return self.stack.enter_context(self.nc.psum_tensor(self._uniq(name), list(shape), dt))

    def _deps(self, reads, writes):
        toks = []
        subs = self._subs = getattr(self, "_subs", {})

        def w_of(k):
            t = self.lastw.get(k)
            if t is not None:
                toks.append(t)

        def r_of(k):
            r = self.readers.get(k)
            if r:
                toks.extend(r.items())

        for k in reads:
            w_of(k)
            if isinstance(k, Sub):
                subs.setdefault(k[0], set()).add(k)
                w_of(k[0])
            else:
                for sk_ in subs.get(k, ()):
                    w_of(sk_)
        for k in writes:
            w_of(k)
            r_of(k)
            if isinstance(k, Sub):
                subs.setdefault(k[0], set()).add(k)
                w_of(k[0])
                r_of(k[0])
            else:
                for sk_ in subs.get(k, ()):
                    w_of(sk_)
                    r_of(sk_)
        return toks

    def _waits(self, eng, toks):
        need = {}
        for sk, v in toks:
            if eng == "pe" and sk == ("e", "pe"):
                continue
            if v > need.get(sk, 0):
                need[sk] = v
        out = []
        w = self.waited[eng]
        for sk, v in need.items():
            if w.get(sk, 0) >= v:
                continue
            w[sk] = v
            out.append((sk, v))
        return out

    def _commit(self, tok, reads, writes):
        sk, v = tok
        for k in writes:
            self.lastw[k] = tok
            self.readers[k] = {}
        for k in reads:
            d = self.readers.setdefault(k, {})
            if v > d.get(sk, 0):
                d[sk] = v

    LIMIT = None

    def op(self, eng, fn, reads=(), writes=()):
        if self.LIMIT is not None and self.nops >= self.LIMIT:
            return
        toks = self._deps(reads, writes)
        waits = self._waits(eng, toks)
        self.ecnt[eng] += 1
        sk = ("e", eng)
        tok = (sk, self.ecnt[eng])
        self.streams[eng].append((waits, fn, sk, 1))
        self._commit(tok, reads, writes)
        self.nops += 1

    QMAP = {}
    slot_prefix = ""

    def _slot(self, eng, slot):
        self._slotmap = getattr(self, "_slotmap", {})
        self._nexti = getattr(self, "_nexti", {})
        key = (eng, slot)
        if key not in self._slotmap:
            i = self._nexti.get(eng, 0)
            self._nexti[eng] = i + 1
            pk = "%s%d" % (eng, i)
            if pk not in self.slots:
                self.slots[pk] = [self.top.enter_context(self.nc.semaphore("ds_" + pk)), 0]
            self._slotmap[key] = pk
        return self._slotmap[key]

    def dma(self, eng, out, in_, reads=(), writes=(), slot=None, **kw):
        assert slot is not None
        if self.LIMIT is not None and self.nops >= self.LIMIT:
            return
        eng = self.QMAP.get(eng, eng)
        slot = self._slot(eng, slot)
        toks = self._deps(reads, writes)
        waits = self._waits(eng, toks)
        s = self.slots[slot]
        s[1] += 16
        sk = ("s", slot)
        tok = (sk, s[1])
        self.streams[eng].append((waits, (lambda e, o=out, i=in_, kw=kw: e.dma_start(out=o, in_=i, **kw)), sk, 16))
        self._commit(tok, reads, writes)
        self.nops += 1

    def coll(self, eng, kind, in_ap, out_ap, groups, reads=(), writes=(), slot=None):
        slot = self._slot(eng, slot)
        toks = self._deps(reads, writes)
        waits = self._waits(eng, toks)
        s = self.slots[slot]
        s[1] += 16
        sk = ("s", slot)
        self.streams[eng].append((waits, (lambda e, i=in_ap, o=out_ap: e.collective_compute(kind, ALU.bypass, replica_groups=groups, ins=[i], outs=[o])), sk, 16))
        self._commit((sk, s[1]), reads, writes)
        self.nops += 1

    def _sem(self, sk):
        return self.esem[sk[1]] if sk[0] == "e" else self.slots[sk[1]][0]

    def barrier(self):
        toks = [(("e", e), self.ecnt[e]) for e in self.ENG if self.ecnt[e] > 0]
        toks += [(("s", s), v[1]) for s, v in self.slots.items() if v[1] > 0]
        for e in self.ENG:
            waits = self._waits(e, toks)
            self.streams[e].append((waits, None, None, 0))
        self._slotmap, self._nexti = {}, {}

    def final_wait(self, eng, keys):
        toks = [self.lastw[k] for k in keys if k in self.lastw]
        waits = self._waits(eng, toks)
        self.streams[eng].append((waits, None, None, 0))

    def emit(self):
        nc = self.nc
        awaited = {e: set() for e in self.ENG}
        for name in self.ENG:
            for waits, fn, sk, inc in self.streams[name]:
                for wsk, v in waits:
                    if wsk[0] == "e":
                        awaited[wsk[1]].add(v)
        rank = {e: {v: i + 1 for i, v in enumerate(sorted(awaited[e]))} for e in self.ENG}
        with nc.Block() as block:
            def mk(name):
                def body(e):
                    n = 0
                    for waits, fn, sk, inc in self.streams[name]:
                        for wsk, v in waits:
                            if wsk[0] == "e":
                                e.wait_ge(self.esem[wsk[1]], rank[wsk[1]][v])
                            else:
                                e.wait_ge(self.slots[wsk[1]][0], v)
                        if fn is not None:
                            ins = fn(e)
                            if sk[0] == "e":
                                n += 1
                                if n in rank[name]:
                                    ins.then_inc(self.esem[name], 1)
                            else:
                                ins.then_inc(self.slots[sk[1]][0], inc)
                return body
            block.tensor(mk("pe"))
            block.scalar(mk("act"))
            block.vector(mk("dve"))
            block.gpsimd(mk("pool"))
            block.sync(mk("sp"))


class Rot:
    def __init__(self, P, name, n, shape, dt, psum=False):
        self.items = []
        for i in range(n):
            if name == DBG.get("padname") and i == 1:
                P.sb(name + "_pad", [128, DBG.get("pad", 512)], F32)
            t = P.ps(f"{name}{i}", shape, dt) if psum else P.sb(f"{name}{i}", shape, dt)
            self.items.append((t, f"{name}{i}"))
        self.i = 0

    def next(self):
        it = self.items[self.i % len(self.items)]
        self.i += 1
        return it


class Ctx:
    def __init__(self, nc, P, idb):
        self.nc, self.P, self.idb = nc, P, idb
        self.scratch = {}


def _din(nc, cx, sfx, name, shape, dt=F32):
    return nc.dram_tensor(name + sfx, list(shape), dt, kind="ExternalInput").ap()


def _dsc(nc, cx, name, shape, dt):
    if cx is None:
        return nc.dram_tensor(name, list(shape), dt, kind="Internal").ap()
    if name not in cx.scratch:
        cx.scratch[name] = nc.dram_tensor(name, list(shape), dt, kind="Internal").ap()
    return cx.scratch[name]

def load_consts(P, ident_ap):
    idf = P.sb("idf", [128, 128], F32)
    idb = P.sb("idb", [128, 128], BF16)
    P.dma("sp", idf[:], ident_ap, writes=["idf"], slot="idf")
    P.op("dve", lambda e: e.tensor_copy(out=idb[:], in_=idf[:]), reads=["idf"], writes=["idb"])
    return idb


def load_weight_bf16(P, wb, w_ap, ncols, name, chunk=256):
    stg = Rot(P, name + "_stg", 2, [128, 8, chunk], F32)
    wv = w_ap.rearrange("(c p) n -> p c n", p=128)
    for i, c0 in enumerate(range(0, ncols, chunk)):
        cw = min(chunk, ncols - c0)
        t, k = stg.next()
        P.dma("sp" if i % 2 == 0 else "pool", t[:, :, 0:cw], wv[:, :, c0:c0 + cw], writes=[k], slot=k)
        eng = "dve" if i % 2 == 0 else "pool"
        P.op(eng, lambda e, t=t, c0=c0, cw=cw: e.tensor_copy(out=wb[:, :, c0:c0 + cw], in_=t[:, :, 0:cw]),
             reads=[k], writes=[(name, c0)])
    return [(name, c0) for c0 in range(0, ncols, chunk)]


class NormT:
    def __init__(self, P, idb, gcol_ap):
        self.P = P
        self.idb = idb
        self.xt = Rot(P, "nx", 4, [128, 1024], F32)
        self.sq = P.sb("nsq", [128, 1024], F32)
        self.st = Rot(P, "nst", 4, [128, 4], F32)
        self.hb = Rot(P, "nhb", 4, [128, 1024], BF16)
        self.pT = Rot(P, "npT", 4, [128, 8, 128], BF16, psum=True)
        self.gs = P.sb("ngs", [128, 8], F32)
        self.gB = P.sb("ngB", [128, 8, 128], F32)
        self.mh = P.sb("nmh", [128, 1], F32)
        P.dma("sp", self.gs[:], gcol_ap, writes=["ngs"], slot="ngs")
        P.op("pool", lambda e: e.memset(self.gB[:], 1.0), writes=["ngB"])
        P.op("pool", lambda e: e.memset(self.mh[:], -0.5), writes=["nmh"])
        for c in range(8):
            P.op("dve", lambda e, c=c: e.tensor_scalar(out=self.gB[:, c, :], in0=self.gB[:, c, :], scalar1=self.gs[:, c:c + 1],
                                                       scalar2=None, op0=ALU.mult), reads=["ngs", "ngB"], writes=["ngB"])

    def load(self, x_rows_ap, xkey_reads=()):
        P = self.P
        xt, xk = self.xt.next()
        P.dma("sp", xt[:], x_rows_ap, reads=list(xkey_reads), writes=[xk], slot=xk)
        return xt, xk

    def run4(self, xs, hT, hk):
        P = self.P
        bufs = [(self.st.next(), self.hb.next(), self.pT.next()) for _ in range(4)]
        for tt in range(4):
            (xt, xk), ((st, sk), (hb, hbk), _) = xs[tt], bufs[tt]
            P.op("dve", lambda e, xt=xt, st=st, hb=hb: e.scalar_tensor_tensor(out=hb[:], in0=xt[:], scalar=1.0, in1=xt[:], op0=ALU.mult, op1=ALU.mult,
                                                                             accum_out=st[:, 0:1]), reads=[xk], writes=[hbk, sk])
        for tt in range(4):
            (st, sk) = bufs[tt][0]
            P.op("dve", lambda e, st=st: e.tensor_scalar(out=st[:, 1:2], in0=st[:, 0:1], scalar1=1.0 / 1024, scalar2=1e-6, op0=ALU.mult, op1=ALU.add),
                 reads=[sk], writes=[sk])
        for tt in range(4):
            (st, sk) = bufs[tt][0]
            P.op("pool", lambda e, st=st: e.tensor_tensor(out=st[:, 2:3], in0=st[:, 1:2], in1=self.mh[:], op=ALU.pow), reads=[sk, "nmh"], writes=[sk])
        for tt in range(4):
            (xt, xk), ((st, sk), (hb, hbk), _) = xs[tt], bufs[tt]
            P.op("dve", lambda e, xt=xt, st=st, hb=hb: e.tensor_scalar(out=hb[:], in0=xt[:], scalar1=st[:, 2:3], scalar2=None, op0=ALU.mult),
                 reads=[xk, sk], writes=[hbk])
        for tt in range(4):
            (hb, hbk), (pT, pk) = bufs[tt][1], bufs[tt][2]
            for c in range(8):
                P.op("pe", lambda e, c=c, hb=hb, pT=pT: e.transpose(out=pT[:, c, :], in_=hb[:, c * 128:(c + 1) * 128], identity=self.idb[:]),
                     reads=[hbk, "idb"], writes=[pk])
        for tt in range(4):
            (pT, pk) = bufs[tt][2]
            P.op("dve", lambda e, pT=pT, tt=tt: e.tensor_tensor(out=hT[:, :, tt * 128:(tt + 1) * 128], in0=pT[:], in1=self.gB[:], op=ALU.mult),
                 reads=[pk, "ngB"], writes=[sub(hk, tt)])

    def run(self, xt, xk, hT_out_ap, hT_key):
        P = self.P
        st, sk = self.st.next()
        hb, hk = self.hb.next()
        pT, pk = self.pT.next()
        sq = self.sq
        P.op("dve", lambda e: e.scalar_tensor_tensor(out=sq[:], in0=xt[:], scalar=1.0, in1=xt[:], op0=ALU.mult, op1=ALU.mult,
                                                     accum_out=st[:, 0:1]), reads=[xk], writes=["nsq", sk])
        P.op("dve", lambda e: e.tensor_scalar(out=st[:, 1:2], in0=st[:, 0:1], scalar1=1.0 / 1024, scalar2=1e-6, op0=ALU.mult,
                                              op1=ALU.add), reads=[sk], writes=[sk])
        P.op("pool", lambda e: e.tensor_tensor(out=st[:, 2:3], in0=st[:, 1:2], in1=self.mh[:], op=ALU.pow),
             reads=[sk, "nmh"], writes=[sk])
        P.op("dve", lambda e: e.tensor_scalar(out=hb[:], in0=xt[:], scalar1=st[:, 2:3], scalar2=None, op0=ALU.mult),
             reads=[xk, sk], writes=[hk])
        for c in range(8):
            P.op("pe", lambda e, c=c: e.transpose(out=pT[:, c, :], in_=hb[:, c * 128:(c + 1) * 128], identity=self.idb[:]),
                 reads=[hk, "idb"], writes=[pk])
        P.op("dve", lambda e: e.tensor_tensor(out=hT_out_ap, in0=pT[:], in1=self.gB[:], op=ALU.mult),
             reads=[pk, "ngB"], writes=[hT_key])


class AttnCore:
    def __init__(self, P, idb, nO=2, skew=2, wide=False):
        self.P = P
        self.idb = idb
        self.skew = skew
        self.wide = wide
        W = 1024 if wide else 512
        self.psS = Rot(P, "aS", skew + 1, [128, W], F32, psum=True)
        self.psO = Rot(P, "aO", nO, [128, 512], F32, psum=True)
        self.pT = Rot(P, "aP", skew + 2, [128, W], BF16)

    def group(self, QT, qkeys, KT, kkeys, kd, q0, tiles, v_of, pv_extra=None):
        P = self.P
        SKEW = self.skew
        psO, ok = self.psO.next()
        last_for_qs = {}
        for ti, t in enumerate(tiles):
            for qs in range(t["qs_min"], t.get("qs_max", 4)):
                last_for_qs[qs] = ti
        full = lambda t: t["qs_min"] == 0 and t.get("qs_max", 4) == 4
        units = []
        ti = 0
        while ti < len(tiles):
            if self.wide and ti + 1 < len(tiles) and full(tiles[ti]) and full(tiles[ti + 1]):
                units.append([ti, ti + 1])
                ti += 2
            else:
                units.append([ti])
                ti += 1
        n = len(units)
        pend = {}

        def emit_scores(ui):
            psS, sk = self.psS.next()
            pT, pk = self.pT.next()
            for u, ti in enumerate(units[ui]):
                t = tiles[ti]
                j = t["j"]
                ex = t["extras"]
                o = 512 * u
                c0, c1 = 128 * t["qs_min"], 128 * t.get("qs_max", 4)
                P.op("pe", lambda e, psS=psS, j=j, nx=len(ex), c0=c0, c1=c1, o=o: e.matmul(psS[:, o + c0:o + c1], lhsT=KT[0:kd, j * 128:(j + 1) * 128],
                                                                                         rhs=QT[0:kd, q0 + c0:q0 + c1], start=True, stop=(nx == 0)),
                     reads=list(qkeys) + list(kkeys), writes=[sk])
                for xi, (la, ra, xk) in enumerate(ex):
                    P.op("pe", lambda e, psS=psS, la=la, ra=ra, last=(xi == len(ex) - 1), c0=c0, c1=c1, o=o: e.matmul(psS[:, o + c0:o + c1], lhsT=la, rhs=ra[:, c0:c1],
                                                                                                                 start=False, stop=last),
                         reads=list(xk), writes=[sk])
            if len(units[ui]) == 2:
                P.op("act", lambda e, psS=psS, pT=pT: e.activation(out=pT[:, 0:1024], in_=psS[:, 0:1024], func=AF.Exp), reads=[sk], writes=[pk])
            else:
                t = tiles[units[ui][0]]
                c0, c1 = 128 * t["qs_min"], 128 * t.get("qs_max", 4)
                P.op("act", lambda e, psS=psS, pT=pT, c0=c0, c1=c1: e.activation(out=pT[:, c0:c1], in_=psS[:, c0:c1], func=AF.Exp), reads=[sk], writes=[pk])
            pend[ui] = (pT, pk)

        def emit_pv(ui):
            pT, pk = pend.pop(ui)
            for u, ti in enumerate(units[ui]):
                t = tiles[ti]
                o = 512 * u
                va, vk = v_of(t["j"])
                for qs in range(t["qs_min"], t.get("qs_max", 4)):
                    P.op("pe", lambda e, psO=psO, pT=pT, va=va, qs=qs, o=o, st=(ti == 0 and qs == tiles[0]["qs_min"]), sp=(last_for_qs[qs] == ti):
                         e.matmul(psO[:, qs * 65:(qs + 1) * 65], lhsT=pT[:, o + qs * 128:o + (qs + 1) * 128], rhs=va, start=st, stop=sp, skip_group_check=True),
                         reads=[pk] + list(vk), writes=[ok])
                if pv_extra is not None:
                    pv_extra(ti, t, pT[:, o:o + 512], pk)

        for step in range(n + SKEW):
            if step < n:
                emit_scores(step)
            if step - SKEW >= 0:
                emit_pv(step - SKEW)
        return psO, ok


def causal_tiles(i, diag_masks, extra_fn=None):
    tiles = []
    for j in range(4 * i + 4):
        ex = []
        if extra_fn is not None:
            ex.extend(extra_fn(j))
        m = j - 4 * i
        if m >= 0:
            ex.append(diag_masks(m))
        tiles.append(dict(j=j, extras=ex, qs_min=max(m, 0)))
    return tiles


def build_A(S, phases=(1, 2, 3), cx=None, sfx="", io=None):
    NT, NG = S // 128, S // 512
    nc = cx.nc if cx else bass.Bass("TRN2", target_bir_lowering=False)
    din = lambda n, sh, dt=F32: _din(nc, cx, sfx, n, sh, dt)
    x = io["x"] if io else din("x", [S, 1024])
    gcol = din("gcol", [128, 8])
    wA = din("wA", [1024, 2308])
    bfv = din("bf", [4, 1])
    gng = din("gng", [128, 256])
    ident = None if cx else din("ident", [128, 128])
    cmask = din("cmask", [128, 4, 512], BF16)
    rinner = din("rinner", [128, 4, 128])
    rcross = din("rcross", [64, 4, 128])
    rkdec = din("rkdec", [128, 4])
    rcdec = din("rcdec", [64, 4])
    if io:
        Yf, Yr = io["Yf"], io["Yr"]
    else:
        Y = nc.dram_tensor("Y", [S, 512], BF16, kind="ExternalOutput").ap()
        Yf, Yr = Y[:, 0:256], Y[:, 256:512]
    QKT = _dsc(nc, cx, "QKT", [1024, S], BF16)
    CQ = _dsc(nc, cx, "CQ", [4, 6, S], BF16)
    VT = _dsc(nc, cx, "VT", [S, 768], BF16)
    ZS = _dsc(nc, cx, "ZS", [S, 512], F32)

    with ExitStack() as st:
        P = cx.P if cx else Prog(nc, st)
        idb = cx.idb if cx else load_consts(P, ident)
        if cx:
            P.stack = st
            P.slot_prefix = sfx[:2]
            P.barrier()
        with ExitStack() as st1:
            P.stack = st if FLAT else st1
            wb = P.sb("wb", [128, 8, 2308], BF16)
            wkeys = load_weight_bf16(P, wb, wA, 2308, "wb")
            nt = NormT(P, idb, gcol)
            hT = Rot(P, "hT", 2, [128, 8, 512], BF16)
            psA = Rot(P, "psA", 3, [128, 512], F32, psum=True)
            fmst = Rot(P, "fmst", 3, [128, 512], BF16)
            tmst = Rot(P, "tmst", 2, [128, 768], BF16)
            zst = Rot(P, "zst", DBG.get("zst", 2), [128, 512], F32)
            bfs = P.sb("bfs", [4, 1], F32)
            ones4 = P.sb("ones4", [4, 512], F32)
            carry = P.sb("carry", [4, 1], F32)
            P.dma("sp", bfs[:], bfv, writes=["bfs"], slot="bfs")
            P.op("pool", lambda e: e.memset(ones4[:], 1.0), writes=["ones4"])
            P.op("pool", lambda e: e.memset(carry[:], 0.0), writes=["carry"])
            ft = {n: P.sb("ft_" + n, [4, 512], F32) for n in ("sg", "ls", "c", "r1", "r2")}
            fb = {n: P.sb("fb_" + n, [4, 512], BF16) for n in ("hi", "mid", "lo", "nhi", "nmid", "nlo")}
            def prep(g):
                h, hk = hT.next()
                xs = [nt.load(x[g * 512 + tt * 128:g * 512 + (tt + 1) * 128, :]) for tt in range(4)]
                nt.run4(xs, h, hk)
                return h, hk

            nxt = prep(0)
            for g in range(NG):
                h, hk = nxt
                if g + 1 < NG:
                    nxt = prep(g + 1)
                for ct in range(8):
                    ps, pk = psA.next()
                    for c in range(8):
                        P.op("pe", lambda e, ps=ps, c=c, ct=ct, h=h: e.matmul(ps[:], lhsT=wb[:, c, ct * 128:(ct + 1) * 128], rhs=h[:, c, :],
                                                                             start=(c == 0), stop=(c == 7)),
                             reads=[hk] + wkeys, writes=[pk])
                    sg, sgk = fmst.next()
                    scl = 0.125 if ct in (0, 1, 4, 5) else 1.0
                    if ct % 2 == 0:
                        P.op("act", lambda e, ps=ps, sg=sg, scl=scl: e.activation(out=sg[:], in_=ps[:], func=AF.Copy, scale=scl),
                             reads=[pk], writes=[sgk])
                    else:
                        P.op("dve", lambda e, ps=ps, sg=sg, scl=scl: e.tensor_scalar(out=sg[:], in0=ps[:], scalar1=scl, scalar2=None, op0=ALU.mult),
                             reads=[pk], writes=[sgk])
                    P.dma("pool", QKT[ct * 128:(ct + 1) * 128, g * 512:(g + 1) * 512], sg[:], reads=[sgk], writes=[("QKT", ct, g)], slot=sgk)
                ps, pk = psA.next()
                for c in range(8):
                    P.op("pe", lambda e, ps=ps, c=c, h=h: e.matmul(ps[0:4, :], lhsT=wb[:, c, 1024:1028], rhs=h[:, c, :], start=(c == 0), stop=(c == 7)),
                         reads=[hk] + wkeys, writes=[pk])
                P.op("act", lambda e, ps=ps: e.activation(out=ft["sg"][:], in_=ps[0:4, :], func=AF.Sigmoid, bias=bfs[:, 0:1]),
                     reads=[pk, "bfs"], writes=["ft_sg"])
                P.op("act", lambda e: e.activation(out=ft["ls"][:], in_=ft["sg"][:], func=AF.Ln), reads=["ft_sg"], writes=["ft_ls"])
                P.op("dve", lambda e: e.tensor_tensor_scan(out=ft["c"][:], data0=ones4[:], data1=ft["ls"][:], initial=carry[:, 0:1],
                                                           op0=ALU.mult, op1=ALU.add), reads=["ft_ls", "ones4", "carry"], writes=["ft_c"])
                P.op("dve", lambda e: e.tensor_copy(out=carry[:], in_=ft["c"][:, 511:512]), reads=["ft_c"], writes=["carry"])
                P.op("dve", lambda e: e.tensor_copy(out=fb["hi"][:], in_=ft["c"][:]), reads=["ft_c"], writes=["fb_hi"])
                P.op("dve", lambda e: e.tensor_tensor(out=ft["r1"][:], in0=ft["c"][:], in1=fb["hi"][:], op=ALU.subtract), reads=["ft_c", "fb_hi"], writes=["ft_r1"])
                P.op("dve", lambda e: e.tensor_copy(out=fb["mid"][:], in_=ft["r1"][:]), reads=["ft_r1"], writes=["fb_mid"])
                P.op("dve", lambda e: e.tensor_tensor(out=ft["r2"][:], in0=ft["r1"][:], in1=fb["mid"][:], op=ALU.subtract), reads=["ft_r1", "fb_mid"], writes=["ft_r2"])
                P.op("dve", lambda e: e.tensor_copy(out=fb["lo"][:], in_=ft["r2"][:]), reads=["ft_r2"], writes=["fb_lo"])
                for a, b_ in (("hi", "nhi"), ("mid", "nmid"), ("lo", "nlo")):
                    P.op("dve", lambda e, a=a, b_=b_: e.tensor_scalar(out=fb[b_][:], in0=fb[a][:], scalar1=-1.0, scalar2=None, op0=ALU.mult),
                         reads=["fb_" + a], writes=["fb_" + b_])
                for r, n in enumerate(("hi", "mid", "lo", "nhi", "nmid", "nlo")):
                    P.dma("pool", CQ[:, r, g * 512:(g + 1) * 512], fb[n][:], reads=["fb_" + n], writes=[("CQ", g)],
                          slot="fb_" + n)
                for tt in range(4):
                    r0 = g * 512 + tt * 128
                    tm, tmk = tmst.next()
                    zs, zk = zst.next()
                    for ci, (c0, cw) in enumerate(((0, 512), (512, 512), (1024, 256))):
                        ps, pk = psA.next()
                        for c in range(8):
                            P.op("pe", lambda e, ps=ps, c=c, h=h, tt=tt, c0=c0, cw=cw: e.matmul(ps[:, 0:cw], lhsT=h[:, c, tt * 128:(tt + 1) * 128],
                                                                                            rhs=wb[:, c, 1028 + c0:1028 + c0 + cw], start=(c == 0), stop=(c == 7)),
                                 reads=[hk] + wkeys, writes=[pk])
                        if ci == 0:
                            P.op("dve", lambda e, ps=ps, tm=tm: e.tensor_copy(out=tm[:, 0:512], in_=ps[:]), reads=[pk], writes=[tmk])
                        elif ci == 1:
                            P.op("dve", lambda e, ps=ps, tm=tm: e.tensor_copy(out=tm[:, 512:768], in_=ps[:, 0:256]), reads=[pk], writes=[tmk])
                            P.op("act", lambda e, ps=ps, zs=zs: e.activation(out=zs[:, 0:256], in_=ps[:, 256:512], func=AF.Silu), reads=[pk], writes=[zk, pk])
                        else:
                            P.op("act", lambda e, ps=ps, zs=zs: e.activation(out=zs[:, 256:512], in_=ps[:, 0:256], func=AF.Silu), reads=[pk], writes=[zk])
                    P.dma("pool", VT[r0:r0 + 128, :], tm[:], reads=[tmk], writes=[("VT", g)], slot=tmk)
                    P.dma("pool", ZS[r0:r0 + 128, :], zs[:], reads=[zk], writes=[("ZS", g)], slot=zk)
        P.stack = st
        P.barrier()
        allQKT = [("QKT", ct, g) for ct in range(8) for g in range(NG)]
        allCQ = [("CQ", g) for g in range(NG)]
        allVT = [("VT", g) for g in range(NG)]
        allZS = [("ZS", g) for g in range(NG)]
        with ExitStack() as st2:
          if 2 in phases:
              P.stack = st if FLAT else st2
              ac = AttnCore(P, idb, nO=2, skew=1, wide=True)
              cm = P.sb("cm", [128, 4, 512], BF16)
              P.dma("sp", cm[:], cmask, writes=["cm"], slot="cm")
              QTb = Rot(P, "QT", 2, [70, S], BF16)
              KTb = Rot(P, "KT", 2, [70, S], BF16)
              Vb = Rot(P, "Va", 2, [128, NT, 65], BF16)
              for (t, k) in QTb.items + KTb.items:
                  P.op("pool", lambda e, t=t: e.memset(t[64:70, :], 1.0), writes=[k])
              for (t, k) in Vb.items:
                  P.op("pool", lambda e, t=t: e.memset(t[:, :, 64:65], 1.0), writes=[k])
              zt = Rot(P, "zt", 2, [128, 4, 64], F32)
              yt = Rot(P, "yt", 2, [128, 4, 64], BF16)
              rv = Rot(P, "rv", 2, [128, 4], F32)
              for hh in range(4):
                  QT, qk = QTb.next()
                  KT, kk = KTb.next()
                  Va, vk = Vb.next()
                  P.dma("sp", QT[0:64, :], QKT[hh * 64:(hh + 1) * 64, :], reads=allQKT, writes=[qk], slot=qk)
                  P.dma("sp", QT[64:67, :], CQ[hh, 0:3, :], reads=allCQ, writes=[qk], slot=qk)
                  P.dma("sp", KT[0:64, :], QKT[256 + hh * 64:256 + (hh + 1) * 64, :], reads=allQKT, writes=[kk], slot=kk)
                  P.dma("sp", KT[67:70, :], CQ[hh, 3:6, :], reads=allCQ, writes=[kk], slot=kk)
                  P.dma("sp", Va[:, :, 0:64], VT[:, hh * 64:(hh + 1) * 64].rearrange("(n p) d -> p n d", p=128), reads=allVT, writes=[vk], slot=vk)
                  for i in range(NG):
                      tiles = causal_tiles(i, lambda m: (idb[:], cm[:, m, :], ["idb", "cm"]))
                      psO, ok = ac.group(QT, [qk], KT, [kk], 70, i * 512, tiles, lambda j: (Va[:, j, :], [vk]))
                      z, zk = zt.next()
                      y, yk = yt.next()
                      r, rk = rv.next()
                      P.dma("sp", z[:], ZS[i * 512:(i + 1) * 512, hh * 64:(hh + 1) * 64].rearrange("(q p) d -> p q d", p=128),
                            reads=allZS, writes=[zk], slot=zk)
                      P.op("dve", lambda e, psO=psO, r=r: e.reciprocal(out=r[:], in_=psO[:, 64:260:65]), reads=[ok], writes=[rk])
                      for qs in range(4):
                          P.op("dve", lambda e, psO=psO, r=r, y=y, z=z, qs=qs: e.scalar_tensor_tensor(
                              out=y[:, qs, :], in0=psO[:, qs * 65:qs * 65 + 64], scalar=r[:, qs:qs + 1], in1=z[:, qs, :], op0=ALU.mult, op1=ALU.mult),
                              reads=[ok, rk, zk], writes=[sub(yk, qs)])
                      P.dma("pool", Yf[i * 512:(i + 1) * 512, hh * 64:(hh + 1) * 64].rearrange("(q p) d -> p q d", p=128), y[:],
                            reads=[yk], writes=[("Y", hh, i)], slot=yk)
        P.stack = st
        P.barrier()
        with ExitStack() as st3:
          if 3 in phases:
              P.stack = st if FLAT else st3
              Q4r = Rot(P, "Q4", 2, [64, 4, 512], BF16)
              K4r = Rot(P, "K4", 2, [64, 4, 512], BF16)
              QC4r = Rot(P, "QC4", 2, [64, 4, 512], BF16)
              VR = P.sb("VR", [128, NT, 256], BF16)
              KD = P.sb("KD", [128, NT, 256], BF16)
              inn = P.sb("inn", [128, 4, 128], F32)
              crs = P.sb("crs", [64, 4, 128], F32)
              kdc = P.sb("kdc", [128, 4], F32)
              cdc = P.sb("cdc", [64, 4], F32)
              gg = P.sb("gg", [128, 256], F32)
              Sf = P.sb("Sf", [64, 4, 64], F32)
              Sb = Rot(P, "Sb", 2, [64, 4, 64], BF16)
              P.dma("sp", VR[:], VT[:, 256:512].rearrange("(n p) d -> p n d", p=128), reads=allVT, writes=["VR"], slot="VR")
              P.dma("sp", KD[:], VT[:, 512:768].rearrange("(n p) d -> p n d", p=128), reads=allVT, writes=["KD"], slot="KD")
              for t, a, k in ((inn, rinner, "inn"), (crs, rcross, "crs"), (kdc, rkdec, "kdc"), (cdc, rcdec, "cdc"), (gg, gng, "gg")):
                  P.dma("sp", t[:], a, writes=[k], slot=k)
              for hh in range(4):
                  P.op("dve", lambda e, hh=hh: e.tensor_scalar(out=KD[:, :, hh * 64:(hh + 1) * 64], in0=KD[:, :, hh * 64:(hh + 1) * 64],
                                                               scalar1=kdc[:, hh:hh + 1], scalar2=None, op0=ALU.mult), reads=["KD", "kdc"], writes=["KD"])
              P.op("pool", lambda e: e.memset(Sf[:], 0.0), writes=["Sf"])
              psR = Rot(P, "psR", 2, [128, 4, 128], F32, psum=True)
              psU = Rot(P, "psU", 2, [128, 4, 128], F32, psum=True)
              psOr = Rot(P, "psOr", 2, [128, 8, 64], F32, psum=True)
              aTb = Rot(P, "aTb", 2, [128, 4, 128], BF16)
              osb = Rot(P, "osb", 2, [128, 4, 64], F32)
              sqb = P.sb("sqb", [128, 4, 64], F32)
              stt = Rot(P, "stt", 2, [128, 16], F32)
              zt2 = Rot(P, "zt2", 2, [128, 256], F32)
              yt2 = Rot(P, "yt2", 2, [128, 256], BF16)
              mh = P.sb("mh2", [128, 4], F32)
              P.op("pool", lambda e: e.memset(mh[:], -0.5), writes=["mh2"])
              for n in range(NT):
                  g, c4 = n // 4, n % 4
                  cs = slice(c4 * 128, (c4 + 1) * 128)
                  if c4 == 0:
                      Q4, q4k = Q4r.next()
                      K4, k4k = K4r.next()
                      QC4, qc4k = QC4r.next()
                      P.dma("sp", Q4[:], QKT[512:768, g * 512:(g + 1) * 512].rearrange("(h d) t -> d h t", d=64), reads=allQKT, writes=[q4k], slot=q4k)
                      P.dma("sp", K4[:], QKT[768:1024, g * 512:(g + 1) * 512].rearrange("(h d) t -> d h t", d=64), reads=allQKT, writes=[k4k], slot=k4k)
                      for cc in range(4):
                          P.op("pool", lambda e, cc=cc, Q4=Q4, QC4=QC4: e.tensor_tensor(out=QC4[:, :, cc * 128:(cc + 1) * 128], in0=Q4[:, :, cc * 128:(cc + 1) * 128],
                                                                                        in1=crs[:], op=ALU.mult), reads=[q4k, "crs"], writes=[qc4k])
                  sb_, sbk = Sb.next()
                  P.op("act", lambda e, sb_=sb_: e.copy(out=sb_[:], in_=Sf[:]), reads=["Sf"], writes=[sbk])
                  pr, prk = psR.next()
                  for hh in range(4):
                      P.op("pe", lambda e, pr=pr, hh=hh, cs=cs, K4=K4, Q4=Q4: e.matmul(pr[:, hh, :], lhsT=K4[:, hh, cs], rhs=Q4[:, hh, cs], start=True, stop=True),
                           reads=[k4k, q4k], writes=[prk])
                  at, atk = aTb.next()
                  P.op("dve", lambda e, pr=pr, at=at: e.tensor_tensor(out=at[:], in0=pr[:], in1=inn[:], op=ALU.mult), reads=[prk, "inn"], writes=[atk])
                  po, pok = psOr.next()
                  for hh in range(4):
                      P.op("pe", lambda e, po=po, at=at, hh=hh, n=n: e.matmul(po[:, hh, :], lhsT=at[:, hh, :], rhs=VR[:, n, hh * 64:(hh + 1) * 64], start=True, stop=False),
                           reads=[atk, "VR"], writes=[pok])
                      P.op("pe", lambda e, po=po, sb_=sb_, hh=hh, cs=cs, QC4=QC4: e.matmul(po[:, hh, :], lhsT=QC4[:, hh, cs], rhs=sb_[:, hh, :], start=False, stop=True),
                           reads=[qc4k, sbk], writes=[pok])
                  pu, puk = psU.next()
                  for hh in range(4):
                      P.op("pe", lambda e, pu=pu, hh=hh, n=n: e.matmul(pu[0:64, hh, 0:64], lhsT=KD[:, n, hh * 64:(hh + 1) * 64], rhs=VR[:, n, hh * 64:(hh + 1) * 64], start=True, stop=True),
                           reads=["KD", "VR"], writes=[puk])
                  for hh in range(4):
                      P.op("dve", lambda e, pu=pu, hh=hh: e.scalar_tensor_tensor(out=Sf[:, hh, :], in0=Sf[:, hh, :], scalar=cdc[:, hh:hh + 1],
                                                                                in1=pu[0:64, hh, 0:64], op0=ALU.mult, op1=ALU.add),
                           reads=["Sf", puk, "cdc", sbk], writes=["Sf"])
                  ob, obk = osb.next()
                  s_, sk_ = stt.next()
                  z, zk = zt2.next()
                  y, yk = yt2.next()
                  P.dma("sp", z[:], ZS[n * 128:(n + 1) * 128, 256:512], reads=allZS, writes=[zk], slot=zk)
                  P.op("act", lambda e, po=po, ob=ob: e.copy(out=ob[:], in_=po[:, 0:4, :]), reads=[pok], writes=[obk])
                  P.op("dve", lambda e, ob=ob, s_=s_: e.tensor_reduce(out=s_[:, 0:4], in_=ob[:], axis=AX.X, op=ALU.add), reads=[obk], writes=[sk_])
                  P.op("pool", lambda e, ob=ob: e.tensor_tensor(out=sqb[:], in0=ob[:], in1=ob[:], op=ALU.mult), reads=[obk], writes=["sqb"])
                  P.op("dve", lambda e, s_=s_: e.tensor_reduce(out=s_[:, 4:8], in_=sqb[:], axis=AX.X, op=ALU.add), reads=["sqb", sk_], writes=[sk_])
                  P.op("dve", lambda e, s_=s_: e.tensor_scalar(out=s_[:, 0:8], in0=s_[:, 0:8], scalar1=1.0 / 64, scalar2=None, op0=ALU.mult), reads=[sk_], writes=[sk_])
                  P.op("dve", lambda e, s_=s_: e.tensor_tensor(out=s_[:, 8:12], in0=s_[:, 0:4], in1=s_[:, 0:4], op=ALU.mult), reads=[sk_], writes=[sk_])
                  P.op("dve", lambda e, s_=s_: e.tensor_tensor(out=s_[:, 8:12], in0=s_[:, 4:8], in1=s_[:, 8:12], op=ALU.subtract), reads=[sk_], writes=[sk_])
                  P.op("dve", lambda e, s_=s_: e.tensor_scalar(out=s_[:, 8:12], in0=s_[:, 8:12], scalar1=1e-5, scalar2=None, op0=ALU.add), reads=[sk_], writes=[sk_])
                  P.op("pool", lambda e, s_=s_: e.tensor_tensor(out=s_[:, 12:16], in0=s_[:, 8:12], in1=mh[:], op=ALU.pow), reads=[sk_, "mh2"], writes=[sk_])
                  for hh in range(4):
                      P.op("dve", lambda e, ob=ob, s_=s_, hh=hh: e.tensor_scalar(out=ob[:, hh, :], in0=ob[:, hh, :], scalar1=s_[:, hh:hh + 1], scalar2=s_[:, 12 + hh:13 + hh],
                                                                              op0=ALU.subtract, op1=ALU.mult), reads=[sub(obk, hh), sk_], writes=[sub(obk, hh)])
                  P.op("pool", lambda e, ob=ob: e.tensor_tensor(out=ob[:].rearrange("p h d -> p (h d)"), in0=ob[:].rearrange("p h d -> p (h d)"), in1=gg[:], op=ALU.mult),
                       reads=[obk, "gg"], writes=[obk])
                  P.op("dve", lambda e, ob=ob, z=z, y=y: e.tensor_tensor(out=y[:], in0=ob[:].rearrange("p h d -> p (h d)"), in1=z[:], op=ALU.mult),
                       reads=[obk, zk], writes=[yk])
                  P.dma("pool", Yr[n * 128:(n + 1) * 128, :], y[:], reads=[yk], writes=[("Yr", n)], slot=yk)
        P.stack = st
        if cx is None:
            P.final_wait("sp", [("Y", hh, i) for hh in range(4) for i in range(NG)] + [("Yr", n) for n in range(NT)])
            P.emit()
    return nc


def consts_A(hh):
    k = np.arange(128)[:, None]
    q = np.arange(512)[None, :]
    cm = np.stack([np.where(128 * m + k <= q, 0.0, NEGM) for m in range(4)], axis=1).astype(NPBF)
    heads = np.arange(4) + 4 * hh
    lg = np.log(1.0 - 2.0 ** (-5.0 - heads.astype(np.float64)))
    i = np.arange(128)
    diff = i[None, :] - i[:, None]
    innerT = np.where(diff[None] >= 0, np.exp(lg[:, None, None] * np.maximum(diff, 0)[None]), 0.0)
    rinner = np.ascontiguousarray(innerT.transpose(1, 0, 2)).astype(np.float32)
    cross = np.exp(lg[:, None] * (i[None, :] + 1))
    rcross = np.ascontiguousarray(np.broadcast_to(cross[None, :, :], (64, 4, 128))).astype(np.float32)
    kdec = np.exp(lg[:, None] * (127 - i)[None, :])
    rkdec = np.ascontiguousarray(kdec.T).astype(np.float32)
    cdec = np.exp(lg * 128)
    rcdec = np.ascontiguousarray(np.broadcast_to(cdec[None, :], (64, 4))).astype(np.float32)
    return dict(cmask=cm, rinner=rinner, rcross=rcross, rkdec=rkdec, rcdec=rcdec, ident=np.eye(128, dtype=np.float32))


def inputs_A(x_b, norm_g, w_in, b_f, gn_g, hh):
    sl = lambda o, h0, n: np.arange(o + h0 * 64, o + (h0 + n) * 64)
    o_qf, o_kf, o_vf, o_fl, o_qr, o_kr, o_vr, o_z = 0, 512, 1024, 1536, 1544, 2056, 2568, 3080
    h0 = 4 * hh
    cols = np.concatenate([sl(o_qf, h0, 4), sl(o_kf, h0, 4), sl(o_qr, h0, 4), sl(o_kr, h0, 4),
                           np.arange(o_fl + h0, o_fl + h0 + 4),
                           sl(o_vf, h0, 4), sl(o_vr, h0, 4), sl(o_kr, h0, 4),
                           sl(o_z, h0, 4), sl(o_z + 512, h0, 4)])
    d = dict(x=np.ascontiguousarray(x_b), gcol=np.ascontiguousarray(norm_g.reshape(8, 128).T),
             wA=np.ascontiguousarray(w_in[:, cols]), bf=np.ascontiguousarray(b_f[h0:h0 + 4].reshape(4, 1)),
             gng=np.ascontiguousarray(np.broadcast_to(gn_g[h0 * 64:(h0 + 4) * 64][None, :], (128, 256))))
    d.update(consts_A(hh))
    return d


def build_O(T, final, cx=None, sfx="", io=None):
    NTT = T // 128
    nc = cx.nc if cx else bass.Bass("TRN2", target_bir_lowering=False)
    din = lambda n, sh, dt=F32: _din(nc, cx, sfx, n, sh, dt)
    y = io["y"] if io else din("y", [T, 1024], BF16)
    x = io["x"] if io else din("x", [T, 1024])
    w = din("w", [1024, 1024])
    gf = din("gf", [128, 1024])
    ident = None if cx else din("ident", [128, 128])
    out = io["out"] if io else nc.dram_tensor("out", [T, 1024], F32, kind="ExternalOutput").ap()
    with ExitStack() as st:
        P = cx.P if cx else Prog(nc, st)
        idb = cx.idb if cx else load_consts(P, ident)
        if cx:
            P.stack = st
            P.slot_prefix = sfx[:2]
            P.barrier()
        wb = P.sb("wb", [128, 8, 1024], BF16)
        wkeys = load_weight_bf16(P, wb, w, 1024, "wb")
        gft = P.sb("gft", [128, 1024], F32)
        mh = P.sb("mh", [128, 1], F32)
        if final:
            P.dma("sp", gft[:], gf, writes=["gft"], slot="gft")
            P.op("pool", lambda e: e.memset(mh[:], -0.5), writes=["mh"])
        yt = Rot(P, "yt", 3, [128, 1024], BF16)
        xt = Rot(P, "xt", 3, [128, 1024], F32)
        ot = Rot(P, "ot", 2, [128, 1024], F32)
        sq = P.sb("sq", [128, 1024], F32)
        stt = Rot(P, "stt", 2, [128, 4], F32)
        pT = Rot(P, "pT", 2, [128, 8, 128], BF16, psum=True)
        pO = Rot(P, "pO", 4, [128, 512], F32, psum=True)
        yT = Rot(P, "yT", 3, [128, 8, 128], BF16)

        def prep(t):
            r = slice(t * 128, (t + 1) * 128)
            yb, yk = yt.next()
            xb, xk = xt.next()
            P.dma("sp", yb[:], y[r, :], writes=[yk], slot=yk)
            P.dma("sp", xb[:], x[r, :], writes=[xk], slot=xk)
            pt, ptk = pT.next()
            for c in range(8):
                P.op("pe", lambda e, c=c, pt=pt, yb=yb: e.transpose(out=pt[:, c, :], in_=yb[:, c * 128:(c + 1) * 128], identity=idb[:]),
                     reads=[yk, "idb"], writes=[ptk])
            ytt, ytk = yT.next()
            P.op("act", lambda e, pt=pt, ytt=ytt: e.copy(out=ytt[:], in_=pt[:]), reads=[ptk], writes=[ytk])
            return (xb, xk, ytt, ytk)

        def mm(t, st_):
            xb, xk, ytt, ytk = st_
            r = slice(t * 128, (t + 1) * 128)
            ob, obk = ot.next()
            for hf in range(2):
                po, pok = pO.next()
                for c in range(8):
                    P.op("pe", lambda e, c=c, po=po, ytt=ytt, hf=hf: e.matmul(po[:], lhsT=ytt[:, c, :], rhs=wb[:, c, hf * 512:(hf + 1) * 512],
                                                                             start=(c == 0), stop=(c == 7)), reads=[ytk] + wkeys, writes=[pok])
                P.op("dve", lambda e, po=po, ob=ob, xb=xb, hf=hf: e.tensor_tensor(out=ob[:, hf * 512:(hf + 1) * 512], in0=po[:], in1=xb[:, hf * 512:(hf + 1) * 512],
                                                                                op=ALU.add), reads=[pok, xk], writes=[obk])
            if final:
                s_, sk_ = stt.next()
                P.op("dve", lambda e, ob=ob, s_=s_: e.scalar_tensor_tensor(out=sq[:], in0=ob[:], scalar=1.0, in1=ob[:], op0=ALU.mult, op1=ALU.mult,
                                                                         accum_out=s_[:, 0:1]), reads=[obk], writes=["sq", sk_])
                P.op("dve", lambda e, s_=s_: e.tensor_scalar(out=s_[:, 1:2], in0=s_[:, 0:1], scalar1=1.0 / 1024, scalar2=1e-6, op0=ALU.mult, op1=ALU.add),
                     reads=[sk_], writes=[sk_])
                P.op("pool", lambda e, s_=s_: e.tensor_tensor(out=s_[:, 2:3], in0=s_[:, 1:2], in1=mh[:], op=ALU.pow), reads=[sk_, "mh"], writes=[sk_])
                P.op("dve", lambda e, ob=ob, s_=s_: e.scalar_tensor_tensor(out=ob[:], in0=ob[:], scalar=s_[:, 2:3], in1=gft[:], op0=ALU.mult, op1=ALU.mult),
                     reads=[obk, sk_, "gft"], writes=[obk])
            P.dma("pool", out[r, :], ob[:], reads=[obk], writes=[("out", t)], slot=obk)

        nxt = prep(0)
        for t in range(NTT):
            cur = nxt
            if t + 1 < NTT:
                nxt = prep(t + 1)
            mm(t, cur)
        if cx is None or final:
            P.final_wait("sp", [("out", t) for t in range(NTT)])
        if cx is None:
            P.emit()
    return nc


def build_C(S, cx=None, sfx="", io=None):
    NT, NG = S // 128, S // 512
    NCMP = (S - 32) // 16 + 1
    NCT = max(1, S // 2048)
    nc = cx.nc if cx else bass.Bass("TRN2", target_bir_lowering=False)
    din = lambda n, sh, dt=F32: _din(nc, cx, sfx, n, sh, dt)
    x = io["x"] if io else din("x", [S, 1024])
    gcol = din("gcol", [128, 8])
    wC = din("wC", [1024, 1816])
    bg = din("bg", [128, 24])
    ident = None if cx else din("ident", [128, 128])
    peT = din("peT", [2, 64, 32])
    w1 = din("w1", [2, 2048, 256])
    w2 = din("w2", [2, 256, 64])
    AQ = din("AQ", [8, 9, S], BF16)
    AK = din("AK", [9, S], BF16)
    AKc = din("AKc", [9, 128 * NCT], BF16)
    cmask = din("cmask", [128, 4, 512], BF16)
    wmask = din("wmask", [128, 8, 512], BF16)
    cmk = din("cmk", [128, 5, 512], BF16)
    Mc = din("Mc", [128, NCT, 128], BF16)
    ADD = din("ADD", [S, 128])
    Ew = din("Ew", [128, S], BF16)
    Y = io["Y"] if io else nc.dram_tensor("Y", [S, 512], BF16, kind="ExternalOutput").ap()
    QKT = _dsc(nc, cx, "QKT", [1024, S], BF16)
    VT = _dsc(nc, cx, "VT2", [S, 256], BF16)
    ZS = _dsc(nc, cx, "ZS", [S, 512], F32)
    GL = _dsc(nc, cx, "GL", [S, 24], F32)
    OC = _dsc(nc, cx, "OC", [S, 512], F32)

    with ExitStack() as st:
        P = cx.P if cx else Prog(nc, st)
        idb = cx.idb if cx else load_consts(P, ident)
        if cx:
            P.stack = st
            P.slot_prefix = sfx[:2]
            P.barrier()
        KcT = [P.sb(f"KcT{g}", [73, 128 * NCT], BF16) for g in range(2)]
        VcA = [P.sb(f"VcA{g}", [128, NCT, 65], BF16) for g in range(2)]
        selT = [P.sb(f"selT{g}", [128, S], BF16) for g in range(2)]
        with ExitStack() as st1:
            P.stack = st1
            wb = P.sb("wb", [128, 8, 1816], BF16)
            wkeys = load_weight_bf16(P, wb, wC, 1816, "wb")
            nt = NormT(P, idb, gcol)
            hT = Rot(P, "hT", 2, [128, 8, 512], BF16)
            psA = Rot(P, "psA", 3, [128, 512], F32, psum=True)
            fmst = Rot(P, "fmst", 3, [128, 512], BF16)
            tmst = Rot(P, "tmst", 2, [128, 256], BF16)
            zst = Rot(P, "zst", 2, [128, 512], F32)
            gst = Rot(P, "gst", 2, [128, 24], F32)
            bgs = P.sb("bgs", [128, 24], F32)
            P.dma("sp", bgs[:], bg, writes=["bgs"], slot="bgs")
            def prep(g):
                h, hk = hT.next()
                xs = [nt.load(x[g * 512 + tt * 128:g * 512 + (tt + 1) * 128, :]) for tt in range(4)]
                nt.run4(xs, h, hk)
                return h, hk

            nxt = prep(0)
            for g in range(NG):
                h, hk = nxt
                if g + 1 < NG:
                    nxt = prep(g + 1)
                for ct in range(8):
                    ps, pk = psA.next()
                    for c in range(8):
                        P.op("pe", lambda e, ps=ps, c=c, ct=ct, h=h: e.matmul(ps[:], lhsT=wb[:, c, ct * 128:(ct + 1) * 128], rhs=h[:, c, :],
                                                                             start=(c == 0), stop=(c == 7)), reads=[hk] + wkeys, writes=[pk])
                    sg, sgk = fmst.next()
                    scl = 0.125 if ct < 4 else 1.0
                    if ct % 2 == 0:
                        P.op("act", lambda e, ps=ps, sg=sg, scl=scl: e.activation(out=sg[:], in_=ps[:], func=AF.Copy, scale=scl), reads=[pk], writes=[sgk])
                    else:
                        P.op("dve", lambda e, ps=ps, sg=sg, scl=scl: e.tensor_scalar(out=sg[:], in0=ps[:], scalar1=scl, scalar2=None, op0=ALU.mult),
                             reads=[pk], writes=[sgk])
                    P.dma("pool", QKT[ct * 128:(ct + 1) * 128, g * 512:(g + 1) * 512], sg[:], reads=[sgk], writes=[("QKT", ct, g)], slot=sgk)
                for tt in range(4):
                    r0 = g * 512 + tt * 128
                    tm, tmk = tmst.next()
                    zs, zk = zst.next()
                    gs, gk = gst.next()
                    ps, pk = psA.next()
                    for c in range(8):
                        P.op("pe", lambda e, ps=ps, c=c, h=h, tt=tt: e.matmul(ps[:], lhsT=h[:, c, tt * 128:(tt + 1) * 128], rhs=wb[:, c, 1024:1536],
                                                                             start=(c == 0), stop=(c == 7)), reads=[hk] + wkeys, writes=[pk])
                    P.op("dve", lambda e, ps=ps, tm=tm: e.tensor_copy(out=tm[:], in_=ps[:, 0:256]), reads=[pk], writes=[tmk])
                    P.op("act", lambda e, ps=ps, zs=zs: e.activation(out=zs[:, 0:256], in_=ps[:, 256:512], func=AF.Silu), reads=[pk], writes=[zk, pk])
                    ps, pk = psA.next()
                    for c in range(8):
                        P.op("pe", lambda e, ps=ps, c=c, h=h, tt=tt: e.matmul(ps[:, 0:280], lhsT=h[:, c, tt * 128:(tt + 1) * 128], rhs=wb[:, c, 1536:1816],
                                                                             start=(c == 0), stop=(c == 7)), reads=[hk] + wkeys, writes=[pk])
                    P.op("act", lambda e, ps=ps, zs=zs: e.activation(out=zs[:, 256:512], in_=ps[:, 0:256], func=AF.Silu), reads=[pk], writes=[zk])
                    P.op("dve", lambda e, ps=ps, gs=gs: e.tensor_tensor(out=gs[:], in0=ps[:, 256:280], in1=bgs[:], op=ALU.add), reads=[pk, "bgs"], writes=[gk, pk])
                    P.op("act", lambda e, gs=gs: e.activation(out=gs[:], in_=gs[:], func=AF.Sigmoid), reads=[gk], writes=[gk])
                    P.dma("pool", VT[r0:r0 + 128, :], tm[:], reads=[tmk], writes=[("VT", g)], slot=tmk)
                    P.dma("pool", ZS[r0:r0 + 128, :], zs[:], reads=[zk], writes=[("ZS", g)], slot=zk)
                    P.dma("pool", GL[r0:r0 + 128, :], gs[:], reads=[gk], writes=[("GL", g)], slot=gk)
        P.stack = st
        P.barrier()
        allQKT = [("QKT", ct, g) for ct in range(8) for g in range(NG)]
        allVT = [("VT", g) for g in range(NG)]
        allZS = [("ZS", g) for g in range(NG)]
        allGL = [("GL", g) for g in range(NG)]
        with ExitStack() as st2:
            P.stack = st2
            ATr = Rot(P, "AT", 2, [64, S], BF16)
            w1s = Rot(P, "w1s", 2, [64, 16, 256], F32)
            w1b = [P.sb(f"w1b{k}", [64, 32, 256], BF16) for k in range(2)]
            w2s = P.sb("w2s", [128, 2, 2, 64], F32)
            w2b = P.sb("w2b", [128, 2, 2, 64], BF16)
            pes = P.sb("pes", [64, 2, 32], F32)
            peb = P.sb("peb", [64, 2, 32], BF16)
            bias = P.sb("cbias", [128, 2, 2], F32)
            hidT = Rot(P, "hidT", 2, [128, 2, 512], BF16)
            psH = Rot(P, "psH", 2, [128, 512], F32, psum=True)
            psK = P.ps("psK", [128, 512], F32)
            psV = P.ps("psV", [128, 512], F32)
            psB = P.ps("psB", [128, 512], F32)
            for k in range(2):
                for half in range(2):
                    t, tk = w1s.next()
                    P.dma("sp", t[:], w1[k, half * 1024:(half + 1) * 1024, :].rearrange("(l d) h -> d l h", d=64), writes=[tk], slot=tk)
                    P.op("dve", lambda e, t=t, k=k, half=half: e.tensor_copy(out=w1b[k][:, half * 16:(half + 1) * 16, :], in_=t[:]), reads=[tk], writes=[f"w1b{k}"])
            P.dma("sp", w2s[:], w2.rearrange("k (c p) d -> p k c d", p=128), writes=["w2s"], slot="w2s")
            P.op("dve", lambda e: e.tensor_copy(out=w2b[:], in_=w2s[:]), reads=["w2s"], writes=["w2b"])
            P.dma("sp", pes[:], peT.rearrange("k d l -> d k l"), writes=["pes"], slot="pes")
            P.op("dve", lambda e: e.tensor_copy(out=peb[:], in_=pes[:]), reads=["pes"], writes=["peb"])
            for (t, tk) in hidT.items:
                P.op("pool", lambda e, t=t: e.memset(t[:], 0.0), writes=[tk])
            for g in range(2):
                P.op("pool", lambda e, g=g: e.memset(VcA[g][:, :, 64:65], 1.0), writes=[f"VcA{g}"])
            for k in range(2):
                for hh2 in range(2):
                    for l in range(32):
                        P.op("pe", lambda e, k=k, hh2=hh2, l=l: e.matmul(psB[:, (k * 2 + hh2):(k * 2 + hh2) + 1], lhsT=w1b[k][:, l, hh2 * 128:(hh2 + 1) * 128],
                                                                        rhs=peb[:, k, l:l + 1], start=(l == 0), stop=(l == 31)),
                             reads=[f"w1b{k}", "peb"], writes=["psB"])
            P.op("dve", lambda e: e.tensor_copy(out=bias[:].rearrange("p a b -> p (a b)"), in_=psB[:, 0:4]), reads=["psB"], writes=["cbias"])
            for g in range(2):
                for k in range(2):
                    AT, atk = ATr.next()
                    row0 = (512 if k == 0 else 640) + g * 64
                    P.dma("sp", AT[:], QKT[row0:row0 + 64, :], reads=allQKT, writes=[atk], slot=atk)
                    hd, hdk = hidT.next()
                    for hh2 in range(2):
                        ps, pk = psH.next()
                        for l in range(32):
                            P.op("pe", lambda e, ps=ps, k=k, hh2=hh2, l=l, AT=AT: e.matmul(ps[:, 0:NCMP], lhsT=w1b[k][:, l, hh2 * 128:(hh2 + 1) * 128],
                                                                                        rhs=AT[:, l:l + 16 * (NCMP - 1) + 1:16], start=(l == 0), stop=(l == 31)),
                                 reads=[f"w1b{k}", atk], writes=[pk])
                        P.op("act", lambda e, ps=ps, hd=hd, hh2=hh2, k=k: e.activation(out=hd[:, hh2, 0:NCMP], in_=ps[:, 0:NCMP], func=AF.Silu, bias=bias[:, k, hh2:hh2 + 1]),
                             reads=[pk, "cbias"], writes=[hdk])
                    if k == 0:
                        for hh2 in range(2):
                            P.op("pe", lambda e, hd=hd, hh2=hh2: e.matmul(psK[0:64, 0:128 * NCT], lhsT=w2b[:, 0, hh2, :], rhs=hd[:, hh2, 0:128 * NCT], start=(hh2 == 0), stop=(hh2 == 1)),
                                 reads=[hdk, "w2b"], writes=["psK"])
                        P.op("dve", lambda e, g=g: e.tensor_copy(out=KcT[g][0:64, :], in_=psK[0:64, 0:128 * NCT]), reads=["psK"], writes=[f"KcT{g}"])
                        P.dma("sp", KcT[g][64:73, :], AKc, writes=[f"KcT{g}"], slot=f"KcT{g}")
                    else:
                        for ct in range(NCT):
                            for hh2 in range(2):
                                P.op("pe", lambda e, hd=hd, hh2=hh2, ct=ct: e.matmul(psV[:, ct * 64:(ct + 1) * 64], lhsT=hd[:, hh2, ct * 128:(ct + 1) * 128], rhs=w2b[:, 1, hh2, :],
                                                                                    start=(hh2 == 0), stop=(hh2 == 1)), reads=[hdk, "w2b"], writes=["psV"])
                        P.op("dve", lambda e, g=g: e.tensor_copy(out=VcA[g][:, :, 0:64], in_=psV[:, 0:64 * NCT].rearrange("p (c d) -> p c d", d=64)),
                             reads=["psV"], writes=[f"VcA{g}"])
        P.stack = st
        P.barrier()
        with ExitStack() as st3:
            P.stack = st3
            ac = AttnCore(P, idb)
            psI = Rot(P, "psI", 2, [128, 4, 128], F32, psum=True)
            psT = P.ps("psT", [128, 8, 128], BF16)
            cmks = P.sb("cmks", [128, 5, 512], BF16)
            Ms = P.sb("Ms", [128, NCT, 128], BF16)
            P.dma("sp", cmks[:], cmk, writes=["cmks"], slot="cmks")
            P.dma("sp", Ms[:], Mc, writes=["Ms"], slot="Ms")
            QTb = Rot(P, "QT", 4, [73, 512], BF16)
            glt = Rot(P, "glt", 2, [128, 4, 24], F32)
            adt = Rot(P, "adt", 2, [128, 4, 128], F32)
            oct_ = Rot(P, "oct", 2, [128, 4, 64], F32)
            rv = Rot(P, "rv", 2, [128, 4], F32)
            imp = P.sb("imp", [128, 4, 128], F32)
            wk = P.sb("wk", [128, 4, 128], F32)
            m8 = P.sb("m8", [128, 4, 16], F32)
            sb16 = P.sb("sb16", [128, 4, 128], BF16)
            for g in range(2):
                for i in range(NG):
                    gl_, glk = glt.next()
                    ad, adk = adt.next()
                    P.dma("sp", gl_[:], GL[i * 512:(i + 1) * 512, :].rearrange("(q p) c -> p q c", p=128), reads=allGL, writes=[glk], slot=glk)
                    P.dma("sp", ad[:], ADD[i * 512:(i + 1) * 512, :].rearrange("(q p) c -> p q c", p=128), writes=[adk], slot=adk)
                    jcs = list(range(0, min(i // 4, NCT - 1) + 1))
                    for hl in range(4):
                        hq = g * 4 + hl
                        QT, qk = QTb.next()
                        P.dma("sp", QT[0:64, :], QKT[hq * 64:(hq + 1) * 64, i * 512:(i + 1) * 512], reads=allQKT, writes=[qk], slot=qk)
                        P.dma("sp", QT[64:73, :], AQ[hq, :, i * 512:(i + 1) * 512], writes=[qk], slot=qk)
                        tiles = []
                        for jc in jcs:
                            dd = (512 * i - 2048 * jc) // 512
                            ex = [(idb[:], cmks[:, dd, :], ["idb", "cmks"])] if dd <= 4 else []
                            tiles.append(dict(j=jc, extras=ex, qs_min=0))
                        pI, pik = psI.next()

                        def pv_extra(ti, t, pT, pk, pI=pI, pik=pik, ntl=len(tiles)):
                            for qs in range(4):
                                P.op("pe", lambda e, pI=pI, pT=pT, qs=qs, jc=t["j"], st_=(ti == 0 and qs == 0), sp_=(ti == ntl - 1):
                                     e.matmul(pI[:, qs, :], lhsT=pT[:, qs * 128:(qs + 1) * 128], rhs=Ms[:, jc, :], start=st_, stop=sp_, skip_group_check=True),
                                     reads=[pk, "Ms"], writes=[pik])
                        psO, ok = ac.group(QT, [qk], KcT[g], [f"KcT{g}"], 73, 0, tiles, lambda j, g=g: (VcA[g][:, j, :], [f"VcA{g}"]), pv_extra=pv_extra)
                        r, rk = rv.next()
                        oc, ock = oct_.next()
                        P.op("dve", lambda e, psO=psO, r=r: e.tensor_scalar(out=r[:], in0=psO[:, 64:260:65], scalar1=1e-30, scalar2=None, op0=ALU.max), reads=[ok], writes=[rk])
                        P.op("dve", lambda e, r=r: e.reciprocal(out=r[:], in_=r[:]), reads=[rk], writes=[rk])
                        for qs in range(4):
                            P.op("dve", lambda e, psO=psO, r=r, oc=oc, gl_=gl_, qs=qs, hq=hq: e.tensor_scalar(
                                out=oc[:, qs, :], in0=psO[:, qs * 65:qs * 65 + 64], scalar1=r[:, qs:qs + 1], scalar2=gl_[:, qs, hq * 3:hq * 3 + 1], op0=ALU.mult, op1=ALU.mult),
                                reads=[ok, rk, glk], writes=[sub(ock, qs)])
                        P.dma("pool", OC[i * 512:(i + 1) * 512, hq * 64:(hq + 1) * 64].rearrange("(q p) d -> p q d", p=128), oc[:], reads=[ock], writes=[("OC", hq, i)], slot=ock)
                        for qs in range(4):
                            if hl == 0:
                                P.op("dve", lambda e, pI=pI, r=r, qs=qs: e.tensor_scalar(out=imp[:, qs, :], in0=pI[:, qs, :], scalar1=r[:, qs:qs + 1], scalar2=None, op0=ALU.mult),
                                     reads=[pik, rk], writes=[sub("imp", qs)])
                            else:
                                P.op("dve", lambda e, pI=pI, r=r, qs=qs: e.scalar_tensor_tensor(out=imp[:, qs, :], in0=pI[:, qs, :], scalar=r[:, qs:qs + 1], in1=imp[:, qs, :],
                                                                                              op0=ALU.mult, op1=ALU.add), reads=[pik, rk, sub("imp", qs)], writes=[sub("imp", qs)])
                    P.op("dve", lambda e, ad=ad: e.tensor_tensor(out=imp[:], in0=imp[:], in1=ad[:], op=ALU.add), reads=["imp", adk], writes=["imp"])
                    for qs in range(4):
                        P.op("dve", lambda e, qs=qs: e.max(out=m8[:, qs, 0:8], in_=imp[:, qs, :]), reads=[sub("imp", qs)], writes=[sub("m8", qs)])
                    for qs in range(4):
                        P.op("dve", lambda e, qs=qs: e.match_replace(out=wk[:, qs, :], in_to_replace=m8[:, qs, 0:8], in_values=imp[:, qs, :], imm_value=-3.0e38),
                             reads=[sub("imp", qs), sub("m8", qs)], writes=[sub("wk", qs)])
                    for qs in range(4):
                        P.op("dve", lambda e, qs=qs: e.max(out=m8[:, qs, 8:16], in_=wk[:, qs, :]), reads=[sub("wk", qs)], writes=[sub("m8", qs)])
                    for qs in range(4):
                        P.op("dve", lambda e, qs=qs: e.tensor_scalar(out=wk[:, qs, :], in0=imp[:, qs, :], scalar1=m8[:, qs, 15:16], scalar2=None, op0=ALU.is_ge),
                             reads=[sub("imp", qs), sub("m8", qs)], writes=[sub("wk", qs)])
                    P.op("dve", lambda e: e.tensor_scalar(out=sb16[:], in0=wk[:], scalar1=-1.0, scalar2=-NEGM, op0=ALU.add, op1=ALU.mult), reads=["wk"], writes=["sb16"])
                    for qs in range(4):
                        P.op("pe", lambda e, qs=qs: e.transpose(out=psT[:, qs, :], in_=sb16[:, qs, :], identity=idb[:]), reads=["sb16", "idb"], writes=["psT"])
                    P.op("act", lambda e, g=g, i=i: e.copy(out=selT[g][:, i * 512:(i + 1) * 512], in_=psT[:, 0:4, :].rearrange("p a b -> p (a b)")),
                         reads=["psT"], writes=[("selT", g, i)])
        P.stack = st
        P.barrier()
        allOC = [("OC", hq, i) for hq in range(8) for i in range(NG)]
        with ExitStack() as st4:
            P.stack = st4
            ac = AttnCore(P, idb, nO=4, skew=1, wide=True)
            cm = P.sb("cm", [128, 4, 512], BF16)
            wm = P.sb("wm", [128, 8, 512], BF16)
            Ews = P.sb("Ews", [128, S], BF16)
            P.dma("sp", cm[:], cmask, writes=["cm"], slot="cm")
            P.dma("sp", wm[:], wmask, writes=["wm"], slot="wm")
            P.dma("sp", Ews[:], Ew, writes=["Ews"], slot="Ews")
            QTb = Rot(P, "QT", 2, [73, S], BF16)
            KsT = P.sb("KsT", [73, S], BF16)
            KwT = P.sb("KwT", [73, S], BF16)
            VsA = P.sb("VsA", [128, NT, 65], BF16)
            VwA = P.sb("VwA", [128, NT, 65], BF16)
            P.op("pool", lambda e: e.memset(VsA[:, :, 64:65], 1.0), writes=["VsA"])
            P.op("pool", lambda e: e.memset(VwA[:, :, 64:65], 1.0), writes=["VwA"])
            glt = Rot(P, "glt", 2, [128, 4, 24], F32)
            zt = Rot(P, "zt", 2, [128, 4, 64], F32)
            oct_ = Rot(P, "oc4", 2, [128, 4, 64], F32)
            acc = Rot(P, "acc", 2, [128, 4, 64], F32)
            yt = Rot(P, "yt", 2, [128, 4, 64], BF16)
            rv = Rot(P, "rv", 2, [128, 8], F32)
            for g in range(2):
                P.dma("sp", KsT[0:64, :], QKT[768 + g * 64:768 + (g + 1) * 64, :], reads=allQKT, writes=["KsT"], slot="KsT")
                P.dma("sp", KsT[64:73, :], AK, writes=["KsT"], slot="KsT")
                P.dma("sp", KwT[0:64, :], QKT[896 + g * 64:896 + (g + 1) * 64, :], reads=allQKT, writes=["KwT"], slot="KwT")
                P.dma("sp", KwT[64:73, :], AK, writes=["KwT"], slot="KwT")
                P.dma("sp", VsA[:, :, 0:64], VT[:, g * 64:(g + 1) * 64].rearrange("(n p) d -> p n d", p=128), reads=allVT, writes=["VsA"], slot="VsA")
                P.dma("sp", VwA[:, :, 0:64], VT[:, 128 + g * 64:128 + (g + 1) * 64].rearrange("(n p) d -> p n d", p=128), reads=allVT, writes=["VwA"], slot="VwA")
                selk = [("selT", g, i) for i in range(NG)]
                for hl in range(4):
                    hq = g * 4 + hl
                    QT, qk = QTb.next()
                    P.dma("sp", QT[0:64, :], QKT[hq * 64:(hq + 1) * 64, :], reads=allQKT, writes=[qk], slot=qk)
                    P.dma("sp", QT[64:73, :], AQ[hq], writes=[qk], slot=qk)
                    for i in range(NG):
                        tiles = causal_tiles(i, lambda m: (idb[:], cm[:, m, :], ["idb", "cm"]),
                                             extra_fn=lambda j, i=i, g=g: [(Ews[:, j * 128:(j + 1) * 128], selT[g][:, i * 512:(i + 1) * 512], ["Ews", ("selT", g, i)])])
                        psOs, oks = ac.group(QT, [qk], KsT, ["KsT"], 73, i * 512, tiles, lambda j: (VsA[:, j, :], ["VsA"]))
                        wt = []
                        for m in range(-4, 4):
                            j = 4 * i + m
                            if j < 0:
                                continue
                            wt.append(dict(j=j, extras=[(idb[:], wm[:, m + 4, :], ["idb", "wm"])], qs_min=max(m, 0), qs_max=min(4, m + 5)))
                        psOw, okw = ac.group(QT, [qk], KwT, ["KwT"], 73, i * 512, wt, lambda j: (VwA[:, j, :], ["VwA"]))
                        gl_, glk = glt.next()
                        z, zk = zt.next()
                        oc, ock = oct_.next()
                        a_, ak_ = acc.next()
                        y, yk = yt.next()
                        r, rk = rv.next()
                        P.dma("sp", gl_[:], GL[i * 512:(i + 1) * 512, :].rearrange("(q p) c -> p q c", p=128), reads=allGL, writes=[glk], slot=glk)
                        P.dma("sp", z[:], ZS[i * 512:(i + 1) * 512, hq * 64:(hq + 1) * 64].rearrange("(q p) d -> p q d", p=128), reads=allZS, writes=[zk], slot=zk)
                        P.dma("sp", oc[:], OC[i * 512:(i + 1) * 512, hq * 64:(hq + 1) * 64].rearrange("(q p) d -> p q d", p=128), reads=allOC, writes=[ock], slot=ock)
                        P.op("dve", lambda e, psOs=psOs, r=r: e.reciprocal(out=r[:, 0:4], in_=psOs[:, 64:260:65]), reads=[oks], writes=[rk])
                        P.op("dve", lambda e, psOw=psOw, r=r: e.reciprocal(out=r[:, 4:8], in_=psOw[:, 64:260:65]), reads=[okw, rk], writes=[rk])
                        P.op("dve", lambda e, r=r, gl_=gl_, hq=hq: e.tensor_tensor(out=r[:, 0:4], in0=r[:, 0:4], in1=gl_[:, :, hq * 3 + 1], op=ALU.mult), reads=[rk, glk], writes=[rk])
                        P.op("dve", lambda e, r=r, gl_=gl_, hq=hq: e.tensor_tensor(out=r[:, 4:8], in0=r[:, 4:8], in1=gl_[:, :, hq * 3 + 2], op=ALU.mult), reads=[rk, glk], writes=[rk])
                        for qs in range(4):
                            P.op("dve", lambda e, psOs=psOs, r=r, a_=a_, oc=oc, qs=qs: e.scalar_tensor_tensor(
                                out=a_[:, qs, :], in0=psOs[:, qs * 65:qs * 65 + 64], scalar=r[:, qs:qs + 1], in1=oc[:, qs, :], op0=ALU.mult, op1=ALU.add),
                                reads=[oks, rk, ock], writes=[sub(ak_, qs)])
                        for qs in range(4):
                            P.op("dve", lambda e, psOw=psOw, r=r, a_=a_, qs=qs: e.scalar_tensor_tensor(
                                out=a_[:, qs, :], in0=psOw[:, qs * 65:qs * 65 + 64], scalar=r[:, 4 + qs:5 + qs], in1=a_[:, qs, :], op0=ALU.mult, op1=ALU.add),
                                reads=[okw, rk, sub(ak_, qs)], writes=[sub(ak_, qs)])
                        P.op("pool", lambda e, a_=a_, z=z, y=y: e.tensor_tensor(out=y[:], in0=a_[:], in1=z[:], op=ALU.mult), reads=[ak_, zk], writes=[yk])
                        P.dma("pool", Y[i * 512:(i + 1) * 512, hq * 64:(hq + 1) * 64].rearrange("(q p) d -> p q d", p=128), y[:], reads=[yk], writes=[("Y", hq, i)], slot=yk)
        P.stack = st
        if cx is None:
            P.final_wait("sp", [("Y", hq, i) for hq in range(8) for i in range(NG)])
            P.emit()
    return nc


def _split3(v):
    v = np.asarray(v, np.float64)
    hi = v.astype(NPBF)
    r1 = v - hi.astype(np.float64)
    mid = r1.astype(NPBF)
    r2 = r1 - mid.astype(np.float64)
    lo = r2.astype(NPBF)
    return hi, mid, lo


def consts_C(S, hh):
    NCMP = (S - 32) // 16 + 1
    NCT = max(1, S // 2048)
    k = np.arange(128)[:, None]
    q = np.arange(512)[None, :]
    cm = np.stack([np.where(128 * m + k <= q, 0.0, NEGM) for m in range(4)], axis=1).astype(NPBF)
    wm = np.stack([np.where((128 * m + k <= q) & (128 * m + k > q - 512), 0.0, NEGM) for m in range(-4, 4)], axis=1).astype(NPBF)
    cmk = np.stack([np.where(16 * k + 31 <= 512 * dd + q, 0.0, NEGM) for dd in range(5)], axis=1).astype(NPBF)
    t = np.arange(S, dtype=np.float64)
    AQ = np.zeros((8, 9, S), NPBF)
    for hl in range(8):
        h = 8 * hh + hl
        slope = np.float64(np.float32(2.0 ** (-8.0 * (h + 1) / 16)))
        a, b, c = _split3(-slope * t)
        s1, s2, s3 = _split3(np.full(S, slope))
        AQ[hl] = np.stack([a, b, c, s1, s1, s2, s2, s3, s3])
    def krows(pos):
        pos = np.asarray(pos, np.float64)
        pa = (np.floor(pos / 128) * 128).astype(NPBF)
        pb = (pos % 128).astype(NPBF)
        one = np.ones(len(pos), NPBF)
        return np.stack([one, one, one, pa, pb, pa, pb, pa, pb])
    AK = krows(np.arange(S))
    AKc = krows(16 * np.arange(128 * NCT) + 31)
    ns = S // 64
    c0 = np.arange(128 * NCT)[:, None] * 16
    s0 = np.arange(128)[None, :] * 64
    ov = np.clip(np.minimum(c0 + 32, s0 + 64) - np.maximum(c0, s0), 0, None) / 16.0
    ov[NCMP:, :] = 0
    ov[:, ns:] = 0
    Mc = np.ascontiguousarray(ov.reshape(NCT, 128, 128).transpose(1, 0, 2)).astype(NPBF)
    tt = np.arange(S)[:, None]
    blk = np.arange(128)[None, :]
    cur = tt // 64
    valid = blk * 64 <= tt
    forced = (blk == 0) | (blk == cur) | (blk == cur - 1)
    ADD = np.where(valid, np.where(forced, 1e6, 0.0), -1e30).astype(np.float32)
    Ew = (np.arange(128)[:, None] == (np.arange(S)[None, :] // 64)).astype(NPBF)
    return dict(AQ=AQ, AK=AK, AKc=AKc, cmask=cm, wmask=wm, cmk=cmk, Mc=Mc, ADD=ADD, Ew=Ew, ident=np.eye(128, dtype=np.float32))


def inputs_C(x_b, norm_g, w_in, b_gate, pe_k, pe_v, wk1, wk2, wv1, wv2, hh, S):
    o_q, o_kc, o_vc, o_ks, o_vs, o_kw, o_vw, o_gl, o_z = 0, 1024, 1280, 1536, 1792, 2048, 2304, 2560, 2608
    g0 = 2 * hh
    gsl = lambda o: np.arange(o + g0 * 64, o + (g0 + 2) * 64)
    cols = np.concatenate([np.arange(o_q + hh * 512, o_q + (hh + 1) * 512), gsl(o_kc), gsl(o_vc), gsl(o_ks), gsl(o_kw),
                           gsl(o_vs), gsl(o_vw), np.arange(o_z + hh * 512, o_z + (hh + 1) * 512),
                           np.arange(o_gl + hh * 24, o_gl + (hh + 1) * 24)])
    d = dict(x=np.ascontiguousarray(x_b), gcol=np.ascontiguousarray(norm_g.reshape(8, 128).T), wC=np.ascontiguousarray(w_in[:, cols]),
             bg=np.ascontiguousarray(np.broadcast_to(b_gate[hh * 24:(hh + 1) * 24][None, :], (128, 24))),
             peT=np.ascontiguousarray(np.stack([pe_k.T, pe_v.T])), w1=np.ascontiguousarray(np.stack([wk1, wv1])), w2=np.ascontiguousarray(np.stack([wk2, wv2])))
    d.update(consts_C(S, hh))
    return d


def build_F(S):
    nc = bass.Bass("TRN2", target_bir_lowering=False)
    x = nc.dram_tensor("x", [S, 1024], F32, kind="ExternalInput").ap()
    ident = nc.dram_tensor("ident", [128, 128], F32, kind="ExternalInput").ap()
    out = nc.dram_tensor("out", [S, 1024], F32, kind="ExternalOutput").ap()
    Y1 = nc.dram_tensor("Y1", [S, 1024], BF16, kind="Internal").ap()
    X1 = nc.dram_tensor("X1", [S, 1024], F32, kind="Internal").ap()
    Y2 = nc.dram_tensor("Y2", [S, 1024], BF16, kind="Internal").ap()
    with ExitStack() as st:
        P = Prog(nc, st)
        idb = load_consts(P, ident)
        cx = Ctx(nc, P, idb)
        for hh in range(2):
            build_A(S, cx=cx, sfx="_a%d" % hh, io=dict(x=x, Yf=Y1[:, hh * 256:(hh + 1) * 256], Yr=Y1[:, 512 + hh * 256:512 + (hh + 1) * 256]))
        build_O(S, False, cx=cx, sfx="_b", io=dict(y=Y1, x=x, out=X1))
        for hh in range(2):
            build_C(S, cx=cx, sfx="_c%d" % hh, io=dict(x=X1, Y=Y2[:, hh * 512:(hh + 1) * 512]))
        build_O(S, True, cx=cx, sfx="_d", io=dict(y=Y2, x=X1, out=out))
        P.stack = st
        P.emit()
    return nc


def inputs_F(xb, p, S):
    f32 = lambda a: np.ascontiguousarray(np.asarray(a, dtype=np.float32))
    m = dict(x=np.ascontiguousarray(xb), ident=np.eye(128, dtype=np.float32))
    gfin = np.ascontiguousarray(np.broadcast_to(f32(p["final_g"])[None, :], (128, 1024)))
    for hh in range(2):
        d = inputs_A(xb, f32(p["even_norm_g"])[0], f32(p["even_w_in"])[0], f32(p["even_b_f"])[0], f32(p["even_gn_g"])[0], hh)
        for k, v in d.items():
            if k not in ("x", "ident"):
                m[k + "_a%d" % hh] = v
        d = inputs_C(xb, f32(p["odd_norm_g"])[0], f32(p["odd_w_in"])[0], f32(p["odd_b_gate"])[0], f32(p["odd_pe_k"])[0], f32(p["odd_pe_v"])[0],
                     f32(p["odd_wk1"])[0], f32(p["odd_wk2"])[0], f32(p["odd_wv1"])[0], f32(p["odd_wv2"])[0], hh, S)
        for k, v in d.items():
            if k not in ("x", "ident"):
                m[k + "_c%d" % hh] = v
    m["w_b"] = f32(p["even_w_out"])[0]
    m["gf_b"] = gfin
    m["w_d"] = f32(p["odd_w_out"])[0]
    m["gf_d"] = gfin
    return m


_NC_CACHE = {}


def _get_nc(name, fn):
    if name not in _NC_CACHE:
        _NC_CACHE[name] = fn()
    return _NC_CACHE[name]


def _run(nc, maps):
    return run_bass_kernel_spmd(nc, maps, core_ids=list(range(len(maps)))).results


def _assemble_y(res, S):
    shards = []
    for b in range(BATCH):
        y0 = np.asarray(res[2 * b]["Y"])
        y1 = np.asarray(res[2 * b + 1]["Y"])
        shards.append((y0, y1))
    return shards


def kernel(x, even_norm_g, even_w_in, even_b_f, even_gn_g, even_w_out,
           odd_norm_g, odd_w_in, odd_b_gate, odd_pe_k, odd_pe_v, odd_wk1, odd_wk2, odd_wv1, odd_wv2, odd_w_out, final_g):
    f32 = lambda a: np.ascontiguousarray(np.asarray(a, dtype=np.float32))
    x = f32(x)
    S = x.shape[1]
    p = dict(even_norm_g=even_norm_g, even_w_in=even_w_in, even_b_f=even_b_f, even_gn_g=even_gn_g, even_w_out=even_w_out,
             odd_norm_g=odd_norm_g, odd_w_in=odd_w_in, odd_b_gate=odd_b_gate, odd_pe_k=odd_pe_k, odd_pe_v=odd_pe_v,
             odd_wk1=odd_wk1, odd_wk2=odd_wk2, odd_wv1=odd_wv1, odd_wv2=odd_wv2, odd_w_out=odd_w_out, final_g=final_g)
    ncF = _get_nc(("F", S), lambda: build_F(S))
    per_b = [inputs_F(x[b], p, S) for b in range(BATCH)]
    maps = [per_b[c // 2] for c in range(8)]
    res = _run(ncF, maps)
    out = np.stack([np.asarray(res[2 * b]["out"]) for b in range(BATCH)])
    return out.astype(np.float32)
```

```python
import numpy as np
import ml_dtypes
from contextlib import ExitStack
import concourse.bass as bass
import concourse.mybir as mybir
from concourse.bass_utils import run_bass_kernel_spmd

F32 = mybir.dt.float32
BF16 = mybir.dt.bfloat16
AF = mybir.ActivationFunctionType
ALU = mybir.AluOpType
AX = mybir.AxisListType
NPBF = ml_dtypes.bfloat16

SEQ = 8192
BATCH = 4
DM = 1024
NEGM = -30000.0
FLAT = False
DBG = {}


class Sub(tuple):
    pass


def sub(base, idx):
    return Sub((base, idx))


class Prog:
    ENG = ("pe", "act", "dve", "pool", "sp")

    def __init__(self, nc, stack):
        self.nc = nc
        self.stack = stack
        self.top = stack
        self.streams = {e: [] for e in self.ENG}
        self.esem = {e: stack.enter_context(nc.semaphore("es_" + e)) for e in self.ENG}
        self.ecnt = {e: 0 for e in self.ENG}
        self.waited = {e: {} for e in self.ENG}
        self.lastw = {}
        self.readers = {}
        self.slots = {}
        self.nops = 0

    def _uniq(self, name):
        self._names = getattr(self, "_names", {})
        n = self._names.get(name, 0)
        self._names[name] = n + 1
        return name if n == 0 else "%s_v%d" % (name, n)

    def sb(self, name, shape, dt):
        return self.stack.enter_context(self.nc.sbuf_tensor(self._uniq(name), list(shape), dt))

    def ps(self, name, shape, dt):
        return self.stack.enter_context(self.nc.psum_tensor(self._uniq(name), list(shape), dt))

    def _deps(self, reads, writes):
        toks = []
        subs = self._subs = getattr(self, "_subs", {})

        def w_of(k):
            t = self.lastw.get(k)
            if t is not None:
                toks.append(t)

        def r_of(k):
            r = self.readers.get(k)
            if r:
                toks.extend(r.items())

        for k in reads:
            w_of(k)
            if isinstance(k, Sub):
                subs.setdefault(k[0], set()).add(k)
                w_of(k[0])
            else:
                for sk_ in subs.get(k, ()):
                    w_of(sk_)
        for k in writes:
            w_of(k)
            r_of(k)
            if isinstance(k, Sub):
                subs.setdefault(k[0], set()).add(k)
                w_of(k[0])
                r_of(k[0])
            else:
                for sk_ in subs.get(k, ()):
                    w_of(sk_)
                    r_of(sk_)
        return toks

    def _waits(self, eng, toks):
        need = {}
        for sk, v in toks:
            if eng == "pe" and sk == ("e", "pe"):
                continue
            if v > need.get(sk, 0):
                need[sk] = v
        out = []
        w = self.waited[eng]
        for sk, v in need.items():
            if w.get(sk, 0) >= v:
                continue
            w[sk] = v
            out.append((sk, v))
        return out

    def _commit(self, tok, reads, writes):
        sk, v = tok
        for k in writes:
            self.lastw[k] = tok
            self.readers[k] = {}
        for k in reads:
            d = self.readers.setdefault(k, {})
            if v > d.get(sk, 0):
                d[sk] = v

    LIMIT = None

    def op(self, eng, fn, reads=(), writes=()):
        if self.LIMIT is not None and self.nops >= self.LIMIT:
            return
        toks = self._deps(reads, writes)
        waits = self._waits(eng, toks)
        self.ecnt[eng] += 1
        sk = ("e", eng)
        tok = (sk, self.ecnt[eng])
        self.streams[eng].append((waits, fn, sk, 1))
        self._commit(tok, reads, writes)
        self.nops += 1

    QMAP = {}
    slot_prefix = ""

    def _slot(self, eng, slot):
        self._slotmap = getattr(self, "_slotmap", {})
        self._nexti = getattr(self, "_nexti", {})
        key = (eng, slot)
        if key not in self._slotmap:
            i = self._nexti.get(eng, 0)
            self._nexti[eng] = i + 1
            pk = "%s%d" % (eng, i)
            if pk not in self.slots:
                self.slots[pk] = [self.top.enter_context(self.nc.semaphore("ds_" + pk)), 0]
            self._slotmap[key] = pk
        return self._slotmap[key]

    def dma(self, eng, out, in_, reads=(), writes=(), slot=None, **kw):
        assert slot is not None
        if self.LIMIT is not None and self.nops >= self.LIMIT:
            return
        eng = self.QMAP.get(eng, eng)
        slot = self._slot(eng, slot)
        toks = self._deps(reads, writes)
        waits = self._waits(eng, toks)
        s = self.slots[slot]
        s[1] += 16
        sk = ("s", slot)
        tok = (sk, s[1])
        self.streams[eng].append((waits, (lambda e, o=out, i=in_, kw=kw: e.dma_start(out=o, in_=i, **kw)), sk, 16))
        self._commit(tok, reads, writes)
        self.nops += 1

    def coll(self, eng, kind, in_ap, out_ap, groups, reads=(), writes=(), slot=None):
        slot = self._slot(eng, slot)
        toks = self._deps(reads, writes)
        waits = self._waits(eng, toks)
        s = self.slots[slot]
        s[1] += 16
        sk = ("s", slot)
        self.streams[eng].append((waits, (lambda e, i=in_ap, o=out_ap: e.collective_compute(kind, ALU.bypass, replica_groups=groups, ins=[i], outs=[o])), sk, 16))
        self._commit((sk, s[1]), reads, writes)
        self.nops += 1

    def _sem(self, sk):
        return self.esem[sk[1]] if sk[0] == "e" else self.slots[sk[1]][0]

    def barrier(self):
        toks = [(("e", e), self.ecnt[e]) for e in self.ENG if self.ecnt[e] > 0]
        toks += [(("s", s), v[1]) for s, v in self.slots.items() if v[1] > 0]
        for e in self.ENG:
            waits = self._waits(e, toks)
            self.streams[e].append((waits, None, None, 0))
        self._slotmap, self._nexti = {}, {}

    def final_wait(self, eng, keys):
        toks = [self.lastw[k] for k in keys if k in self.lastw]
        waits = self._waits(eng, toks)
        self.streams[eng].append((waits, None, None, 0))

    def emit(self):
        nc = self.nc
        awaited = {e: set() for e in self.ENG}
        for name in self.ENG:
            for waits, fn, sk, inc in self.streams[name]:
                for wsk, v in waits:
                    if wsk[0] == "e":
                        awaited[wsk[1]].add(v)
        rank = {e: {v: i + 1 for i, v in enumerate(sorted(awaited[e]))} for e in self.ENG}
        with nc.Block() as block:
            def mk(name):
                def body(e):
                    n = 0
                    for waits, fn, sk, inc in self.streams[name]:
                        for wsk, v in waits:
                            if wsk[0] == "e":
                                e.wait_ge(self.esem[wsk[1]], rank[wsk[1]][v])
                            else:
                                e.wait_ge(self.slots[wsk[1]][0], v)
                        if fn is not None:
                            ins = fn(e)
                            if sk[0] == "e":
                                n += 1
                                if n in rank[name]:
                                    ins.then_inc(self.esem[name], 1)
                            else:
                                ins.then_inc(self.slots[sk[1]][0], inc)
                return body
            block.tensor(mk("pe"))
            block.scalar(mk("act"))
            block.vector(mk("dve"))
            block.gpsimd(mk("pool"))
            block.sync(mk("sp"))


class Rot:
    def __init__(self, P, name, n, shape, dt, psum=False):
        self.items = []
        for i in range(n):
            if name == DBG.get("padname") and i == 1:
                P.sb(name + "_pad", [128, DBG.get("pad", 512)], F32)
            t = P.ps(f"{name}{i}", shape, dt) if psum else P.sb(f"{name}{i}", shape, dt)
            self.items.append((t, f"{name}{i}"))
        self.i = 0

    def next(self):
        it = self.items[self.i % len(self.items)]
        self.i += 1
        return it


class Ctx:
    def __init__(self, nc, P, idb):
        self.nc, self.P, self.idb = nc, P, idb
        self.scratch = {}


def _din(nc, cx, sfx, name, shape, dt=F32):
    return nc.dram_tensor(name + sfx, list(shape), dt, kind="ExternalInput").ap()


def _dsc(nc, cx, name, shape, dt):
    if cx is None:
        return nc.dram_tensor(name, list(shape), dt, kind="Internal").ap()
    if name not in cx.scratch:
        cx.scratch[name] = nc.dram_tensor(name, list(shape), dt, kind="Internal").ap()
    return cx.scratch[name]

def load_consts(P, ident_ap):
    idf = P.sb("idf", [128, 128], F32)
    idb = P.sb("idb", [128, 128], BF16)
    P.dma("sp", idf[:], ident_ap, writes=["idf"], slot="idf")
    P.op("dve", lambda e: e.tensor_copy(out=idb[:], in_=idf[:]), reads=["idf"], writes=["idb"])
    return idb


def load_weight_bf16(P, wb, w_ap, ncols, name, chunk=256):
    stg = Rot(P, name + "_stg", 2, [128, 8, chunk], F32)
    wv = w_ap.rearrange("(c p) n -> p c n", p=128)
    for i, c0 in enumerate(range(0, ncols, chunk)):
        cw = min(chunk, ncols - c0)
        t, k = stg.next()
        P.dma("sp" if i % 2 == 0 else "pool", t[:, :, 0:cw], wv[:, :, c0:c0 + cw], writes=[k], slot=k)
        eng = "dve" if i % 2 == 0 else "pool"
        P.op(eng, lambda e, t=t, c0=c0, cw=cw: e.tensor_copy(out=wb[:, :, c0:c0 + cw], in_=t[:, :, 0:cw]),
             reads=[k], writes=[(name, c0)])
    return [(name, c0) for c0 in range(0, ncols, chunk)]


class NormT:
    def __init__(self, P, idb, gcol_ap):
        self.P = P
        self.idb = idb
        self.xt = Rot(P, "nx", 4, [128, 1024], F32)
        self.sq = P.sb("nsq", [128, 1024], F32)
        self.st = Rot(P, "nst", 4, [128, 4], F32)
        self.hb = Rot(P, "nhb", 4, [128, 1024], BF16)
        self.pT = Rot(P, "npT", 4, [128, 8, 128], BF16, psum=True)
        self.gs = P.sb("ngs", [128, 8], F32)
        self.gB = P.sb("ngB", [128, 8, 128], F32)
        self.mh = P.sb("nmh", [128, 1], F32)
        P.dma("sp", self.gs[:], gcol_ap, writes=["ngs"], slot="ngs")
        P.op("pool", lambda e: e.memset(self.gB[:], 1.0), writes=["ngB"])
        P.op("pool", lambda e: e.memset(self.mh[:], -0.5), writes=["nmh"])
        for c in range(8):
            P.op("dve", lambda e, c=c: e.tensor_scalar(out=self.gB[:, c, :], in0=self.gB[:, c, :], scalar1=self.gs[:, c:c + 1],
                                                       scalar2=None, op0=ALU.mult), reads=["ngs", "ngB"], writes=["ngB"])

    def load(self, x_rows_ap, xkey_reads=()):
        P = self.P
        xt, xk = self.xt.next()
        P.dma("sp", xt[:], x_rows_ap, reads=list(xkey_reads), writes=[xk], slot=xk)
        return xt, xk

    def run4(self, xs, hT, hk):
        P = self.P
        bufs = [(self.st.next(), self.hb.next(), self.pT.next()) for _ in range(4)]
        for tt in range(4):
            (xt, xk), ((st, sk), (hb, hbk), _) = xs[tt], bufs[tt]
            P.op("dve", lambda e, xt=xt, st=st, hb=hb: e.scalar_tensor_tensor(out=hb[:], in0=xt[:], scalar=1.0, in1=xt[:], op0=ALU.mult, op1=ALU.mult,
                                                                             accum_out=st[:, 0:1]), reads=[xk], writes=[hbk, sk])
        for tt in range(4):
            (st, sk) = bufs[tt][0]
            P.op("dve", lambda e, st=st: e.tensor_scalar(out=st[:, 1:2], in0=st[:, 0:1], scalar1=1.0 / 1024, scalar2=1e-6, op0=ALU.mult, op1=ALU.add),
                 reads=[sk], writes=[sk])
        for tt in range(4):
            (st, sk) = bufs[tt][0]
            P.op("pool", lambda e, st=st: e.tensor_tensor(out=st[:, 2:3], in0=st[:, 1:2], in1=self.mh[:], op=ALU.pow), reads=[sk, "nmh"], writes=[sk])
        for tt in range(4):
            (xt, xk), ((st, sk), (hb, hbk), _) = xs[tt], bufs[tt]
            P.op("dve", lambda e, xt=xt, st=st, hb=hb: e.tensor_scalar(out=hb[:], in0=xt[:], scalar1=st[:, 2:3], scalar2=None, op0=ALU.mult),
                 reads=[xk, sk], writes=[hbk])
        for tt in range(4):
            (hb, hbk), (pT, pk) = bufs[tt][1], bufs[tt][2]
            for c in range(8):
                P.op("pe", lambda e, c=c, hb=hb, pT=pT: e.transpose(out=pT[:, c, :], in_=hb[:, c * 128:(c + 1) * 128], identity=self.idb[:]),
                     reads=[hbk, "idb"], writes=[pk])
        for tt in range(4):
            (pT, pk) = bufs[tt][2]
            P.op("dve", lambda e, pT=pT, tt=tt: e.tensor_tensor(out=hT[:, :, tt * 128:(tt + 1) * 128], in0=pT[:], in1=self.gB[:], op=ALU.mult),
                 reads=[pk, "ngB"], writes=[sub(hk, tt)])

    def run(self, xt, xk, hT_out_ap, hT_key):
        P = self.P
        st, sk = self.st.next()
        hb, hk = self.hb.next()
        pT, pk = self.pT.next()
        sq = self.sq
        P.op("dve", lambda e: e.scalar_tensor_tensor(out=sq[:], in0=xt[:], scalar=1.0, in1=xt[:], op0=ALU.mult, op1=ALU.mult,
                                                     accum_out=st[:, 0:1]), reads=[xk], writes=["nsq", sk])
        P.op("dve", lambda e: e.tensor_scalar(out=st[:, 1:2], in0=st[:, 0:1], scalar1=1.0 / 1024, scalar2=1e-6, op0=ALU.mult,
                                              op1=ALU.add), reads=[sk], writes=[sk])
        P.op("pool", lambda e: e.tensor_tensor(out=st[:, 2:3], in0=st[:, 1:2], in1=self.mh[:], op=ALU.pow),
             reads=[sk, "nmh"], writes=[sk])
        P.op("dve", lambda e: e.tensor_scalar(out=hb[:], in0=xt[:], scalar1=st[:, 2:3], scalar2=None, op0=ALU.mult),
             reads=[xk, sk], writes=[hk])
        for c in range(8):
            P.op("pe", lambda e, c=c: e.transpose(out=pT[:, c, :], in_=hb[:, c * 128:(c + 1) * 128], identity=self.idb[:]),
                 reads=[hk, "idb"], writes=[pk])
        P.op("dve", lambda e: e.tensor_tensor(out=hT_out_ap, in0=pT[:], in1=self.gB[:], op=ALU.mult),
             reads=[pk, "ngB"], writes=[hT_key])


class AttnCore:
    def __init__(self, P, idb, nO=2, skew=2):
        self.P = P
        self.idb = idb
        self.skew = skew
        self.psS = Rot(P, "aS", skew + 1, [128, 512], F32, psum=True)
        self.psO = Rot(P, "aO", nO, [128, 512], F32, psum=True)
        self.pT = Rot(P, "aP", skew + 2, [128, 512], BF16)

    def group(self, QT, qkeys, KT, kkeys, kd, q0, tiles, v_of, pv_extra=None):
        P = self.P
        SKEW = self.skew
        psO, ok = self.psO.next()
        last_for_qs = {}
        for ti, t in enumerate(tiles):
            for qs in range(t["qs_min"], t.get("qs_max", 4)):
                last_for_qs[qs] = ti
        n = len(tiles)
        pend = {}

        def emit_scores(ti):
            t = tiles[ti]
            j = t["j"]
            psS, sk = self.psS.next()
            pT, pk = self.pT.next()
            ex = t["extras"]
            c0, c1 = 128 * t["qs_min"], 128 * t.get("qs_max", 4)
            P.op("pe", lambda e, psS=psS, j=j, nx=len(ex), c0=c0, c1=c1: e.matmul(psS[:, c0:c1], lhsT=KT[0:kd, j * 128:(j + 1) * 128], rhs=QT[0:kd, q0 + c0:q0 + c1],
                                                                                start=True, stop=(nx == 0)),
                 reads=list(qkeys) + list(kkeys), writes=[sk])
            for xi, (la, ra, xk) in enumerate(ex):
                P.op("pe", lambda e, psS=psS, la=la, ra=ra, last=(xi == len(ex) - 1), c0=c0, c1=c1: e.matmul(psS[:, c0:c1], lhsT=la, rhs=ra[:, c0:c1], start=False, stop=last),
                     reads=list(xk), writes=[sk])
            P.op("act", lambda e, psS=psS, pT=pT, c0=c0, c1=c1: e.activation(out=pT[:, c0:c1], in_=psS[:, c0:c1], func=AF.Exp), reads=[sk], writes=[pk])
            pend[ti] = (pT, pk)

        def emit_pv(ti):
            t = tiles[ti]
            pT, pk = pend.pop(ti)
            va, vk = v_of(t["j"])
            for qs in range(t["qs_min"], t.get("qs_max", 4)):
                P.op("pe", lambda e, psO=psO, pT=pT, va=va, qs=qs, st=(ti == 0 and qs == tiles[0]["qs_min"]), sp=(last_for_qs[qs] == ti):
                     e.matmul(psO[:, qs * 65:(qs + 1) * 65], lhsT=pT[:, qs * 128:(qs + 1) * 128], rhs=va, start=st, stop=sp, skip_group_check=True),
                     reads=[pk] + list(vk), writes=[ok])
            if pv_extra is not None:
                pv_extra(ti, t, pT, pk)

        for step in range(n + SKEW):
            if step < n:
                emit_scores(step)
            if step - SKEW >= 0:
                emit_pv(step - SKEW)
        return psO, ok


def causal_tiles(i, diag_masks, extra_fn=None):
    tiles = []
    for j in range(4 * i + 4):
        ex = []
        if extra_fn is not None:
            ex.extend(extra_fn(j))
        m = j - 4 * i
        if m >= 0:
            ex.append(diag_masks(m))
        tiles.append(dict(j=j, extras=ex, qs_min=max(m, 0)))
    return tiles


def build_A(S, phases=(1, 2, 3), cx=None, sfx="", io=None):
    NT, NG = S // 128, S // 512
    nc = cx.nc if cx else bass.Bass("TRN2", target_bir_lowering=False)
    din = lambda n, sh, dt=F32: _din(nc, cx, sfx, n, sh, dt)
    x = io["x"] if io else din("x", [S, 1024])
    gcol = din("gcol", [128, 8])
    wA = din("wA", [1024, 2308])
    bfv = din("bf", [4, 1])
    gng = din("gng", [128, 256])
    ident = None if cx else din("ident", [128, 128])
    cmask = din("cmask", [128, 4, 512], BF16)
    rinner = din("rinner", [128, 4, 128])
    rcross = din("rcross", [64, 4, 128])
    rkdec = din("rkdec", [128, 4])
    rcdec = din("rcdec", [64, 4])
    if io:
        Yf, Yr = io["Yf"], io["Yr"]
    else:
        Y = nc.dram_tensor("Y", [S, 512], BF16, kind="ExternalOutput").ap()
        Yf, Yr = Y[:, 0:256], Y[:, 256:512]
    QKT = _dsc(nc, cx, "QKT", [1024, S], BF16)
    CQ = _dsc(nc, cx, "CQ", [4, 6, S], BF16)
    VT = _dsc(nc, cx, "VT", [S, 768], BF16)
    ZS = _dsc(nc, cx, "ZS", [S, 512], F32)

    with ExitStack() as st:
        P = cx.P if cx else Prog(nc, st)
        idb = cx.idb if cx else load_consts(P, ident)
        if cx:
            P.stack = st
            P.slot_prefix = sfx[:2]
            P.barrier()
        with ExitStack() as st1:
            P.stack = st if FLAT else st1
            wb = P.sb("wb", [128, 8, 2308], BF16)
            wkeys = load_weight_bf16(P, wb, wA, 2308, "wb")
            nt = NormT(P, idb, gcol)
            hT = Rot(P, "hT", 2, [128, 8, 512], BF16)
            psA = Rot(P, "psA", 3, [128, 512], F32, psum=True)
            fmst = Rot(P, "fmst", 3, [128, 512], BF16)
            tmst = Rot(P, "tmst", 2, [128, 768], BF16)
            zst = Rot(P, "zst", DBG.get("zst", 2), [128, 512], F32)
            bfs = P.sb("bfs", [4, 1], F32)
            ones4 = P.sb("ones4", [4, 512], F32)
            carry = P.sb("carry", [4, 1], F32)
            P.dma("sp", bfs[:], bfv, writes=["bfs"], slot="bfs")
            P.op("pool", lambda e: e.memset(ones4[:], 1.0), writes=["ones4"])
            P.op("pool", lambda e: e.memset(carry[:], 0.0), writes=["carry"])
            ft = {n: P.sb("ft_" + n, [4, 512], F32) for n in ("sg", "ls", "c", "r1", "r2")}
            fb = {n: P.sb("fb_" + n, [4, 512], BF16) for n in ("hi", "mid", "lo", "nhi", "nmid", "nlo")}
            def prep(g):
                h, hk = hT.next()
                xs = [nt.load(x[g * 512 + tt * 128:g * 512 + (tt + 1) * 128, :]) for tt in range(4)]
                nt.run4(xs, h, hk)
                return h, hk

            nxt = prep(0)
            for g in range(NG):
                h, hk = nxt
                if g + 1 < NG:
                    nxt = prep(g + 1)
                for ct in range(8):
                    ps, pk = psA.next()
                    for c in range(8):
                        P.op("pe", lambda e, ps=ps, c=c, ct=ct, h=h: e.matmul(ps[:], lhsT=wb[:, c, ct * 128:(ct + 1) * 128], rhs=h[:, c, :],
                                                                             start=(c == 0), stop=(c == 7)),
                             reads=[hk] + wkeys, writes=[pk])
                    sg, sgk = fmst.next()
                    scl = 0.125 if ct in (0, 1, 4, 5) else 1.0
                    if ct % 2 == 0:
                        P.op("act", lambda e, ps=ps, sg=sg, scl=scl: e.activation(out=sg[:], in_=ps[:], func=AF.Copy, scale=scl),
                             reads=[pk], writes=[sgk])
                    else:
                        P.op("dve", lambda e, ps=ps, sg=sg, scl=scl: e.tensor_scalar(out=sg[:], in0=ps[:], scalar1=scl, scalar2=None, op0=ALU.mult),
                             reads=[pk], writes=[sgk])
                    P.dma("pool", QKT[ct * 128:(ct + 1) * 128, g * 512:(g + 1) * 512], sg[:], reads=[sgk], writes=[("QKT", ct, g)], slot=sgk)
                ps, pk = psA.next()
                for c in range(8):
                    P.op("pe", lambda e, ps=ps, c=c, h=h: e.matmul(ps[0:4, :], lhsT=wb[:, c, 1024:1028], rhs=h[:, c, :], start=(c == 0), stop=(c == 7)),
                         reads=[hk] + wkeys, writes=[pk])
                P.op("act", lambda e, ps=ps: e.activation(out=ft["sg"][:], in_=ps[0:4, :], func=AF.Sigmoid, bias=bfs[:, 0:1]),
                     reads=[pk, "bfs"], writes=["ft_sg"])
                P.op("act", lambda e: e.activation(out=ft["ls"][:], in_=ft["sg"][:], func=AF.Ln), reads=["ft_sg"], writes=["ft_ls"])
                P.op("dve", lambda e: e.tensor_tensor_scan(out=ft["c"][:], data0=ones4[:], data1=ft["ls"][:], initial=carry[:, 0:1],
                                                           op0=ALU.mult, op1=ALU.add), reads=["ft_ls", "ones4", "carry"], writes=["ft_c"])
                P.op("dve", lambda e: e.tensor_copy(out=carry[:], in_=ft["c"][:, 511:512]), reads=["ft_c"], writes=["carry"])
                P.op("dve", lambda e: e.tensor_copy(out=fb["hi"][:], in_=ft["c"][:]), reads=["ft_c"], writes=["fb_hi"])
                P.op("dve", lambda e: e.tensor_tensor(out=ft["r1"][:], in0=ft["c"][:], in1=fb["hi"][:], op=ALU.subtract), reads=["ft_c", "fb_hi"], writes=["ft_r1"])
                P.op("dve", lambda e: e.tensor_copy(out=fb["mid"][:], in_=ft["r1"][:]), reads=["ft_r1"], writes=["fb_mid"])
                P.op("dve", lambda e: e.tensor_tensor(out=ft["r2"][:], in0=ft["r1"][:], in1=fb["mid"][:], op=ALU.subtract), reads=["ft_r1", "fb_mid"], writes=["ft_r2"])
                P.op("dve", lambda e: e.tensor_copy(out=fb["lo"][:], in_=ft["r2"][:]), reads=["ft_r2"], writes=["fb_lo"])
                for a, b_ in (("hi", "nhi"), ("mid", "nmid"), ("lo", "nlo")):
                    P.op("dve", lambda e, a=a, b_=b_: e.tensor_scalar(out=fb[b_][:], in0=fb[a][:], scalar1=-1.0, scalar2=None, op0=ALU.mult),
                         reads=["fb_" + a], writes=["fb_" + b_])
                for r, n in enumerate(("hi", "mid", "lo", "nhi", "nmid", "nlo")):
                    P.dma("pool", CQ[:, r, g * 512:(g + 1) * 512], fb[n][:], reads=["fb_" + n], writes=[("CQ", g)],
                          slot="fb_" + n)
                for tt in range(4):
                    r0 = g * 512 + tt * 128
                    tm, tmk = tmst.next()
                    zs, zk = zst.next()
                    for ci, (c0, cw) in enumerate(((0, 512), (512, 512), (1024, 256))):
                        ps, pk = psA.next()
                        for c in range(8):
                            P.op("pe", lambda e, ps=ps, c=c, h=h, tt=tt, c0=c0, cw=cw: e.matmul(ps[:, 0:cw], lhsT=h[:, c, tt * 128:(tt + 1) * 128],
                                                                                            rhs=wb[:, c, 1028 + c0:1028 + c0 + cw], start=(c == 0), stop=(c == 7)),
                                 reads=[hk] + wkeys, writes=[pk])
                        if ci == 0:
                            P.op("dve", lambda e, ps=ps, tm=tm: e.tensor_copy(out=tm[:, 0:512], in_=ps[:]), reads=[pk], writes=[tmk])
                        elif ci == 1:
                            P.op("dve", lambda e, ps=ps, tm=tm: e.tensor_copy(out=tm[:, 512:768], in_=ps[:, 0:256]), reads=[pk], writes=[tmk])
                            P.op("act", lambda e, ps=ps, zs=zs: e.activation(out=zs[:, 0:256], in_=ps[:, 256:512], func=AF.Silu), reads=[pk], writes=[zk, pk])
                        else:
                            P.op("act", lambda e, ps=ps, zs=zs: e.activation(out=zs[:, 256:512], in_=ps[:, 0:256], func=AF.Silu), reads=[pk], writes=[zk])
                    P.dma("pool", VT[r0:r0 + 128, :], tm[:], reads=[tmk], writes=[("VT", g)], slot=tmk)
                    P.dma("pool", ZS[r0:r0 + 128, :], zs[:], reads=[zk], writes=[("ZS", g)], slot=zk)
        P.stack = st
        P.barrier()
        allQKT = [("QKT", ct, g) for ct in range(8) for g in range(NG)]
        allCQ = [("CQ", g) for g in range(NG)]
        allVT = [("VT", g) for g in range(NG)]
        allZS = [("ZS", g) for g in range(NG)]
        with ExitStack() as st2:
          if 2 in phases:
              P.stack = st if FLAT else st2
              ac = AttnCore(P, idb, nO=2, skew=3)
              cm = P.sb("cm", [128, 4, 512], BF16)
              P.dma("sp", cm[:], cmask, writes=["cm"], slot="cm")
              QTb = Rot(P, "QT", 2, [70, S], BF16)
              KTb = Rot(P, "KT", 2, [70, S], BF16)
              Vb = Rot(P, "Va", 2, [128, NT, 65], BF16)
              for (t, k) in QTb.items + KTb.items:
                  P.op("pool", lambda e, t=t: e.memset(t[64:70, :], 1.0), writes=[k])
              for (t, k) in Vb.items:
                  P.op("pool", lambda e, t=t: e.memset(t[:, :, 64:65], 1.0), writes=[k])
              zt = Rot(P, "zt", 2, [128, 4, 64], F32)
              yt = Rot(P, "yt", 2, [128, 4, 64], BF16)
              rv = Rot(P, "rv", 2, [128, 4], F32)
              for hh in range(4):
                  QT, qk = QTb.next()
                  KT, kk = KTb.next()
                  Va, vk = Vb.next()
                  P.dma("sp", QT[0:64, :], QKT[hh * 64:(hh + 1) * 64, :], reads=allQKT, writes=[qk], slot=qk)
                  P.dma("sp", QT[64:67, :], CQ[hh, 0:3, :], reads=allCQ, writes=[qk], slot=qk)
                  P.dma("sp", KT[0:64, :], QKT[256 + hh * 64:256 + (hh + 1) * 64, :], reads=allQKT, writes=[kk], slot=kk)
                  P.dma("sp", KT[67:70, :], CQ[hh, 3:6, :], reads=allCQ, writes=[kk], slot=kk)
                  P.dma("sp", Va[:, :, 0:64], VT[:, hh * 64:(hh + 1) * 64].rearrange("(n p) d -> p n d", p=128), reads=allVT, writes=[vk], slot=vk)
                  for i in range(NG):
                      tiles = causal_tiles(i, lambda m: (idb[:], cm[:, m, :], ["idb", "cm"]))
                      psO, ok = ac.group(QT, [qk], KT, [kk], 70, i * 512, tiles, lambda j: (Va[:, j, :], [vk]))
                      z, zk = zt.next()
                      y, yk = yt.next()
                      r, rk = rv.next()
                      P.dma("sp", z[:], ZS[i * 512:(i + 1) * 512, hh * 64:(hh + 1) * 64].rearrange("(q p) d -> p q d", p=128),
                            reads=allZS, writes=[zk], slot=zk)
                      P.op("dve", lambda e, psO=psO, r=r: e.reciprocal(out=r[:], in_=psO[:, 64:260:65]), reads=[ok], writes=[rk])
                      for qs in range(4):
                          P.op("dve", lambda e, psO=psO, r=r, y=y, z=z, qs=qs: e.scalar_tensor_tensor(
                              out=y[:, qs, :], in0=psO[:, qs * 65:qs * 65 + 64], scalar=r[:, qs:qs + 1], in1=z[:, qs, :], op0=ALU.mult, op1=ALU.mult),
                              reads=[ok, rk, zk], writes=[sub(yk, qs)])
                      P.dma("pool", Yf[i * 512:(i + 1) * 512, hh * 64:(hh + 1) * 64].rearrange("(q p) d -> p q d", p=128), y[:],
                            reads=[yk], writes=[("Y", hh, i)], slot=yk)
        P.stack = st
        P.barrier()
        with ExitStack() as st3:
          if 3 in phases:
              P.stack = st if FLAT else st3
              Q4r = Rot(P, "Q4", 2, [64, 4, 512], BF16)
              K4r = Rot(P, "K4", 2, [64, 4, 512], BF16)
              QC4r = Rot(P, "QC4", 2, [64, 4, 512], BF16)
              VR = P.sb("VR", [128, NT, 256], BF16)
              KD = P.sb("KD", [128, NT, 256], BF16)
              inn = P.sb("inn", [128, 4, 128], F32)
              crs = P.sb("crs", [64, 4, 128], F32)
              kdc = P.sb("kdc", [128, 4], F32)
              cdc = P.sb("cdc", [64, 4], F32)
              gg = P.sb("gg", [128, 256], F32)
              Sf = P.sb("Sf", [64, 4, 64], F32)
              Sb = Rot(P, "Sb", 2, [64, 4, 64], BF16)
              P.dma("sp", VR[:], VT[:, 256:512].rearrange("(n p) d -> p n d", p=128), reads=allVT, writes=["VR"], slot="VR")
              P.dma("sp", KD[:], VT[:, 512:768].rearrange("(n p) d -> p n d", p=128), reads=allVT, writes=["KD"], slot="KD")
              for t, a, k in ((inn, rinner, "inn"), (crs, rcross, "crs"), (kdc, rkdec, "kdc"), (cdc, rcdec, "cdc"), (gg, gng, "gg")):
                  P.dma("sp", t[:], a, writes=[k], slot=k)
              for hh in range(4):
                  P.op("dve", lambda e, hh=hh: e.tensor_scalar(out=KD[:, :, hh * 64:(hh + 1) * 64], in0=KD[:, :, hh * 64:(hh + 1) * 64],
                                                               scalar1=kdc[:, hh:hh + 1], scalar2=None, op0=ALU.mult), reads=["KD", "kdc"], writes=["KD"])
              P.op("pool", lambda e: e.memset(Sf[:], 0.0), writes=["Sf"])
              psR = Rot(P, "psR", 2, [128, 4, 128], F32, psum=True)
              psU = Rot(P, "psU", 2, [128, 4, 128], F32, psum=True)
              psOr = Rot(P, "psOr", 2, [128, 8, 64], F32, psum=True)
              aTb = Rot(P, "aTb", 2, [128, 4, 128], BF16)
              osb = Rot(P, "osb", 2, [128, 4, 64], F32)
              sqb = P.sb("sqb", [128, 4, 64], F32)
              stt = Rot(P, "stt", 2, [128, 16], F32)
              zt2 = Rot(P, "zt2", 2, [128, 256], F32)
              yt2 = Rot(P, "yt2", 2, [128, 256], BF16)
              mh = P.sb("mh2", [128, 4], F32)
              P.op("pool", lambda e: e.memset(mh[:], -0.5), writes=["mh2"])
              for n in range(NT):
                  g, c4 = n // 4, n % 4
                  cs = slice(c4 * 128, (c4 + 1) * 128)
                  if c4 == 0:
                      Q4, q4k = Q4r.next()
                      K4, k4k = K4r.next()
                      QC4, qc4k = QC4r.next()
                      P.dma("sp", Q4[:], QKT[512:768, g * 512:(g + 1) * 512].rearrange("(h d) t -> d h t", d=64), reads=allQKT, writes=[q4k], slot=q4k)
                      P.dma("sp", K4[:], QKT[768:1024, g * 512:(g + 1) * 512].rearrange("(h d) t -> d h t", d=64), reads=allQKT, writes=[k4k], slot=k4k)
                      for cc in range(4):
                          P.op("pool", lambda e, cc=cc, Q4=Q4, QC4=QC4: e.tensor_tensor(out=QC4[:, :, cc * 128:(cc + 1) * 128], in0=Q4[:, :, cc * 128:(cc + 1) * 128],
                                                                                        in1=crs[:], op=ALU.mult), reads=[q4k, "crs"], writes=[qc4k])
                  sb_, sbk = Sb.next()
                  P.op("act", lambda e, sb_=sb_: e.copy(out=sb_[:], in_=Sf[:]), reads=["Sf"], writes=[sbk])
                  pr, prk = psR.next()
                  for hh in range(4):
                      P.op("pe", lambda e, pr=pr, hh=hh, cs=cs, K4=K4, Q4=Q4: e.matmul(pr[:, hh, :], lhsT=K4[:, hh, cs], rhs=Q4[:, hh, cs], start=True, stop=True),
                           reads=[k4k, q4k], writes=[prk])
                  at, atk = aTb.next()
                  P.op("dve", lambda e, pr=pr, at=at: e.tensor_tensor(out=at[:], in0=pr[:], in1=inn[:], op=ALU.mult), reads=[prk, "inn"], writes=[atk])
                  po, pok = psOr.next()
                  for hh in range(4):
                      P.op("pe", lambda e, po=po, at=at, hh=hh, n=n: e.matmul(po[:, hh, :], lhsT=at[:, hh, :], rhs=VR[:, n, hh * 64:(hh + 1) * 64], start=True, stop=False),
                           reads=[atk, "VR"], writes=[pok])
                      P.op("pe", lambda e, po=po, sb_=sb_, hh=hh, cs=cs, QC4=QC4: e.matmul(po[:, hh, :], lhsT=QC4[:, hh, cs], rhs=sb_[:, hh, :], start=False, stop=True),
                           reads=[qc4k, sbk], writes=[pok])
                  pu, puk = psU.next()
                  for hh in range(4):
                      P.op("pe", lambda e, pu=pu, hh=hh, n=n: e.matmul(pu[0:64, hh, 0:64], lhsT=KD[:, n, hh * 64:(hh + 1) * 64], rhs=VR[:, n, hh * 64:(hh + 1) * 64], start=True, stop=True),
                           reads=["KD", "VR"], writes=[puk])
                  for hh in range(4):
                      P.op("dve", lambda e, pu=pu, hh=hh: e.scalar_tensor_tensor(out=Sf[:, hh, :], in0=Sf[:, hh, :], scalar=cdc[:, hh:hh + 1],
                                                                                in1=pu[0:64, hh, 0:64], op0=ALU.mult, op1=ALU.add),
                           reads=["Sf", puk, "cdc", sbk], writes=["Sf"])
                  ob, obk = osb.next()
                  s_, sk_ = stt.next()
                  z, zk = zt2.next()
                  y, yk = yt2.next()
                  P.dma("sp", z[:], ZS[n * 128:(n + 1) * 128, 256:512], reads=allZS, writes=[zk], slot=zk)
                  P.op("act", lambda e, po=po, ob=ob: e.copy(out=ob[:], in_=po[:, 0:4, :]), reads=[pok], writes=[obk])
                  P.op("dve", lambda e, ob=ob, s_=s_: e.tensor_reduce(out=s_[:, 0:4], in_=ob[:], axis=AX.X, op=ALU.add), reads=[obk], writes=[sk_])
                  P.op("pool", lambda e, ob=ob: e.tensor_tensor(out=sqb[:], in0=ob[:], in1=ob[:], op=ALU.mult), reads=[obk], writes=["sqb"])
                  P.op("dve", lambda e, s_=s_: e.tensor_reduce(out=s_[:, 4:8], in_=sqb[:], axis=AX.X, op=ALU.add), reads=["sqb", sk_], writes=[sk_])
                  P.op("dve", lambda e, s_=s_: e.tensor_scalar(out=s_[:, 0:8], in0=s_[:, 0:8], scalar1=1.0 / 64, scalar2=None, op0=ALU.mult), reads=[sk_], writes=[sk_])
                  P.op("dve", lambda e, s_=s_: e.tensor_tensor(out=s_[:, 8:12], in0=s_[:, 0:4], in1=s_[:, 0:4], op=ALU.mult), reads=[sk_], writes=[sk_])
                  P.op("dve", lambda e, s_=s_: e.tensor_tensor(out=s_[:, 8:12], in0=s_[:, 4:8], in1=s_[:, 8:12], op=ALU.subtract), reads=[sk_], writes=[sk_])
                  P.op("dve", lambda e, s_=s_: e.tensor_scalar(out=s_[:, 8:12], in0=s_[:, 8:12], scalar1=1e-5, scalar2=None, op0=ALU.add), reads=[sk_], writes=[sk_])
                  P.op("pool", lambda e, s_=s_: e.tensor_tensor(out=s_[:, 12:16], in0=s_[:, 8:12], in1=mh[:], op=ALU.pow), reads=[sk_, "mh2"], writes=[sk_])
                  for hh in range(4):
                      P.op("dve", lambda e, ob=ob, s_=s_, hh=hh: e.tensor_scalar(out=ob[:, hh, :], in0=ob[:, hh, :], scalar1=s_[:, hh:hh + 1], scalar2=s_[:, 12 + hh:13 + hh],
                                                                              op0=ALU.subtract, op1=ALU.mult), reads=[sub(obk, hh), sk_], writes=[sub(obk, hh)])
                  P.op("pool", lambda e, ob=ob: e.tensor_tensor(out=ob[:].rearrange("p h d -> p (h d)"), in0=ob[:].rearrange("p h d -> p (h d)"), in1=gg[:], op=ALU.mult),
                       reads=[obk, "gg"], writes=[obk])
                  P.op("dve", lambda e, ob=ob, z=z, y=y: e.tensor_tensor(out=y[:], in0=ob[:].rearrange("p h d -> p (h d)"), in1=z[:], op=ALU.mult),
                       reads=[obk, zk], writes=[yk])
                  P.dma("pool", Yr[n * 128:(n + 1) * 128, :], y[:], reads=[yk], writes=[("Yr", n)], slot=yk)
        P.stack = st
        if cx is None:
            P.final_wait("sp", [("Y", hh, i) for hh in range(4) for i in range(NG)] + [("Yr", n) for n in range(NT)])
            P.emit()
    return nc


def consts_A(hh):
    k = np.arange(128)[:, None]
    q = np.arange(512)[None, :]
    cm = np.stack([np.where(128 * m + k <= q, 0.0, NEGM) for m in range(4)], axis=1).astype(NPBF)
    heads = np.arange(4) + 4 * hh
    lg = np.log(1.0 - 2.0 ** (-5.0 - heads.astype(np.float64)))
    i = np.arange(128)
    diff = i[None, :] - i[:, None]
    innerT = np.where(diff[None] >= 0, np.exp(lg[:, None, None] * np.maximum(diff, 0)[None]), 0.0)
    rinner = np.ascontiguousarray(innerT.transpose(1, 0, 2)).astype(np.float32)
    cross = np.exp(lg[:, None] * (i[None, :] + 1))
    rcross = np.ascontiguousarray(np.broadcast_to(cross[None, :, :], (64, 4, 128))).astype(np.float32)
    kdec = np.exp(lg[:, None] * (127 - i)[None, :])
    rkdec = np.ascontiguousarray(kdec.T).astype(np.float32)
    cdec = np.exp(lg * 128)
    rcdec = np.ascontiguousarray(np.broadcast_to(cdec[None, :], (64, 4))).astype(np.float32)
    return dict(cmask=cm, rinner=rinner, rcross=rcross, rkdec=rkdec, rcdec=rcdec, ident=np.eye(128, dtype=np.float32))


def inputs_A(x_b, norm_g, w_in, b_f, gn_g, hh):
    sl = lambda o, h0, n: np.arange(o + h0 * 64, o + (h0 + n) * 64)
    o_qf, o_kf, o_vf, o_fl, o_qr, o_kr, o_vr, o_z = 0, 512, 1024, 1536, 1544, 2056, 2568, 3080
    h0 = 4 * hh
    cols = np.concatenate([sl(o_qf, h0, 4), sl(o_kf, h0, 4), sl(o_qr, h0, 4), sl(o_kr, h0, 4),
                           np.arange(o_fl + h0, o_fl + h0 + 4),
                           sl(o_vf, h0, 4), sl(o_vr, h0, 4), sl(o_kr, h0, 4),
                           sl(o_z, h0, 4), sl(o_z + 512, h0, 4)])
    d = dict(x=np.ascontiguousarray(x_b), gcol=np.ascontiguousarray(norm_g.reshape(8, 128).T),
             wA=np.ascontiguousarray(w_in[:, cols]), bf=np.ascontiguousarray(b_f[h0:h0 + 4].reshape(4, 1)),
             gng=np.ascontiguousarray(np.broadcast_to(gn_g[h0 * 64:(h0 + 4) * 64][None, :], (128, 256))))
    d.update(consts_A(hh))
    return d


def build_O(T, final, cx=None, sfx="", io=None):
    NTT = T // 128
    nc = cx.nc if cx else bass.Bass("TRN2", target_bir_lowering=False)
    din = lambda n, sh, dt=F32: _din(nc, cx, sfx, n, sh, dt)
    y = io["y"] if io else din("y", [T, 1024], BF16)
    x = io["x"] if io else din("x", [T, 1024])
    w = din("w", [1024, 1024])
    gf = din("gf", [128, 1024])
    ident = None if cx else din("ident", [128, 128])
    out = io["out"] if io else nc.dram_tensor("out", [T, 1024], F32, kind="ExternalOutput").ap()
    with ExitStack() as st:
        P = cx.P if cx else Prog(nc, st)
        idb = cx.idb if cx else load_consts(P, ident)
        if cx:
            P.stack = st
            P.slot_prefix = sfx[:2]
            P.barrier()
        wb = P.sb("wb", [128, 8, 1024], BF16)
        wkeys = load_weight_bf16(P, wb, w, 1024, "wb")
        gft = P.sb("gft", [128, 1024], F32)
        mh = P.sb("mh", [128, 1], F32)
        if final:
            P.dma("sp", gft[:], gf, writes=["gft"], slot="gft")
            P.op("pool", lambda e: e.memset(mh[:], -0.5), writes=["mh"])
        yt = Rot(P, "yt", 3, [128, 1024], BF16)
        xt = Rot(P, "xt", 3, [128, 1024], F32)
        ot = Rot(P, "ot", 2, [128, 1024], F32)
        sq = P.sb("sq", [128, 1024], F32)
        stt = Rot(P, "stt", 2, [128, 4], F32)
        pT = Rot(P, "pT", 2, [128, 8, 128], BF16, psum=True)
        pO = Rot(P, "pO", 4, [128, 512], F32, psum=True)
        yT = Rot(P, "yT", 3, [128, 8, 128], BF16)

        def prep(t):
            r = slice(t * 128, (t + 1) * 128)
            yb, yk = yt.next()
            xb, xk = xt.next()
            P.dma("sp", yb[:], y[r, :], writes=[yk], slot=yk)
            P.dma("sp", xb[:], x[r, :], writes=[xk], slot=xk)
            pt, ptk = pT.next()
            for c in range(8):
                P.op("pe", lambda e, c=c, pt=pt, yb=yb: e.transpose(out=pt[:, c, :], in_=yb[:, c * 128:(c + 1) * 128], identity=idb[:]),
                     reads=[yk, "idb"], writes=[ptk])
            ytt, ytk = yT.next()
            P.op("act", lambda e, pt=pt, ytt=ytt: e.copy(out=ytt[:], in_=pt[:]), reads=[ptk], writes=[ytk])
            return (xb, xk, ytt, ytk)

        def mm(t, st_):
            xb, xk, ytt, ytk = st_
            r = slice(t * 128, (t + 1) * 128)
            ob, obk = ot.next()
            for hf in range(2):
                po, pok = pO.next()
                for c in range(8):
                    P.op("pe", lambda e, c=c, po=po, ytt=ytt, hf=hf: e.matmul(po[:], lhsT=ytt[:, c, :], rhs=wb[:, c, hf * 512:(hf + 1) * 512],
                                                                             start=(c == 0), stop=(c == 7)), reads=[ytk] + wkeys, writes=[pok])
                P.op("dve", lambda e, po=po, ob=ob, xb=xb, hf=hf: e.tensor_tensor(out=ob[:, hf * 512:(hf + 1) * 512], in0=po[:], in1=xb[:, hf * 512:(hf + 1) * 512],
                                                                                op=ALU.add), reads=[pok, xk], writes=[obk])
            if final:
                s_, sk_ = stt.next()
                P.op("dve", lambda e, ob=ob, s_=s_: e.scalar_tensor_tensor(out=sq[:], in0=ob[:], scalar=1.0, in1=ob[:], op0=ALU.mult, op1=ALU.mult,
                                                                         accum_out=s_[:, 0:1]), reads=[obk], writes=["sq", sk_])
                P.op("dve", lambda e, s_=s_: e.tensor_scalar(out=s_[:, 1:2], in0=s_[:, 0:1], scalar1=1.0 / 1024, scalar2=1e-6, op0=ALU.mult, op1=ALU.add),
                     reads=[sk_], writes=[sk_])
                P.op("pool", lambda e, s_=s_: e.tensor_tensor(out=s_[:, 2:3], in0=s_[:, 1:2], in1=mh[:], op=ALU.pow), reads=[sk_, "mh"], writes=[sk_])
                P.op("dve", lambda e, ob=ob, s_=s_: e.scalar_tensor_tensor(out=ob[:], in0=ob[:], scalar=s_[:, 2:3], in1=gft[:], op0=ALU.mult, op1=ALU.mult),
                     reads=[obk, sk_, "gft"], writes=[obk])
            P.dma("pool", out[r, :], ob[:], reads=[obk], writes=[("out", t)], slot=obk)

        nxt = prep(0)
        for t in range(NTT):
            cur = nxt
            if t + 1 < NTT:
                nxt = prep(t + 1)
            mm(t, cur)
        if cx is None or final:
            P.final_wait("sp", [("out", t) for t in range(NTT)])
        if cx is None:
            P.emit()
    return nc


def build_C(S, cx=None, sfx="", io=None):
    NT, NG = S // 128, S // 512
    NCMP = (S - 32) // 16 + 1
    NCT = max(1, S // 2048)
    nc = cx.nc if cx else bass.Bass("TRN2", target_bir_lowering=False)
    din = lambda n, sh, dt=F32: _din(nc, cx, sfx, n, sh, dt)
    x = io["x"] if io else din("x", [S, 1024])
    gcol = din("gcol", [128, 8])
    wC = din("wC", [1024, 1816])
    bg = din("bg", [128, 24])
    ident = None if cx else din("ident", [128, 128])
    peT = din("peT", [2, 64, 32])
    w1 = din("w1", [2, 2048, 256])
    w2 = din("w2", [2, 256, 64])
    AQ = din("AQ", [8, 9, S], BF16)
    AK = din("AK", [9, S], BF16)
    AKc = din("AKc", [9, 128 * NCT], BF16)
    cmask = din("cmask", [128, 4, 512], BF16)
    wmask = din("wmask", [128, 8, 512], BF16)
    cmk = din("cmk", [128, 5, 512], BF16)
    Mc = din("Mc", [128, NCT, 128], BF16)
    ADD = din("ADD", [S, 128])
    Ew = din("Ew", [128, S], BF16)
    Y = io["Y"] if io else nc.dram_tensor("Y", [S, 512], BF16, kind="ExternalOutput").ap()
    QKT = _dsc(nc, cx, "QKT", [1024, S], BF16)
    VT = _dsc(nc, cx, "VT2", [S, 256], BF16)
    ZS = _dsc(nc, cx, "ZS", [S, 512], F32)
    GL = _dsc(nc, cx, "GL", [S, 24], F32)
    OC = _dsc(nc, cx, "OC", [S, 512], F32)

    with ExitStack() as st:
        P = cx.P if cx else Prog(nc, st)
        idb = cx.idb if cx else load_consts(P, ident)
        if cx:
            P.stack = st
            P.slot_prefix = sfx[:2]
            P.barrier()
        KcT = [P.sb(f"KcT{g}", [73, 128 * NCT], BF16) for g in range(2)]
        VcA = [P.sb(f"VcA{g}", [128, NCT, 65], BF16) for g in range(2)]
        selT = [P.sb(f"selT{g}", [128, S], BF16) for g in range(2)]
        with ExitStack() as st1:
            P.stack = st1
            wb = P.sb("wb", [128, 8, 1816], BF16)
            wkeys = load_weight_bf16(P, wb, wC, 1816, "wb")
            nt = NormT(P, idb, gcol)
            hT = Rot(P, "hT", 2, [128, 8, 512], BF16)
            psA = Rot(P, "psA", 3, [128, 512], F32, psum=True)
            fmst = Rot(P, "fmst", 3, [128, 512], BF16)
            tmst = Rot(P, "tmst", 2, [128, 256], BF16)
            zst = Rot(P, "zst", 2, [128, 512], F32)
            gst = Rot(P, "gst", 2, [128, 24], F32)
            bgs = P.sb("bgs", [128, 24], F32)
            P.dma("sp", bgs[:], bg, writes=["bgs"], slot="bgs")
            def prep(g):
                h, hk = hT.next()
                xs = [nt.load(x[g * 512 + tt * 128:g * 512 + (tt + 1) * 128, :]) for tt in range(4)]
                nt.run4(xs, h, hk)
                return h, hk

            nxt = prep(0)
            for g in range(NG):
                h, hk = nxt
                if g + 1 < NG:
                    nxt = prep(g + 1)
                for ct in range(8):
                    ps, pk = psA.next()
                    for c in range(8):
                        P.op("pe", lambda e, ps=ps, c=c, ct=ct, h=h: e.matmul(ps[:], lhsT=wb[:, c, ct * 128:(ct + 1) * 128], rhs=h[:, c, :],
                                                                             start=(c == 0), stop=(c == 7)), reads=[hk] + wkeys, writes=[pk])
                    sg, sgk = fmst.next()
                    scl = 0.125 if ct < 4 else 1.0
                    if ct % 2 == 0:
                        P.op("act", lambda e, ps=ps, sg=sg, scl=scl: e.activation(out=sg[:], in_=ps[:], func=AF.Copy, scale=scl), reads=[pk], writes=[sgk])
                    else:
                        P.op("dve", lambda e, ps=ps, sg=sg, scl=scl: e.tensor_scalar(out=sg[:], in0=ps[:], scalar1=scl, scalar2=None, op0=ALU.mult),
                             reads=[pk], writes=[sgk])
                    P.dma("pool", QKT[ct * 128:(ct + 1) * 128, g * 512:(g + 1) * 512], sg[:], reads=[sgk], writes=[("QKT", ct, g)], slot=sgk)
                for tt in range(4):
                    r0 = g * 512 + tt * 128
                    tm, tmk = tmst.next()
                    zs, zk = zst.next()
                    gs, gk = gst.next()
                    ps, pk = psA.next()
                    for c in range(8):
                        P.op("pe", lambda e, ps=ps, c=c, h=h, tt=tt: e.matmul(ps[:], lhsT=h[:, c, tt * 128:(tt + 1) * 128], rhs=wb[:, c, 1024:1536],
                                                                             start=(c == 0), stop=(c == 7)), reads=[hk] + wkeys, writes=[pk])
                    P.op("dve", lambda e, ps=ps, tm=tm: e.tensor_copy(out=tm[:], in_=ps[:, 0:256]), reads=[pk], writes=[tmk])
                    P.op("act", lambda e, ps=ps, zs=zs: e.activation(out=zs[:, 0:256], in_=ps[:, 256:512], func=AF.Silu), reads=[pk], writes=[zk, pk])
                    ps, pk = psA.next()
                    for c in range(8):
                        P.op("pe", lambda e, ps=ps, c=c, h=h, tt=tt: e.matmul(ps[:, 0:280], lhsT=h[:, c, tt * 128:(tt + 1) * 128], rhs=wb[:, c, 1536:1816],
                                                                             start=(c == 0), stop=(c == 7)), reads=[hk] + wkeys, writes=[pk])
                    P.op("act", lambda e, ps=ps, zs=zs: e.activation(out=zs[:, 256:512], in_=ps[:, 0:256], func=AF.Silu), reads=[pk], writes=[zk])
                    P.op("dve", lambda e, ps=ps, gs=gs: e.tensor_tensor(out=gs[:], in0=ps[:, 256:280], in1=bgs[:], op=ALU.add), reads=[pk, "bgs"], writes=[gk, pk])
                    P.op("act", lambda e, gs=gs: e.activation(out=gs[:], in_=gs[:], func=AF.Sigmoid), reads=[gk], writes=[gk])
                    P.dma("pool", VT[r0:r0 + 128, :], tm[:], reads=[tmk], writes=[("VT", g)], slot=tmk)
                    P.dma("pool", ZS[r0:r0 + 128, :], zs[:], reads=[zk], writes=[("ZS", g)], slot=zk)
                    P.dma("pool", GL[r0:r0 + 128, :], gs[:], reads=[gk], writes=[("GL", g)], slot=gk)
        P.stack = st
        P.barrier()
        allQKT = [("QKT", ct, g) for ct in range(8) for g in range(NG)]
        allVT = [("VT", g) for g in range(NG)]
        allZS = [("ZS", g) for g in range(NG)]
        allGL = [("GL", g) for g in range(NG)]
        with ExitStack() as st2:
            P.stack = st2
            ATr = Rot(P, "AT", 2, [64, S], BF16)
            w1s = Rot(P, "w1s", 2, [64, 16, 256], F32)
            w1b = [P.sb(f"w1b{k}", [64, 32, 256], BF16) for k in range(2)]
            w2s = P.sb("w2s", [128, 2, 2, 64], F32)
            w2b = P.sb("w2b", [128, 2, 2, 64], BF16)
            pes = P.sb("pes", [64, 2, 32], F32)
            peb = P.sb("peb", [64, 2, 32], BF16)
            bias = P.sb("cbias", [128, 2, 2], F32)
            hidT = Rot(P, "hidT", 2, [128, 2, 512], BF16)
            psH = Rot(P, "psH", 2, [128, 512], F32, psum=True)
            psK = P.ps("psK", [128, 512], F32)
            psV = P.ps("psV", [128, 512], F32)
            psB = P.ps("psB", [128, 512], F32)
            for k in range(2):
                for half in range(2):
                    t, tk = w1s.next()
                    P.dma("sp", t[:], w1[k, half * 1024:(half + 1) * 1024, :].rearrange("(l d) h -> d l h", d=64), writes=[tk], slot=tk)
                    P.op("dve", lambda e, t=t, k=k, half=half: e.tensor_copy(out=w1b[k][:, half * 16:(half + 1) * 16, :], in_=t[:]), reads=[tk], writes=[f"w1b{k}"])
            P.dma("sp", w2s[:], w2.rearrange("k (c p) d -> p k c d", p=128), writes=["w2s"], slot="w2s")
            P.op("dve", lambda e: e.tensor_copy(out=w2b[:], in_=w2s[:]), reads=["w2s"], writes=["w2b"])
            P.dma("sp", pes[:], peT.rearrange("k d l -> d k l"), writes=["pes"], slot="pes")
            P.op("dve", lambda e: e.tensor_copy(out=peb[:], in_=pes[:]), reads=["pes"], writes=["peb"])
            for (t, tk) in hidT.items:
                P.op("pool", lambda e, t=t: e.memset(t[:], 0.0), writes=[tk])
            for g in range(2):
                P.op("pool", lambda e, g=g: e.memset(VcA[g][:, :, 64:65], 1.0), writes=[f"VcA{g}"])
            for k in range(2):
                for hh2 in range(2):
                    for l in range(32):
                        P.op("pe", lambda e, k=k, hh2=hh2, l=l: e.matmul(psB[:, (k * 2 + hh2):(k * 2 + hh2) + 1], lhsT=w1b[k][:, l, hh2 * 128:(hh2 + 1) * 128],
                                                                        rhs=peb[:, k, l:l + 1], start=(l == 0), stop=(l == 31)),
                             reads=[f"w1b{k}", "peb"], writes=["psB"])
            P.op("dve", lambda e: e.tensor_copy(out=bias[:].rearrange("p a b -> p (a b)"), in_=psB[:, 0:4]), reads=["psB"], writes=["cbias"])
            for g in range(2):
                for k in range(2):
                    AT, atk = ATr.next()
                    row0 = (512 if k == 0 else 640) + g * 64
                    P.dma("sp", AT[:], QKT[row0:row0 + 64, :], reads=allQKT, writes=[atk], slot=atk)
                    hd, hdk = hidT.next()
                    for hh2 in range(2):
                        ps, pk = psH.next()
                        for l in range(32):
                            P.op("pe", lambda e, ps=ps, k=k, hh2=hh2, l=l, AT=AT: e.matmul(ps[:, 0:NCMP], lhsT=w1b[k][:, l, hh2 * 128:(hh2 + 1) * 128],
                                                                                        rhs=AT[:, l:l + 16 * (NCMP - 1) + 1:16], start=(l == 0), stop=(l == 31)),
                                 reads=[f"w1b{k}", atk], writes=[pk])
                        P.op("act", lambda e, ps=ps, hd=hd, hh2=hh2, k=k: e.activation(out=hd[:, hh2, 0:NCMP], in_=ps[:, 0:NCMP], func=AF.Silu, bias=bias[:, k, hh2:hh2 + 1]),
                             reads=[pk, "cbias"], writes=[hdk])
                    if k == 0:
                        for hh2 in range(2):
                            P.op("pe", lambda e, hd=hd, hh2=hh2: e.matmul(psK[0:64, 0:128 * NCT], lhsT=w2b[:, 0, hh2, :], rhs=hd[:, hh2, 0:128 * NCT], start=(hh2 == 0), stop=(hh2 == 1)),
                                 reads=[hdk, "w2b"], writes=["psK"])
                        P.op("dve", lambda e, g=g: e.tensor_copy(out=KcT[g][0:64, :], in_=psK[0:64, 0:128 * NCT]), reads=["psK"], writes=[f"KcT{g}"])
                        P.dma("sp", KcT[g][64:73, :], AKc, writes=[f"KcT{g}"], slot=f"KcT{g}")
                    else:
                        for ct in range(NCT):
                            for hh2 in range(2):
                                P.op("pe", lambda e, hd=hd, hh2=hh2, ct=ct: e.matmul(psV[:, ct * 64:(ct + 1) * 64], lhsT=hd[:, hh2, ct * 128:(ct + 1) * 128], rhs=w2b[:, 1, hh2, :],
                                                                                    start=(hh2 == 0), stop=(hh2 == 1)), reads=[hdk, "w2b"], writes=["psV"])
                        P.op("dve", lambda e, g=g: e.tensor_copy(out=VcA[g][:, :, 0:64], in_=psV[:, 0:64 * NCT].rearrange("p (c d) -> p c d", d=64)),
                             reads=["psV"], writes=[f"VcA{g}"])
        P.stack = st
        P.barrier()
        with ExitStack() as st3:
            P.stack = st3
            ac = AttnCore(P, idb)
            psI = Rot(P, "psI", 2, [128, 4, 128], F32, psum=True)
            psT = P.ps("psT", [128, 8, 128], BF16)
            cmks = P.sb("cmks", [128, 5, 512], BF16)
            Ms = P.sb("Ms", [128, NCT, 128], BF16)
            P.dma("sp", cmks[:], cmk, writes=["cmks"], slot="cmks")
            P.dma("sp", Ms[:], Mc, writes=["Ms"], slot="Ms")
            QTb = Rot(P, "QT", 4, [73, 512], BF16)
            glt = Rot(P, "glt", 2, [128, 4, 24], F32)
            adt = Rot(P, "adt", 2, [128, 4, 128], F32)
            oct_ = Rot(P, "oct", 2, [128, 4, 64], F32)
            rv = Rot(P, "rv", 2, [128, 4], F32)
            imp = P.sb("imp", [128, 4, 128], F32)
            wk = P.sb("wk", [128, 4, 128], F32)
            m8 = P.sb("m8", [128, 4, 16], F32)
            sb16 = P.sb("sb16", [128, 4, 128], BF16)
            for g in range(2):
                for i in range(NG):
                    gl_, glk = glt.next()
                    ad, adk = adt.next()
                    P.dma("sp", gl_[:], GL[i * 512:(i + 1) * 512, :].rearrange("(q p) c -> p q c", p=128), reads=allGL, writes=[glk], slot=glk)
                    P.dma("sp", ad[:], ADD[i * 512:(i + 1) * 512, :].rearrange("(q p) c -> p q c", p=128), writes=[adk], slot=adk)
                    jcs = list(range(0, min(i // 4, NCT - 1) + 1))
                    for hl in range(4):
                        hq = g * 4 + hl
                        QT, qk = QTb.next()
                        P.dma("sp", QT[0:64, :], QKT[hq * 64:(hq + 1) * 64, i * 512:(i + 1) * 512], reads=allQKT, writes=[qk], slot=qk)
                        P.dma("sp", QT[64:73, :], AQ[hq, :, i * 512:(i + 1) * 512], writes=[qk], slot=qk)
                        tiles = []
                        for jc in jcs:
                            dd = (512 * i - 2048 * jc) // 512
                            ex = [(idb[:], cmks[:, dd, :], ["idb", "cmks"])] if dd <= 4 else []
                            tiles.append(dict(j=jc, extras=ex, qs_min=0))
                        pI, pik = psI.next()

                        def pv_extra(ti, t, pT, pk, pI=pI, pik=pik, ntl=len(tiles)):
                            for qs in range(4):
                                P.op("pe", lambda e, pI=pI, pT=pT, qs=qs, jc=t["j"], st_=(ti == 0 and qs == 0), sp_=(ti == ntl - 1):
                                     e.matmul(pI[:, qs, :], lhsT=pT[:, qs * 128:(qs + 1) * 128], rhs=Ms[:, jc, :], start=st_, stop=sp_, skip_group_check=True),
                                     reads=[pk, "Ms"], writes=[pik])
                        psO, ok = ac.group(QT, [qk], KcT[g], [f"KcT{g}"], 73, 0, tiles, lambda j, g=g: (VcA[g][:, j, :], [f"VcA{g}"]), pv_extra=pv_extra)
                        r, rk = rv.next()
                        oc, ock = oct_.next()
                        P.op("dve", lambda e, psO=psO, r=r: e.tensor_scalar(out=r[:], in0=psO[:, 64:260:65], scalar1=1e-30, scalar2=None, op0=ALU.max), reads=[ok], writes=[rk])
                        P.op("dve", lambda e, r=r: e.reciprocal(out=r[:], in_=r[:]), reads=[rk], writes=[rk])
                        for qs in range(4):
                            P.op("dve", lambda e, psO=psO, r=r, oc=oc, gl_=gl_, qs=qs, hq=hq: e.tensor_scalar(
                                out=oc[:, qs, :], in0=psO[:, qs * 65:qs * 65 + 64], scalar1=r[:, qs:qs + 1], scalar2=gl_[:, qs, hq * 3:hq * 3 + 1], op0=ALU.mult, op1=ALU.mult),
                                reads=[ok, rk, glk], writes=[sub(ock, qs)])
                        P.dma("pool", OC[i * 512:(i + 1) * 512, hq * 64:(hq + 1) * 64].rearrange("(q p) d -> p q d", p=128), oc[:], reads=[ock], writes=[("OC", hq, i)], slot=ock)
                        for qs in range(4):
                            if hl == 0:
                                P.op("dve", lambda e, pI=pI, r=r, qs=qs: e.tensor_scalar(out=imp[:, qs, :], in0=pI[:, qs, :], scalar1=r[:, qs:qs + 1], scalar2=None, op0=ALU.mult),
                                     reads=[pik, rk], writes=[sub("imp", qs)])
                            else:
                                P.op("dve", lambda e, pI=pI, r=r, qs=qs: e.scalar_tensor_tensor(out=imp[:, qs, :], in0=pI[:, qs, :], scalar=r[:, qs:qs + 1], in1=imp[:, qs, :],
                                                                                              op0=ALU.mult, op1=ALU.add), reads=[pik, rk, sub("imp", qs)], writes=[sub("imp", qs)])
                    P.op("dve", lambda e, ad=ad: e.tensor_tensor(out=imp[:], in0=imp[:], in1=ad[:], op=ALU.add), reads=["imp", adk], writes=["imp"])
                    for qs in range(4):
                        P.op("dve", lambda e, qs=qs: e.max(out=m8[:, qs, 0:8], in_=imp[:, qs, :]), reads=[sub("imp", qs)], writes=[sub("m8", qs)])
                    for qs in range(4):
                        P.op("dve", lambda e, qs=qs: e.match_replace(out=wk[:, qs, :], in_to_replace=m8[:, qs, 0:8], in_values=imp[:, qs, :], imm_value=-3.0e38),
                             reads=[sub("imp", qs), sub("m8", qs)], writes=[sub("wk", qs)])
                    for qs in range(4):
                        P.op("dve", lambda e, qs=qs: e.max(out=m8[:, qs, 8:16], in_=wk[:, qs, :]), reads=[sub("wk", qs)], writes=[sub("m8", qs)])
                    for qs in range(4):
                        P.op("dve", lambda e, qs=qs: e.tensor_scalar(out=wk[:, qs, :], in0=imp[:, qs, :], scalar1=m8[:, qs, 15:16], scalar2=None, op0=ALU.is_ge),
                             reads=[sub("imp", qs), sub("m8", qs)], writes=[sub("wk", qs)])
                    P.op("dve", lambda e: e.tensor_scalar(out=sb16[:], in0=wk[:], scalar1=-1.0, scalar2=-NEGM, op0=ALU.add, op1=ALU.mult), reads=["wk"], writes=["sb16"])
                    for qs in range(4):
                        P.op("pe", lambda e, qs=qs: e.transpose(out=psT[:, qs, :], in_=sb16[:, qs, :], identity=idb[:]), reads=["sb16", "idb"], writes=["psT"])
                    P.op("act", lambda e, g=g, i=i: e.copy(out=selT[g][:, i * 512:(i + 1) * 512], in_=psT[:, 0:4, :].rearrange("p a b -> p (a b)")),
                         reads=["psT"], writes=[("selT", g, i)])
        P.stack = st
        P.barrier()
        allOC = [("OC", hq, i) for hq in range(8) for i in range(NG)]
        with ExitStack() as st4:
            P.stack = st4
            ac = AttnCore(P, idb, nO=4, skew=3)
            cm = P.sb("cm", [128, 4, 512], BF16)
            wm = P.sb("wm", [128, 8, 512], BF16)
            Ews = P.sb("Ews", [128, S], BF16)
            P.dma("sp", cm[:], cmask, writes=["cm"], slot="cm")
            P.dma("sp", wm[:], wmask, writes=["wm"], slot="wm")
            P.dma("sp", Ews[:], Ew, writes=["Ews"], slot="Ews")
            QTb = Rot(P, "QT", 2, [73, S], BF16)
            KsT = P.sb("KsT", [73, S], BF16)
            KwT = P.sb("KwT", [73, S], BF16)
            VsA = P.sb("VsA", [128, NT, 65], BF16)
            VwA = P.sb("VwA", [128, NT, 65], BF16)
            P.op("pool", lambda e: e.memset(VsA[:, :, 64:65], 1.0), writes=["VsA"])
            P.op("pool", lambda e: e.memset(VwA[:, :, 64:65], 1.0), writes=["VwA"])
            glt = Rot(P, "glt", 2, [128, 4, 24], F32)
            zt = Rot(P, "zt", 2, [128, 4, 64], F32)
            oct_ = Rot(P, "oc4", 2, [128, 4, 64], F32)
            acc = Rot(P, "acc", 2, [128, 4, 64], F32)
            yt = Rot(P, "yt", 2, [128, 4, 64], BF16)
            rv = Rot(P, "rv", 2, [128, 8], F32)
            for g in range(2):
                P.dma("sp", KsT[0:64, :], QKT[768 + g * 64:768 + (g + 1) * 64, :], reads=allQKT, writes=["KsT"], slot="KsT")
                P.dma("sp", KsT[64:73, :], AK, writes=["KsT"], slot="KsT")
                P.dma("sp", KwT[0:64, :], QKT[896 + g * 64:896 + (g + 1) * 64, :], reads=allQKT, writes=["KwT"], slot="KwT")
                P.dma("sp", KwT[64:73, :], AK, writes=["KwT"], slot="KwT")
                P.dma("sp", VsA[:, :, 0:64], VT[:, g * 64:(g + 1) * 64].rearrange("(n p) d -> p n d", p=128), reads=allVT, writes=["VsA"], slot="VsA")
                P.dma("sp", VwA[:, :, 0:64], VT[:, 128 + g * 64:128 + (g + 1) * 64].rearrange("(n p) d -> p n d", p=128), reads=allVT, writes=["VwA"], slot="VwA")
                selk = [("selT", g, i) for i in range(NG)]
                for hl in range(4):
                    hq = g * 4 + hl
                    QT, qk = QTb.next()
                    P.dma("sp", QT[0:64, :], QKT[hq * 64:(hq + 1) * 64, :], reads=allQKT, writes=[qk], slot=qk)
                    P.dma("sp", QT[64:73, :], AQ[hq], writes=[qk], slot=qk)
                    for i in range(NG):
                        tiles = causal_tiles(i, lambda m: (idb[:], cm[:, m, :], ["idb", "cm"]),
                                             extra_fn=lambda j, i=i, g=g: [(Ews[:, j * 128:(j + 1) * 128], selT[g][:, i * 512:(i + 1) * 512], ["Ews", ("selT", g, i)])])
                        psOs, oks = ac.group(QT, [qk], KsT, ["KsT"], 73, i * 512, tiles, lambda j: (VsA[:, j, :], ["VsA"]))
                        wt = []
                        for m in range(-4, 4):
                            j = 4 * i + m
                            if j < 0:
                                continue
                            wt.append(dict(j=j, extras=[(idb[:], wm[:, m + 4, :], ["idb", "wm"])], qs_min=max(m, 0), qs_max=min(4, m + 5)))
                        psOw, okw = ac.group(QT, [qk], KwT, ["KwT"], 73, i * 512, wt, lambda j: (VwA[:, j, :], ["VwA"]))
                        gl_, glk = glt.next()
                        z, zk = zt.next()
                        oc, ock = oct_.next()
                        a_, ak_ = acc.next()
                        y, yk = yt.next()
                        r, rk = rv.next()
                        P.dma("sp", gl_[:], GL[i * 512:(i + 1) * 512, :].rearrange("(q p) c -> p q c", p=128), reads=allGL, writes=[glk], slot=glk)
                        P.dma("sp", z[:], ZS[i * 512:(i + 1) * 512, hq * 64:(hq + 1) * 64].rearrange("(q p) d -> p q d", p=128), reads=allZS, writes=[zk], slot=zk)
                        P.dma("sp", oc[:], OC[i * 512:(i + 1) * 512, hq * 64:(hq + 1) * 64].rearrange("(q p) d -> p q d", p=128), reads=allOC, writes=[ock], slot=ock)
                        P.op("dve", lambda e, psOs=psOs, r=r: e.reciprocal(out=r[:, 0:4], in_=psOs[:, 64:260:65]), reads=[oks], writes=[rk])
                        P.op("dve", lambda e, psOw=psOw, r=r: e.reciprocal(out=r[:, 4:8], in_=psOw[:, 64:260:65]), reads=[okw, rk], writes=[rk])
                        P.op("dve", lambda e, r=r, gl_=gl_, hq=hq: e.tensor_tensor(out=r[:, 0:4], in0=r[:, 0:4], in1=gl_[:, :, hq * 3 + 1], op=ALU.mult), reads=[rk, glk], writes=[rk])
                        P.op("dve", lambda e, r=r, gl_=gl_, hq=hq: e.tensor_tensor(out=r[:, 4:8], in0=r[:, 4:8], in1=gl_[:, :, hq * 3 + 2], op=ALU.mult), reads=[rk, glk], writes=[rk])
                        for qs in range(4):
                            P.op("dve", lambda e, psOs=psOs, r=r, a_=a_, oc=oc, qs=qs: e.scalar_tensor_tensor(
                                out=a_[:, qs, :], in0=psOs[:, qs * 65:qs * 65 + 64], scalar=r[:, qs:qs + 1], in1=oc[:, qs, :], op0=ALU.mult, op1=ALU.add),
                                reads=[oks, rk, ock], writes=[sub(ak_, qs)])
                        for qs in range(4):
                            P.op("dve", lambda e, psOw=psOw, r=r, a_=a_, qs=qs: e.scalar_tensor_tensor(
                                out=a_[:, qs, :], in0=psOw[:, qs * 65:qs * 65 + 64], scalar=r[:, 4 + qs:5 + qs], in1=a_[:, qs, :], op0=ALU.mult, op1=ALU.add),
                                reads=[okw, rk, sub(ak_, qs)], writes=[sub(ak_, qs)])
                        P.op("pool", lambda e, a_=a_, z=z, y=y: e.tensor_tensor(out=y[:], in0=a_[:], in1=z[:], op=ALU.mult), reads=[ak_, zk], writes=[yk])
                        P.dma("pool", Y[i * 512:(i + 1) * 512, hq * 64:(hq + 1) * 64].rearrange("(q p) d -> p q d", p=128), y[:], reads=[yk], writes=[("Y", hq, i)], slot=yk)
        P.stack = st
        if cx is None:
            P.final_wait("sp", [("Y", hq, i) for hq in range(8) for i in range(NG)])
            P.emit()
    return nc


def _split3(v):
    v = np.asarray(v, np.float64)
    hi = v.astype(NPBF)
    r1 = v - hi.astype(np.float64)
    mid = r1.astype(NPBF)
    r2 = r1 - mid.astype(np.float64)
    lo = r2.astype(NPBF)
    return hi, mid, lo


def consts_C(S, hh):
    NCMP = (S - 32) // 16 + 1
    NCT = max(1, S // 2048)
    k = np.arange(128)[:, None]
    q = np.arange(512)[None, :]
    cm = np.stack([np.where(128 * m + k <= q, 0.0, NEGM) for m in range(4)], axis=1).astype(NPBF)
    wm = np.stack([np.where((128 * m + k <= q) & (128 * m + k > q - 512), 0.0, NEGM) for m in range(-4, 4)], axis=1).astype(NPBF)
    cmk = np.stack([np.where(16 * k + 31 <= 512 * dd + q, 0.0, NEGM) for dd in range(5)], axis=1).astype(NPBF)
    t = np.arange(S, dtype=np.float64)
    AQ = np.zeros((8, 9, S), NPBF)
    for hl in range(8):
        h = 8 * hh + hl
        slope = np.float64(np.float32(2.0 ** (-8.0 * (h + 1) / 16)))
        a, b, c = _split3(-slope * t)
        s1, s2, s3 = _split3(np.full(S, slope))
        AQ[hl] = np.stack([a, b, c, s1, s1, s2, s2, s3, s3])
    def krows(pos):
        pos = np.asarray(pos, np.float64)
        pa = (np.floor(pos / 128) * 128).astype(NPBF)
        pb = (pos % 128).astype(NPBF)
        one = np.ones(len(pos), NPBF)
        return np.stack([one, one, one, pa, pb, pa, pb, pa, pb])
    AK = krows(np.arange(S))
    AKc = krows(16 * np.arange(128 * NCT) + 31)
    ns = S // 64
    c0 = np.arange(128 * NCT)[:, None] * 16
    s0 = np.arange(128)[None, :] * 64
    ov = np.clip(np.minimum(c0 + 32, s0 + 64) - np.maximum(c0, s0), 0, None) / 16.0
    ov[NCMP:, :] = 0
    ov[:, ns:] = 0
    Mc = np.ascontiguousarray(ov.reshape(NCT, 128, 128).transpose(1, 0, 2)).astype(NPBF)
    tt = np.arange(S)[:, None]
    blk = np.arange(128)[None, :]
    cur = tt // 64
    valid = blk * 64 <= tt
    forced = (blk == 0) | (blk == cur) | (blk == cur - 1)
    ADD = np.where(valid, np.where(forced, 1e6, 0.0), -1e30).astype(np.float32)
    Ew = (np.arange(128)[:, None] == (np.arange(S)[None, :] // 64)).astype(NPBF)
    return dict(AQ=AQ, AK=AK, AKc=AKc, cmask=cm, wmask=wm, cmk=cmk, Mc=Mc, ADD=ADD, Ew=Ew, ident=np.eye(128, dtype=np.float32))


def inputs_C(x_b, norm_g, w_in, b_gate, pe_k, pe_v, wk1, wk2, wv1, wv2, hh, S):
    o_q, o_kc, o_vc, o_ks, o_vs, o_kw, o_vw, o_gl, o_z = 0, 1024, 1280, 1536, 1792, 2048, 2304, 2560, 2608
    g0 = 2 * hh
    gsl = lambda o: np.arange(o + g0 * 64, o + (g0 + 2) * 64)
    cols = np.concatenate([np.arange(o_q + hh * 512, o_q + (hh + 1) * 512), gsl(o_kc), gsl(o_vc), gsl(o_ks), gsl(o_kw),
                           gsl(o_vs), gsl(o_vw), np.arange(o_z + hh * 512, o_z + (hh + 1) * 512),
                           np.arange(o_gl + hh * 24, o_gl + (hh + 1) * 24)])
    d = dict(x=np.ascontiguousarray(x_b), gcol=np.ascontiguousarray(norm_g.reshape(8, 128).T), wC=np.ascontiguousarray(w_in[:, cols]),
             bg=np.ascontiguousarray(np.broadcast_to(b_gate[hh * 24:(hh + 1) * 24][None, :], (128, 24))),
             peT=np.ascontiguousarray(np.stack([pe_k.T, pe_v.T])), w1=np.ascontiguousarray(np.stack([wk1, wv1])), w2=np.ascontiguousarray(np.stack([wk2, wv2])))
    d.update(consts_C(S, hh))
    return d


def build_F(S):
    nc = bass.Bass("TRN2", target_bir_lowering=False)
    x = nc.dram_tensor("x", [S, 1024], F32, kind="ExternalInput").ap()
    ident = nc.dram_tensor("ident", [128, 128], F32, kind="ExternalInput").ap()
    out = nc.dram_tensor("out", [S, 1024], F32, kind="ExternalOutput").ap()
    Y1 = nc.dram_tensor("Y1", [S, 1024], BF16, kind="Internal").ap()
    X1 = nc.dram_tensor("X1", [S, 1024], F32, kind="Internal").ap()
    Y2 = nc.dram_tensor("Y2", [S, 1024], BF16, kind="Internal").ap()
    with ExitStack() as st:
        P = Prog(nc, st)
        idb = load_consts(P, ident)
        cx = Ctx(nc, P, idb)
        for hh in range(2):
            build_A(S, cx=cx, sfx="_a%d" % hh, io=dict(x=x, Yf=Y1[:, hh * 256:(hh + 1) * 256], Yr=Y1[:, 512 + hh * 256:512 + (hh + 1) * 256]))
        build_O(S, False, cx=cx, sfx="_b", io=dict(y=Y1, x=x, out=X1))
        for hh in range(2):
            build_C(S, cx=cx, sfx="_c%d" % hh, io=dict(x=X1, Y=Y2[:, hh * 512:(hh + 1) * 512]))
        build_O(S, True, cx=cx, sfx="_d", io=dict(y=Y2, x=X1, out=out))
        P.stack = st
        P.emit()
    return nc


def inputs_F(xb, p, S):
    f32 = lambda a: np.ascontiguousarray(np.asarray(a, dtype=np.float32))
    m = dict(x=np.ascontiguousarray(xb), ident=np.eye(128, dtype=np.float32))
    gfin = np.ascontiguousarray(np.broadcast_to(f32(p["final_g"])[None, :], (128, 1024)))
    for hh in range(2):
        d = inputs_A(xb, f32(p["even_norm_g"])[0], f32(p["even_w_in"])[0], f32(p["even_b_f"])[0], f32(p["even_gn_g"])[0], hh)
        for k, v in d.items():
            if k not in ("x", "ident"):
                m[k + "_a%d" % hh] = v
        d = inputs_C(xb, f32(p["odd_norm_g"])[0], f32(p["odd_w_in"])[0], f32(p["odd_b_gate"])[0], f32(p["odd_pe_k"])[0], f32(p["odd_pe_v"])[0],
                     f32(p["odd_wk1"])[0], f32(p["odd_wk2"])[0], f32(p["odd_wv1"])[0], f32(p["odd_wv2"])[0], hh, S)
        for k, v in d.items():
            if k not in ("x", "ident"):
                m[k + "_c%d" % hh] = v
    m["w_b"] = f32(p["even_w_out"])[0]
    m["gf_b"] = gfin
    m["w_d"] = f32(p["odd_w_out"])[0]
    m["gf_d"] = gfin
    return m


_NC_CACHE = {}


def _get_nc(name, fn):
    if name not in _NC_CACHE:
        _NC_CACHE[name] = fn()
    return _NC_CACHE[name]


def _run(nc, maps):
    return run_bass_kernel_spmd(nc, maps, core_ids=list(range(len(maps)))).results


def _assemble_y(res, S):
    shards = []
    for b in range(BATCH):
        y0 = np.asarray(res[2 * b]["Y"])
        y1 = np.asarray(res[2 * b + 1]["Y"])
        shards.append((y0, y1))
    return shards


def kernel(x, even_norm_g, even_w_in, even_b_f, even_gn_g, even_w_out,
           odd_norm_g, odd_w_in, odd_b_gate, odd_pe_k, odd_pe_v, odd_wk1, odd_wk2, odd_wv1, odd_wv2, odd_w_out, final_g):
    f32 = lambda a: np.ascontiguousarray(np.asarray(a, dtype=np.float32))
    x = f32(x)
    S = x.shape[1]
    p = dict(even_norm_g=even_norm_g, even_w_in=even_w_in, even_b_f=even_b_f, even_gn_g=even_gn_g, even_w_out=even_w_out,
             odd_norm_g=odd_norm_g, odd_w_in=odd_w_in, odd_b_gate=odd_b_gate, odd_pe_k=odd_pe_k, odd_pe_v=odd_pe_v,
             odd_wk1=odd_wk1, odd_wk2=odd_wk2, odd_wv1=odd_wv1, odd_wv2=odd_wv2, odd_w_out=odd_w_out, final_g=final_g)
    ncF = _get_nc(("F", S), lambda: build_F(S))
    per_b = [inputs_F(x[b], p, S) for b in range(BATCH)]
    maps = [per_b[c // 2] for c in range(8)]
    res = _run(ncF, maps)
    out = np.stack([np.asarray(res[2 * b]["out"]) for b in range(BATCH)])
    return out.astype(np.float32)
```

```python
import numpy as np
import ml_dtypes
from contextlib import ExitStack
import concourse.bass as bass
import concourse.mybir as mybir
from concourse.bass_utils import run_bass_kernel_spmd

F32 = mybir.dt.float32
BF16 = mybir.dt.bfloat16
AF = mybir.ActivationFunctionType
ALU = mybir.AluOpType
AX = mybir.AxisListType
NPBF = ml_dtypes.bfloat16

SEQ = 8192
BATCH = 4
DM = 1024
NEGM = -30000.0
FLAT = False
DBG = {}


class Sub(tuple):
    pass


def sub(base, idx):
    return Sub((base, idx))


class Prog:
    ENG = ("pe", "act", "dve", "pool", "sp")

    def __init__(self, nc, stack):
        self.nc = nc
        self.stack = stack
        self.top = stack
        self.streams = {e: [] for e in self.ENG}
        self.esem = {e: stack.enter_context(nc.semaphore("es_" + e)) for e in self.ENG}
        self.ecnt = {e: 0 for e in self.ENG}
        self.waited = {e: {} for e in self.ENG}
        self.lastw = {}
        self.readers = {}
        self.slots = {}
        self.nops = 0

    def _uniq(self, name):
        self._names = getattr(self, "_names", {})
        n = self._names.get(name, 0)
        self._names[name] = n + 1
        return name if n == 0 else "%s_v%d" % (name, n)

    def sb(self, name, shape, dt):
        return self.stack.enter_context(self.nc.sbuf_tensor(self._uniq(name), list(shape), dt))

    def ps(self, name, shape, dt):
        return self.stack.enter_context(self.nc.psum_tensor(self._uniq(name), list(shape), dt))

    def _deps(self, reads, writes):
        toks = []
        subs = self._subs = getattr(self, "_subs", {})

        def w_of(k):
            t = self.lastw.get(k)
            if t is not None:
                toks.append(t)

        def r_of(k):
            r = self.readers.get(k)
            if r:
                toks.extend(r.items())

        for k in reads:
            w_of(k)
            if isinstance(k, Sub):
                subs.setdefault(k[0], set()).add(k)
                w_of(k[0])
            else:
                for sk_ in subs.get(k, ()):
                    w_of(sk_)
        for k in writes:
            w_of(k)
            r_of(k)
            if isinstance(k, Sub):
                subs.setdefault(k[0], set()).add(k)
                w_of(k[0])
                r_of(k[0])
            else:
                for sk_ in subs.get(k, ()):
                    w_of(sk_)
                    r_of(sk_)
        return toks

    def _waits(self, eng, toks):
        need = {}
        for sk, v in toks:
            if eng == "pe" and sk == ("e", "pe"):
                continue
            if v > need.get(sk, 0):
                need[sk] = v
        out = []
        w = self.waited[eng]
        for sk, v in need.items():
            if w.get(sk, 0) >= v:
                continue
            w[sk] = v
            out.append((sk, v))
        return out

    def _commit(self, tok, reads, writes):
        sk, v = tok
        for k in writes:
            self.lastw[k] = tok
            self.readers[k] = {}
        for k in reads:
            d = self.readers.setdefault(k, {})
            if v > d.get(sk, 0):
                d[sk] = v

    LIMIT = None

    def op(self, eng, fn, reads=(), writes=()):
        if self.LIMIT is not None and self.nops >= self.LIMIT:
            return
        toks = self._deps(reads, writes)
        waits = self._waits(eng, toks)
        self.ecnt[eng] += 1
        sk = ("e", eng)
        tok = (sk, self.ecnt[eng])
        self.streams[eng].append((waits, fn, sk, 1))
        self._commit(tok, reads, writes)
        self.nops += 1

    QMAP = {}
    slot_prefix = ""

    def _slot(self, eng, slot):
        self._slotmap = getattr(self, "_slotmap", {})
        self._nexti = getattr(self, "_nexti", {})
        key = (eng, slot)
        if key not in self._slotmap:
            i = self._nexti.get(eng, 0)
            self._nexti[eng] = i + 1
            pk = "%s%d" % (eng, i)
            if pk not in self.slots:
                self.slots[pk] = [self.top.enter_context(self.nc.semaphore("ds_" + pk)), 0]
            self._slotmap[key] = pk
        return self._slotmap[key]

    def dma(self, eng, out, in_, reads=(), writes=(), slot=None, **kw):
        assert slot is not None
        if self.LIMIT is not None and self.nops >= self.LIMIT:
            return
        eng = self.QMAP.get(eng, eng)
        slot = self._slot(eng, slot)
        toks = self._deps(reads, writes)
        waits = self._waits(eng, toks)
        s = self.slots[slot]
        s[1] += 16
        sk = ("s", slot)
        tok = (sk, s[1])
        self.streams[eng].append((waits, (lambda e, o=out, i=in_, kw=kw: e.dma_start(out=o, in_=i, **kw)), sk, 16))
        self._commit(tok, reads, writes)
        self.nops += 1

    def coll(self, eng, kind, in_ap, out_ap, groups, reads=(), writes=(), slot=None):
        slot = self._slot(eng, slot)
        toks = self._deps(reads, writes)
        waits = self._waits(eng, toks)
        s = self.slots[slot]
        s[1] += 16
        sk = ("s", slot)
        self.streams[eng].append((waits, (lambda e, i=in_ap, o=out_ap: e.collective_compute(kind, ALU.bypass, replica_groups=groups, ins=[i], outs=[o])), sk, 16))
        self._commit((sk, s[1]), reads, writes)
        self.nops += 1

    def _sem(self, sk):
        return self.esem[sk[1]] if sk[0] == "e" else self.slots[sk[1]][0]

    def barrier(self):
        toks = [(("e", e), self.ecnt[e]) for e in self.ENG if self.ecnt[e] > 0]
        toks += [(("s", s), v[1]) for s, v in self.slots.items() if v[1] > 0]
        for e in self.ENG:
            waits = self._waits(e, toks)
            self.streams[e].append((waits, None, None, 0))
        self._slotmap, self._nexti = {}, {}

    def final_wait(self, eng, keys):
        toks = [self.lastw[k] for k in keys if k in self.lastw]
        waits = self._waits(eng, toks)
        self.streams[eng].append((waits, None, None, 0))

    def emit(self):
        nc = self.nc
        awaited = {e: set() for e in self.ENG}
        for name in self.ENG:
            for waits, fn, sk, inc in self.streams[name]:
                for wsk, v in waits:
                    if wsk[0] == "e":
                        awaited[wsk[1]].add(v)
        rank = {e: {v: i + 1 for i, v in enumerate(sorted(awaited[e]))} for e in self.ENG}
        with nc.Block() as block:
            def mk(name):
                def body(e):
                    n = 0
                    for waits, fn, sk, inc in self.streams[name]:
                        for wsk, v in waits:
                            if wsk[0] == "e":
                                e.wait_ge(self.esem[wsk[1]], rank[wsk[1]][v])
                            else:
                                e.wait_ge(self.slots[wsk[1]][0], v)
                        if fn is not None:
                            ins = fn(e)
                            if sk[0] == "e":
                                n += 1
                                if n in rank[name]:
                                    ins.then_inc(self.esem[name], 1)
                            else:
                                ins.then_inc(self.slots[sk[1]][0], inc)
                return body
            block.tensor(mk("pe"))
            block.scalar(mk("act"))
            block.vector(mk("dve"))
            block.gpsimd(mk("pool"))
            block.sync(mk("sp"))


class Rot:
    def __init__(self, P, name, n, shape, dt, psum=False):
        self.items = []
        for i in range(n):
            if name == DBG.get("padname") and i == 1:
                P.sb(name + "_pad", [128, DBG.get("pad", 512)], F32)
            t = P.ps(f"{name}{i}", shape, dt) if psum else P.sb(f"{name}{i}", shape, dt)
            self.items.append((t, f"{name}{i}"))
        self.i = 0

    def next(self):
        it = self.items[self.i % len(self.items)]
        self.i += 1
        return it


class Ctx:
    def __init__(self, nc, P, idb):
        self.nc, self.P, self.idb = nc, P, idb
        self.scratch = {}


def _din(nc, cx, sfx, name, shape, dt=F32):
    return nc.dram_tensor(name + sfx, list(shape), dt, kind="ExternalInput").ap()


def _dsc(nc, cx, name, shape, dt):
    if cx is None:
        return nc.dram_tensor(name, list(shape), dt, kind="Internal").ap()
    if name not in cx.scratch:
        cx.scratch[name] = nc.dram_tensor(name, list(shape), dt, kind="Internal").ap()
    return cx.scratch[name]

def load_consts(P, ident_ap):
    idf = P.sb("idf", [128, 128], F32)
    idb = P.sb("idb", [128, 128], BF16)
    P.dma("sp", idf[:], ident_ap, writes=["idf"], slot="idf")
    P.op("dve", lambda e: e.tensor_copy(out=idb[:], in_=idf[:]), reads=["idf"], writes=["idb"])
    return idb


def load_weight_bf16(P, wb, w_ap, ncols, name, chunk=256):
    stg = Rot(P, name + "_stg", 2, [128, 8, chunk], F32)
    wv = w_ap.rearrange("(c p) n -> p c n", p=128)
    for i, c0 in enumerate(range(0, ncols, chunk)):
        cw = min(chunk, ncols - c0)
        t, k = stg.next()
        P.dma("sp" if i % 2 == 0 else "pool", t[:, :, 0:cw], wv[:, :, c0:c0 + cw], writes=[k], slot=k)
        eng = "dve" if i % 2 == 0 else "pool"
        P.op(eng, lambda e, t=t, c0=c0, cw=cw: e.tensor_copy(out=wb[:, :, c0:c0 + cw], in_=t[:, :, 0:cw]),
             reads=[k], writes=[(name, c0)])
    return [(name, c0) for c0 in range(0, ncols, chunk)]


class NormT:
    def __init__(self, P, idb, gcol_ap):
        self.P = P
        self.idb = idb
        self.xt = Rot(P, "nx", 4, [128, 1024], F32)
        self.sq = P.sb("nsq", [128, 1024], F32)
        self.st = Rot(P, "nst", 4, [128, 4], F32)
        self.hb = Rot(P, "nhb", 4, [128, 1024], BF16)
        self.pT = Rot(P, "npT", 4, [128, 8, 128], BF16, psum=True)
        self.gs = P.sb("ngs", [128, 8], F32)
        self.gB = P.sb("ngB", [128, 8, 128], F32)
        self.mh = P.sb("nmh", [128, 1], F32)
        P.dma("sp", self.gs[:], gcol_ap, writes=["ngs"], slot="ngs")
        P.op("pool", lambda e: e.memset(self.gB[:], 1.0), writes=["ngB"])
        P.op("pool", lambda e: e.memset(self.mh[:], -0.5), writes=["nmh"])
        for c in range(8):
            P.op("dve", lambda e, c=c: e.tensor_scalar(out=self.gB[:, c, :], in0=self.gB[:, c, :], scalar1=self.gs[:, c:c + 1],
                                                       scalar2=None, op0=ALU.mult), reads=["ngs", "ngB"], writes=["ngB"])

    def load(self, x_rows_ap, xkey_reads=()):
        P = self.P
        xt, xk = self.xt.next()
        P.dma("sp", xt[:], x_rows_ap, reads=list(xkey_reads), writes=[xk], slot=xk)
        return xt, xk

    def run4(self, xs, hT, hk):
        P = self.P
        bufs = [(self.st.next(), self.hb.next(), self.pT.next()) for _ in range(4)]
        for tt in range(4):
            (xt, xk), ((st, sk), (hb, hbk), _) = xs[tt], bufs[tt]
            P.op("dve", lambda e, xt=xt, st=st, hb=hb: e.scalar_tensor_tensor(out=hb[:], in0=xt[:], scalar=1.0, in1=xt[:], op0=ALU.mult, op1=ALU.mult,
                                                                             accum_out=st[:, 0:1]), reads=[xk], writes=[hbk, sk])
        for tt in range(4):
            (st, sk) = bufs[tt][0]
            P.op("dve", lambda e, st=st: e.tensor_scalar(out=st[:, 1:2], in0=st[:, 0:1], scalar1=1.0 / 1024, scalar2=1e-6, op0=ALU.mult, op1=ALU.add),
                 reads=[sk], writes=[sk])
        for tt in range(4):
            (st, sk) = bufs[tt][0]
            P.op("pool", lambda e, st=st: e.tensor_tensor(out=st[:, 2:3], in0=st[:, 1:2], in1=self.mh[:], op=ALU.pow), reads=[sk, "nmh"], writes=[sk])
        for tt in range(4):
            (xt, xk), ((st, sk), (hb, hbk), _) = xs[tt], bufs[tt]
            P.op("dve", lambda e, xt=xt, st=st, hb=hb: e.tensor_scalar(out=hb[:], in0=xt[:], scalar1=st[:, 2:3], scalar2=None, op0=ALU.mult),
                 reads=[xk, sk], writes=[hbk])
        for tt in range(4):
            (hb, hbk), (pT, pk) = bufs[tt][1], bufs[tt][2]
            for c in range(8):
                P.op("pe", lambda e, c=c, hb=hb, pT=pT: e.transpose(out=pT[:, c, :], in_=hb[:, c * 128:(c + 1) * 128], identity=self.idb[:]),
                     reads=[hbk, "idb"], writes=[pk])
        for tt in range(4):
            (pT, pk) = bufs[tt][2]
            P.op("dve", lambda e, pT=pT, tt=tt: e.tensor_tensor(out=hT[:, :, tt * 128:(tt + 1) * 128], in0=pT[:], in1=self.gB[:], op=ALU.mult),
                 reads=[pk, "ngB"], writes=[sub(hk, tt)])

    def run(self, xt, xk, hT_out_ap, hT_key):
        P = self.P
        st, sk = self.st.next()
        hb, hk = self.hb.next()
        pT, pk = self.pT.next()
        sq = self.sq
        P.op("dve", lambda e: e.scalar_tensor_tensor(out=sq[:], in0=xt[:], scalar=1.0, in1=xt[:], op0=ALU.mult, op1=ALU.mult,
                                                     accum_out=st[:, 0:1]), reads=[xk], writes=["nsq", sk])
        P.op("dve", lambda e: e.tensor_scalar(out=st[:, 1:2], in0=st[:, 0:1], scalar1=1.0 / 1024, scalar2=1e-6, op0=ALU.mult,
                                              op1=ALU.add), reads=[sk], writes=[sk])
        P.op("pool", lambda e: e.tensor_tensor(out=st[:, 2:3], in0=st[:, 1:2], in1=self.mh[:], op=ALU.pow),
             reads=[sk, "nmh"], writes=[sk])
        P.op("dve", lambda e: e.tensor_scalar(out=hb[:], in0=xt[:], scalar1=st[:, 2:3], scalar2=None, op0=ALU.mult),
             reads=[xk, sk], writes=[hk])
        for c in range(8):
            P.op("pe", lambda e, c=c: e.transpose(out=pT[:, c, :], in_=hb[:, c * 128:(c + 1) * 128], identity=self.idb[:]),
                 reads=[hk, "idb"], writes=[pk])
        P.op("dve", lambda e: e.tensor_tensor(out=hT_out_ap, in0=pT[:], in1=self.gB[:], op=ALU.mult),
             reads=[pk, "ngB"], writes=[hT_key])


class AttnCore:
    def __init__(self, P, idb, nO=2, skew=2):
        self.P = P
        self.idb = idb
        self.skew = skew
        self.psS = Rot(P, "aS", skew + 1, [128, 512], F32, psum=True)
        self.psO = Rot(P, "aO", nO, [128, 512], F32, psum=True)
        self.pT = Rot(P, "aP", skew + 2, [128, 512], BF16)

    def group(self, QT, qkeys, KT, kkeys, kd, q0, tiles, v_of, pv_extra=None):
        P = self.P
        SKEW = self.skew
        psO, ok = self.psO.next()
        last_for_qs = {}
        for ti, t in enumerate(tiles):
            for qs in range(t["qs_min"], t.get("qs_max", 4)):
                last_for_qs[qs] = ti
        n = len(tiles)
        pend = {}

        def emit_scores(ti):
            t = tiles[ti]
            j = t["j"]
            psS, sk = self.psS.next()
            pT, pk = self.pT.next()
            ex = t["extras"]
            c0, c1 = 128 * t["qs_min"], 128 * t.get("qs_max", 4)
            P.op("pe", lambda e, psS=psS, j=j, nx=len(ex), c0=c0, c1=c1: e.matmul(psS[:, c0:c1], lhsT=KT[0:kd, j * 128:(j + 1) * 128], rhs=QT[0:kd, q0 + c0:q0 + c1],
                                                                                start=True, stop=(nx == 0)),
                 reads=list(qkeys) + list(kkeys), writes=[sk])
            for xi, (la, ra, xk) in enumerate(ex):
                P.op("pe", lambda e, psS=psS, la=la, ra=ra, last=(xi == len(ex) - 1), c0=c0, c1=c1: e.matmul(psS[:, c0:c1], lhsT=la, rhs=ra[:, c0:c1], start=False, stop=last),
                     reads=list(xk), writes=[sk])
            P.op("act", lambda e, psS=psS, pT=pT, c0=c0, c1=c1: e.activation(out=pT[:, c0:c1], in_=psS[:, c0:c1], func=AF.Exp), reads=[sk], writes=[pk])
            pend[ti] = (pT, pk)

        def emit_pv(ti):
            t = tiles[ti]
            pT, pk = pend.pop(ti)
            va, vk = v_of(t["j"])
            for qs in range(t["qs_min"], t.get("qs_max", 4)):
                P.op("pe", lambda e, psO=psO, pT=pT, va=va, qs=qs, st=(ti == 0 and qs == tiles[0]["qs_min"]), sp=(last_for_qs[qs] == ti):
                     e.matmul(psO[:, qs * 65:(qs + 1) * 65], lhsT=pT[:, qs * 128:(qs + 1) * 128], rhs=va, start=st, stop=sp, skip_group_check=True),
                     reads=[pk] + list(vk), writes=[ok])
            if pv_extra is not None:
                pv_extra(ti, t, pT, pk)

        for step in range(n + SKEW):
            if step < n:
                emit_scores(step)
            if step - SKEW >= 0:
                emit_pv(step - SKEW)
        return psO, ok


def causal_tiles(i, diag_masks, extra_fn=None):
    tiles = []
    for j in range(4 * i + 4):
        ex = []
        if extra_fn is not None:
            ex.extend(extra_fn(j))
        m = j - 4 * i
        if m >= 0:
            ex.append(diag_masks(m))
        tiles.append(dict(j=j, extras=ex, qs_min=max(m, 0)))
    return tiles


def build_A(S, phases=(1, 2, 3), cx=None, sfx="", io=None):
    NT, NG = S // 128, S // 512
    nc = cx.nc if cx else bass.Bass("TRN2", target_bir_lowering=False)
    din = lambda n, sh, dt=F32: _din(nc, cx, sfx, n, sh, dt)
    x = io["x"] if io else din("x", [S, 1024])
    gcol = din("gcol", [128, 8])
    wA = din("wA", [1024, 2308])
    bfv = din("bf", [4, 1])
    gng = din("gng", [128, 256])
    ident = None if cx else din("ident", [128, 128])
    cmask = din("cmask", [128, 4, 512], BF16)
    rinner = din("rinner", [128, 4, 128])
    rcross = din("rcross", [64, 4, 128])
    rkdec = din("rkdec", [128, 4])
    rcdec = din("rcdec", [64, 4])
    if io:
        Yf, Yr = io["Yf"], io["Yr"]
    else:
        Y = nc.dram_tensor("Y", [S, 512], BF16, kind="ExternalOutput").ap()
        Yf, Yr = Y[:, 0:256], Y[:, 256:512]
    QKT = _dsc(nc, cx, "QKT", [1024, S], BF16)
    CQ = _dsc(nc, cx, "CQ", [4, 6, S], BF16)
    VT = _dsc(nc, cx, "VT", [S, 768], BF16)
    ZS = _dsc(nc, cx, "ZS", [S, 512], F32)

    with ExitStack() as st:
        P = cx.P if cx else Prog(nc, st)
        idb = cx.idb if cx else load_consts(P, ident)
        if cx:
            P.stack = st
            P.slot_prefix = sfx[:2]
            P.barrier()
        with ExitStack() as st1:
            P.stack = st if FLAT else st1
            wb = P.sb("wb", [128, 8, 2308], BF16)
            wkeys = load_weight_bf16(P, wb, wA, 2308, "wb")
            nt = NormT(P, idb, gcol)
            hT = Rot(P, "hT", 2, [128, 8, 512], BF16)
            psA = Rot(P, "psA", 3, [128, 512], F32, psum=True)
            fmst = Rot(P, "fmst", 3, [128, 512], BF16)
            tmst = Rot(P, "tmst", 2, [128, 768], BF16)
            zst = Rot(P, "zst", DBG.get("zst", 2), [128, 512], F32)
            bfs = P.sb("bfs", [4, 1], F32)
            ones4 = P.sb("ones4", [4, 512], F32)
            carry = P.sb("carry", [4, 1], F32)
            P.dma("sp", bfs[:], bfv, writes=["bfs"], slot="bfs")
            P.op("pool", lambda e: e.memset(ones4[:], 1.0), writes=["ones4"])
            P.op("pool", lambda e: e.memset(carry[:], 0.0), writes=["carry"])
            ft = {n: P.sb("ft_" + n, [4, 512], F32) for n in ("sg", "ls", "c", "r1", "r2")}
            fb = {n: P.sb("fb_" + n, [4, 512], BF16) for n in ("hi", "mid", "lo", "nhi", "nmid", "nlo")}
            def prep(g):
                h, hk = hT.next()
                xs = [nt.load(x[g * 512 + tt * 128:g * 512 + (tt + 1) * 128, :]) for tt in range(4)]
                nt.run4(xs, h, hk)
                return h, hk

            nxt = prep(0)
            for g in range(NG):
                h, hk = nxt
                if g + 1 < NG:
                    nxt = prep(g + 1)
                for ct in range(8):
                    ps, pk = psA.next()
                    for c in range(8):
                        P.op("pe", lambda e, ps=ps, c=c, ct=ct, h=h: e.matmul(ps[:], lhsT=wb[:, c, ct * 128:(ct + 1) * 128], rhs=h[:, c, :],
                                                                             start=(c == 0), stop=(c == 7)),
                             reads=[hk] + wkeys, writes=[pk])
                    sg, sgk = fmst.next()
                    scl = 0.125 if ct in (0, 1, 4, 5) else 1.0
                    if ct % 2 == 0:
                        P.op("act", lambda e, ps=ps, sg=sg, scl=scl: e.activation(out=sg[:], in_=ps[:], func=AF.Copy, scale=scl),
                             reads=[pk], writes=[sgk])
                    else:
                        P.op("dve", lambda e, ps=ps, sg=sg, scl=scl: e.tensor_scalar(out=sg[:], in0=ps[:], scalar1=scl, scalar2=None, op0=ALU.mult),
                             reads=[pk], writes=[sgk])
                    P.dma("pool", QKT[ct * 128:(ct + 1) * 128, g * 512:(g + 1) * 512], sg[:], reads=[sgk], writes=[("QKT", ct, g)], slot=sgk)
                ps, pk = psA.next()
                for c in range(8):
                    P.op("pe", lambda e, ps=ps, c=c, h=h: e.matmul(ps[0:4, :], lhsT=wb[:, c, 1024:1028], rhs=h[:, c, :], start=(c == 0), stop=(c == 7)),
                         reads=[hk] + wkeys, writes=[pk])
                P.op("act", lambda e, ps=ps: e.activation(out=ft["sg"][:], in_=ps[0:4, :], func=AF.Sigmoid, bias=bfs[:, 0:1]),
                     reads=[pk, "bfs"], writes=["ft_sg"])
                P.op("act", lambda e: e.activation(out=ft["ls"][:], in_=ft["sg"][:], func=AF.Ln), reads=["ft_sg"], writes=["ft_ls"])
                P.op("dve", lambda e: e.tensor_tensor_scan(out=ft["c"][:], data0=ones4[:], data1=ft["ls"][:], initial=carry[:, 0:1],
                                                           op0=ALU.mult, op1=ALU.add), reads=["ft_ls", "ones4", "carry"], writes=["ft_c"])
                P.op("dve", lambda e: e.tensor_copy(out=carry[:], in_=ft["c"][:, 511:512]), reads=["ft_c"], writes=["carry"])
                P.op("dve", lambda e: e.tensor_copy(out=fb["hi"][:], in_=ft["c"][:]), reads=["ft_c"], writes=["fb_hi"])
                P.op("dve", lambda e: e.tensor_tensor(out=ft["r1"][:], in0=ft["c"][:], in1=fb["hi"][:], op=ALU.subtract), reads=["ft_c", "fb_hi"], writes=["ft_r1"])
                P.op("dve", lambda e: e.tensor_copy(out=fb["mid"][:], in_=ft["r1"][:]), reads=["ft_r1"], writes=["fb_mid"])
                P.op("dve", lambda e: e.tensor_tensor(out=ft["r2"][:], in0=ft["r1"][:], in1=fb["mid"][:], op=ALU.subtract), reads=["ft_r1", "fb_mid"], writes=["ft_r2"])
                P.op("dve", lambda e: e.tensor_copy(out=fb["lo"][:], in_=ft["r2"][:]), reads=["ft_r2"], writes=["fb_lo"])
                for a, b_ in (("hi", "nhi"), ("mid", "nmid"), ("lo", "nlo")):
                    P.op("dve", lambda e, a=a, b_=b_: e.tensor_scalar(out=fb[b_][:], in0=fb[a][:], scalar1=-1.0, scalar2=None, op0=ALU.mult),
                         reads=["fb_" + a], writes=["fb_" + b_])
                for r, n in enumerate(("hi", "mid", "lo", "nhi", "nmid", "nlo")):
                    P.dma("pool", CQ[:, r, g * 512:(g + 1) * 512], fb[n][:], reads=["fb_" + n], writes=[("CQ", g)],
                          slot="fb_" + n)
                for tt in range(4):
                    r0 = g * 512 + tt * 128
                    tm, tmk = tmst.next()
                    zs, zk = zst.next()
                    for ci, (c0, cw) in enumerate(((0, 512), (512, 512), (1024, 256))):
                        ps, pk = psA.next()
                        for c in range(8):
                            P.op("pe", lambda e, ps=ps, c=c, h=h, tt=tt, c0=c0, cw=cw: e.matmul(ps[:, 0:cw], lhsT=h[:, c, tt * 128:(tt + 1) * 128],
                                                                                            rhs=wb[:, c, 1028 + c0:1028 + c0 + cw], start=(c == 0), stop=(c == 7)),
                                 reads=[hk] + wkeys, writes=[pk])
                        if ci == 0:
                            P.op("dve", lambda e, ps=ps, tm=tm: e.tensor_copy(out=tm[:, 0:512], in_=ps[:]), reads=[pk], writes=[tmk])
                        elif ci == 1:
                            P.op("dve", lambda e, ps=ps, tm=tm: e.tensor_copy(out=tm[:, 512:768], in_=ps[:, 0:256]), reads=[pk], writes=[tmk])
                            P.op("act", lambda e, ps=ps, zs=zs: e.activation(out=zs[:, 0:256], in_=ps[:, 256:512], func=AF.Silu), reads=[pk], writes=[zk, pk])
                        else:
                            P.op("act", lambda e, ps=ps, zs=zs: e.activation(out=zs[:, 256:512], in_=ps[:, 0:256], func=AF.Silu), reads=[pk], writes=[zk])
                    P.dma("pool", VT[r0:r0 + 128, :], tm[:], reads=[tmk], writes=[("VT", g)], slot=tmk)
                    P.dma("pool", ZS[r0:r0 + 128, :], zs[:], reads=[zk], writes=[("ZS", g)], slot=zk)
        P.stack = st
        P.barrier()
        allQKT = [("QKT", ct, g) for ct in range(8) for g in range(NG)]
        allCQ = [("CQ", g) for g in range(NG)]
        allVT = [("VT", g) for g in range(NG)]
        allZS = [("ZS", g) for g in range(NG)]
        with ExitStack() as st2:
          if 2 in phases:
              P.stack = st if FLAT else st2
              ac = AttnCore(P, idb, nO=2, skew=4)
              cm = P.sb("cm", [128, 4, 512], BF16)
              P.dma("sp", cm[:], cmask, writes=["cm"], slot="cm")
              QTb = Rot(P, "QT", 2, [70, S], BF16)
              KTb = Rot(P, "KT", 2, [70, S], BF16)
              Vb = Rot(P, "Va", 2, [128, NT, 65], BF16)
              for (t, k) in QTb.items + KTb.items:
                  P.op("pool", lambda e, t=t: e.memset(t[64:70, :], 1.0), writes=[k])
              for (t, k) in Vb.items:
                  P.op("pool", lambda e, t=t: e.memset(t[:, :, 64:65], 1.0), writes=[k])
              zt = Rot(P, "zt", 2, [128, 4, 64], F32)
              yt = Rot(P, "yt", 2, [128, 4, 64], BF16)
              rv = Rot(P, "rv", 2, [128, 4], F32)
              for hh in range(4):
                  QT, qk = QTb.next()
                  KT, kk = KTb.next()
                  Va, vk = Vb.next()
                  P.dma("sp", QT[0:64, :], QKT[hh * 64:(hh + 1) * 64, :], reads=allQKT, writes=[qk], slot=qk)
                  P.dma("sp", QT[64:67, :], CQ[hh, 0:3, :], reads=allCQ, writes=[qk], slot=qk)
                  P.dma("sp", KT[0:64, :], QKT[256 + hh * 64:256 + (hh + 1) * 64, :], reads=allQKT, writes=[kk], slot=kk)
                  P.dma("sp", KT[67:70, :], CQ[hh, 3:6, :], reads=allCQ, writes=[kk], slot=kk)
                  P.dma("sp", Va[:, :, 0:64], VT[:, hh * 64:(hh + 1) * 64].rearrange("(n p) d -> p n d", p=128), reads=allVT, writes=[vk], slot=vk)
                  for i in range(NG):
                      tiles = causal_tiles(i, lambda m: (idb[:], cm[:, m, :], ["idb", "cm"]))
                      psO, ok = ac.group(QT, [qk], KT, [kk], 70, i * 512, tiles, lambda j: (Va[:, j, :], [vk]))
                      z, zk = zt.next()
                      y, yk = yt.next()
                      r, rk = rv.next()
                      P.dma("sp", z[:], ZS[i * 512:(i + 1) * 512, hh * 64:(hh + 1) * 64].rearrange("(q p) d -> p q d", p=128),
                            reads=allZS, writes=[zk], slot=zk)
                      P.op("dve", lambda e, psO=psO, r=r: e.reciprocal(out=r[:], in_=psO[:, 64:260:65]), reads=[ok], writes=[rk])
                      for qs in range(4):
                          P.op("dve", lambda e, psO=psO, r=r, y=y, z=z, qs=qs: e.scalar_tensor_tensor(
                              out=y[:, qs, :], in0=psO[:, qs * 65:qs * 65 + 64], scalar=r[:, qs:qs + 1], in1=z[:, qs, :], op0=ALU.mult, op1=ALU.mult),
                              reads=[ok, rk, zk], writes=[sub(yk, qs)])
                      P.dma("pool", Yf[i * 512:(i + 1) * 512, hh * 64:(hh + 1) * 64].rearrange("(q p) d -> p q d", p=128), y[:],
                            reads=[yk], writes=[("Y", hh, i)], slot=yk)
        P.stack = st
        P.barrier()
        with ExitStack() as st3:
          if 3 in phases:
              P.stack = st if FLAT else st3
              Q4r = Rot(P, "Q4", 2, [64, 4, 512], BF16)
              K4r = Rot(P, "K4", 2, [64, 4, 512], BF16)
              QC4r = Rot(P, "QC4", 2, [64, 4, 512], BF16)
              VR = P.sb("VR", [128, NT, 256], BF16)
              KD = P.sb("KD", [128, NT, 256], BF16)
              inn = P.sb("inn", [128, 4, 128], F32)
              crs = P.sb("crs", [64, 4, 128], F32)
              kdc = P.sb("kdc", [128, 4], F32)
              cdc = P.sb("cdc", [64, 4], F32)
              gg = P.sb("gg", [128, 256], F32)
              Sf = P.sb("Sf", [64, 4, 64], F32)
              Sb = Rot(P, "Sb", 2, [64, 4, 64], BF16)
              P.dma("sp", VR[:], VT[:, 256:512].rearrange("(n p) d -> p n d", p=128), reads=allVT, writes=["VR"], slot="VR")
              P.dma("sp", KD[:], VT[:, 512:768].rearrange("(n p) d -> p n d", p=128), reads=allVT, writes=["KD"], slot="KD")
              for t, a, k in ((inn, rinner, "inn"), (crs, rcross, "crs"), (kdc, rkdec, "kdc"), (cdc, rcdec, "cdc"), (gg, gng, "gg")):
                  P.dma("sp", t[:], a, writes=[k], slot=k)
              for hh in range(4):
                  P.op("dve", lambda e, hh=hh: e.tensor_scalar(out=KD[:, :, hh * 64:(hh + 1) * 64], in0=KD[:, :, hh * 64:(hh + 1) * 64],
                                                               scalar1=kdc[:, hh:hh + 1], scalar2=None, op0=ALU.mult), reads=["KD", "kdc"], writes=["KD"])
              P.op("pool", lambda e: e.memset(Sf[:], 0.0), writes=["Sf"])
              psR = Rot(P, "psR", 2, [128, 4, 128], F32, psum=True)
              psU = Rot(P, "psU", 2, [128, 4, 128], F32, psum=True)
              psOr = Rot(P, "psOr", 2, [128, 8, 64], F32, psum=True)
              aTb = Rot(P, "aTb", 2, [128, 4, 128], BF16)
              osb = Rot(P, "osb", 2, [128, 4, 64], F32)
              sqb = P.sb("sqb", [128, 4, 64], F32)
              stt = Rot(P, "stt", 2, [128, 16], F32)
              zt2 = Rot(P, "zt2", 2, [128, 256], F32)
              yt2 = Rot(P, "yt2", 2, [128, 256], BF16)
              mh = P.sb("mh2", [128, 4], F32)
              P.op("pool", lambda e: e.memset(mh[:], -0.5), writes=["mh2"])
              for n in range(NT):
                  g, c4 = n // 4, n % 4
                  cs = slice(c4 * 128, (c4 + 1) * 128)
                  if c4 == 0:
                      Q4, q4k = Q4r.next()
                      K4, k4k = K4r.next()
                      QC4, qc4k = QC4r.next()
                      P.dma("sp", Q4[:], QKT[512:768, g * 512:(g + 1) * 512].rearrange("(h d) t -> d h t", d=64), reads=allQKT, writes=[q4k], slot=q4k)
                      P.dma("sp", K4[:], QKT[768:1024, g * 512:(g + 1) * 512].rearrange("(h d) t -> d h t", d=64), reads=allQKT, writes=[k4k], slot=k4k)
                      for cc in range(4):
                          P.op("pool", lambda e, cc=cc, Q4=Q4, QC4=QC4: e.tensor_tensor(out=QC4[:, :, cc * 128:(cc + 1) * 128], in0=Q4[:, :, cc * 128:(cc + 1) * 128],
                                                                                        in1=crs[:], op=ALU.mult), reads=[q4k, "crs"], writes=[qc4k])
                  sb_, sbk = Sb.next()
                  P.op("act", lambda e, sb_=sb_: e.copy(out=sb_[:], in_=Sf[:]), reads=["Sf"], writes=[sbk])
                  pr, prk = psR.next()
                  for hh in range(4):
                      P.op("pe", lambda e, pr=pr, hh=hh, cs=cs, K4=K4, Q4=Q4: e.matmul(pr[:, hh, :], lhsT=K4[:, hh, cs], rhs=Q4[:, hh, cs], start=True, stop=True),
                           reads=[k4k, q4k], writes=[prk])
                  at, atk = aTb.next()
                  P.op("dve", lambda e, pr=pr, at=at: e.tensor_tensor(out=at[:], in0=pr[:], in1=inn[:], op=ALU.mult), reads=[prk, "inn"], writes=[atk])
                  po, pok = psOr.next()
                  for hh in range(4):
                      P.op("pe", lambda e, po=po, at=at, hh=hh, n=n: e.matmul(po[:, hh, :], lhsT=at[:, hh, :], rhs=VR[:, n, hh * 64:(hh + 1) * 64], start=True, stop=False),
                           reads=[atk, "VR"], writes=[pok])
                      P.op("pe", lambda e, po=po, sb_=sb_, hh=hh, cs=cs, QC4=QC4: e.matmul(po[:, hh, :], lhsT=QC4[:, hh, cs], rhs=sb_[:, hh, :], start=False, stop=True),
                           reads=[qc4k, sbk], writes=[pok])
                  pu, puk = psU.next()
                  for hh in range(4):
                      P.op("pe", lambda e, pu=pu, hh=hh, n=n: e.matmul(pu[0:64, hh, 0:64], lhsT=KD[:, n, hh * 64:(hh + 1) * 64], rhs=VR[:, n, hh * 64:(hh + 1) * 64], start=True, stop=True),
                           reads=["KD", "VR"], writes=[puk])
                  for hh in range(4):
                      P.op("dve", lambda e, pu=pu, hh=hh: e.scalar_tensor_tensor(out=Sf[:, hh, :], in0=Sf[:, hh, :], scalar=cdc[:, hh:hh + 1],
                                                                                in1=pu[0:64, hh, 0:64], op0=ALU.mult, op1=ALU.add),
                           reads=["Sf", puk, "cdc", sbk], writes=["Sf"])
                  ob, obk = osb.next()
                  s_, sk_ = stt.next()
                  z, zk = zt2.next()
                  y, yk = yt2.next()
                  P.dma("sp", z[:], ZS[n * 128:(n + 1) * 128, 256:512], reads=allZS, writes=[zk], slot=zk)
                  P.op("act", lambda e, po=po, ob=ob: e.copy(out=ob[:], in_=po[:, 0:4, :]), reads=[pok], writes=[obk])
                  P.op("dve", lambda e, ob=ob, s_=s_: e.tensor_reduce(out=s_[:, 0:4], in_=ob[:], axis=AX.X, op=ALU.add), reads=[obk], writes=[sk_])
                  P.op("pool", lambda e, ob=ob: e.tensor_tensor(out=sqb[:], in0=ob[:], in1=ob[:], op=ALU.mult), reads=[obk], writes=["sqb"])
                  P.op("dve", lambda e, s_=s_: e.tensor_reduce(out=s_[:, 4:8], in_=sqb[:], axis=AX.X, op=ALU.add), reads=["sqb", sk_], writes=[sk_])
                  P.op("dve", lambda e, s_=s_: e.tensor_scalar(out=s_[:, 0:8], in0=s_[:, 0:8], scalar1=1.0 / 64, scalar2=None, op0=ALU.mult), reads=[sk_], writes=[sk_])
                  P.op("dve", lambda e, s_=s_: e.tensor_tensor(out=s_[:, 8:12], in0=s_[:, 0:4], in1=s_[:, 0:4], op=ALU.mult), reads=[sk_], writes=[sk_])
                  P.op("dve", lambda e, s_=s_: e.tensor_tensor(out=s_[:, 8:12], in0=s_[:, 4:8], in1=s_[:, 8:12], op=ALU.subtract), reads=[sk_], writes=[sk_])
                  P.op("dve", lambda e, s_=s_: e.tensor_scalar(out=s_[:, 8:12], in0=s_[:, 8:12], scalar1=1e-5, scalar2=None, op0=ALU.add), reads=[sk_], writes=[sk_])
                  P.op("pool", lambda e, s_=s_: e.tensor_tensor(out=s_[:, 12:16], in0=s_[:, 8:12], in1=mh[:], op=ALU.pow), reads=[sk_, "mh2"], writes=[sk_])
                  for hh in range(4):
                      P.op("dve", lambda e, ob=ob, s_=s_, hh=hh: e.tensor_scalar(out=ob[:, hh, :], in0=ob[:, hh, :], scalar1=s_[:, hh:hh + 1], scalar2=s_[:, 12 + hh:13 + hh],
                                                                              op0=ALU.subtract, op1=ALU.mult), reads=[sub(obk, hh), sk_], writes=[sub(obk, hh)])
                  P.op("pool", lambda e, ob=ob: e.tensor_tensor(out=ob[:].rearrange("p h d -> p (h d)"), in0=ob[:].rearrange("p h d -> p (h d)"), in1=gg[:], op=ALU.mult),
                       reads=[obk, "gg"], writes=[obk])
                  P.op("dve", lambda e, ob=ob, z=z, y=y: e.tensor_tensor(out=y[:], in0=ob[:].rearrange("p h d -> p (h d)"), in1=z[:], op=ALU.mult),
                       reads=[obk, zk], writes=[yk])
                  P.dma("pool", Yr[n * 128:(n + 1) * 128, :], y[:], reads=[yk], writes=[("Yr", n)], slot=yk)
        P.stack = st
        if cx is None:
            P.final_wait("sp", [("Y", hh, i) for hh in range(4) for i in range(NG)] + [("Yr", n) for n in range(NT)])
            P.emit()
    return nc


def consts_A(hh):
    k = np.arange(128)[:, None]
    q = np.arange(512)[None, :]
    cm = np.stack([np.where(128 * m + k <= q, 0.0, NEGM) for m in range(4)], axis=1).astype(NPBF)
    heads = np.arange(4) + 4 * hh
    lg = np.log(1.0 - 2.0 ** (-5.0 - heads.astype(np.float64)))
    i = np.arange(128)
    diff = i[None, :] - i[:, None]
    innerT = np.where(diff[None] >= 0, np.exp(lg[:, None, None] * np.maximum(diff, 0)[None]), 0.0)
    rinner = np.ascontiguousarray(innerT.transpose(1, 0, 2)).astype(np.float32)
    cross = np.exp(lg[:, None] * (i[None, :] + 1))
    rcross = np.ascontiguousarray(np.broadcast_to(cross[None, :, :], (64, 4, 128))).astype(np.float32)
    kdec = np.exp(lg[:, None] * (127 - i)[None, :])
    rkdec = np.ascontiguousarray(kdec.T).astype(np.float32)
    cdec = np.exp(lg * 128)
    rcdec = np.ascontiguousarray(np.broadcast_to(cdec[None, :], (64, 4))).astype(np.float32)
    return dict(cmask=cm, rinner=rinner, rcross=rcross, rkdec=rkdec, rcdec=rcdec, ident=np.eye(128, dtype=np.float32))


def inputs_A(x_b, norm_g, w_in, b_f, gn_g, hh):
    sl = lambda o, h0, n: np.arange(o + h0 * 64, o + (h0 + n) * 64)
    o_qf, o_kf, o_vf, o_fl, o_qr, o_kr, o_vr, o_z = 0, 512, 1024, 1536, 1544, 2056, 2568, 3080
    h0 = 4 * hh
    cols = np.concatenate([sl(o_qf, h0, 4), sl(o_kf, h0, 4), sl(o_qr, h0, 4), sl(o_kr, h0, 4),
                           np.arange(o_fl + h0, o_fl + h0 + 4),
                           sl(o_vf, h0, 4), sl(o_vr, h0, 4), sl(o_kr, h0, 4),
                           sl(o_z, h0, 4), sl(o_z + 512, h0, 4)])
    d = dict(x=np.ascontiguousarray(x_b), gcol=np.ascontiguousarray(norm_g.reshape(8, 128).T),
             wA=np.ascontiguousarray(w_in[:, cols]), bf=np.ascontiguousarray(b_f[h0:h0 + 4].reshape(4, 1)),
             gng=np.ascontiguousarray(np.broadcast_to(gn_g[h0 * 64:(h0 + 4) * 64][None, :], (128, 256))))
    d.update(consts_A(hh))
    return d


def build_O(T, final, cx=None, sfx="", io=None):
    NTT = T // 128
    nc = cx.nc if cx else bass.Bass("TRN2", target_bir_lowering=False)
    din = lambda n, sh, dt=F32: _din(nc, cx, sfx, n, sh, dt)
    y = io["y"] if io else din("y", [T, 1024], BF16)
    x = io["x"] if io else din("x", [T, 1024])
    w = din("w", [1024, 1024])
    gf = din("gf", [128, 1024])
    ident = None if cx else din("ident", [128, 128])
    out = io["out"] if io else nc.dram_tensor("out", [T, 1024], F32, kind="ExternalOutput").ap()
    with ExitStack() as st:
        P = cx.P if cx else Prog(nc, st)
        idb = cx.idb if cx else load_consts(P, ident)
        if cx:
            P.stack = st
            P.slot_prefix = sfx[:2]
            P.barrier()
        wb = P.sb("wb", [128, 8, 1024], BF16)
        wkeys = load_weight_bf16(P, wb, w, 1024, "wb")
        gft = P.sb("gft", [128, 1024], F32)
        mh = P.sb("mh", [128, 1], F32)
        if final:
            P.dma("sp", gft[:], gf, writes=["gft"], slot="gft")
            P.op("pool", lambda e: e.memset(mh[:], -0.5), writes=["mh"])
        yt = Rot(P, "yt", 3, [128, 1024], BF16)
        xt = Rot(P, "xt", 3, [128, 1024], F32)
        ot = Rot(P, "ot", 2, [128, 1024], F32)
        sq = P.sb("sq", [128, 1024], F32)
        stt = Rot(P, "stt", 2, [128, 4], F32)
        pT = Rot(P, "pT", 2, [128, 8, 128], BF16, psum=True)
        pO = Rot(P, "pO", 4, [128, 512], F32, psum=True)
        yT = Rot(P, "yT", 3, [128, 8, 128], BF16)

        def prep(t):
            r = slice(t * 128, (t + 1) * 128)
            yb, yk = yt.next()
            xb, xk = xt.next()
            P.dma("sp", yb[:], y[r, :], writes=[yk], slot=yk)
            P.dma("sp", xb[:], x[r, :], writes=[xk], slot=xk)
            pt, ptk = pT.next()
            for c in range(8):
                P.op("pe", lambda e, c=c, pt=pt, yb=yb: e.transpose(out=pt[:, c, :], in_=yb[:, c * 128:(c + 1) * 128], identity=idb[:]),
                     reads=[yk, "idb"], writes=[ptk])
            ytt, ytk = yT.next()
            P.op("act", lambda e, pt=pt, ytt=ytt: e.copy(out=ytt[:], in_=pt[:]), reads=[ptk], writes=[ytk])
            return (xb, xk, ytt, ytk)

        def mm(t, st_):
            xb, xk, ytt, ytk = st_
            r = slice(t * 128, (t + 1) * 128)
            ob, obk = ot.next()
            for hf in range(2):
                po, pok = pO.next()
                for c in range(8):
                    P.op("pe", lambda e, c=c, po=po, ytt=ytt, hf=hf: e.matmul(po[:], lhsT=ytt[:, c, :], rhs=wb[:, c, hf * 512:(hf + 1) * 512],
                                                                             start=(c == 0), stop=(c == 7)), reads=[ytk] + wkeys, writes=[pok])
                P.op("dve", lambda e, po=po, ob=ob, xb=xb, hf=hf: e.tensor_tensor(out=ob[:, hf * 512:(hf + 1) * 512], in0=po[:], in1=xb[:, hf * 512:(hf + 1) * 512],
                                                                                op=ALU.add), reads=[pok, xk], writes=[obk])
            if final:
                s_, sk_ = stt.next()
                P.op("dve", lambda e, ob=ob, s_=s_: e.scalar_tensor_tensor(out=sq[:], in0=ob[:], scalar=1.0, in1=ob[:], op0=ALU.mult, op1=ALU.mult,
                                                                         accum_out=s_[:, 0:1]), reads=[obk], writes=["sq", sk_])
                P.op("dve", lambda e, s_=s_: e.tensor_scalar(out=s_[:, 1:2], in0=s_[:, 0:1], scalar1=1.0 / 1024, scalar2=1e-6, op0=ALU.mult, op1=ALU.add),
                     reads=[sk_], writes=[sk_])
                P.op("pool", lambda e, s_=s_: e.tensor_tensor(out=s_[:, 2:3], in0=s_[:, 1:2], in1=mh[:], op=ALU.pow), reads=[sk_, "mh"], writes=[sk_])
                P.op("dve", lambda e, ob=ob, s_=s_: e.scalar_tensor_tensor(out=ob[:], in0=ob[:], scalar=s_[:, 2:3], in1=gft[:], op0=ALU.mult, op1=ALU.mult),
                     reads=[obk, sk_, "gft"], writes=[obk])
            P.dma("pool", out[r, :], ob[:], reads=[obk], writes=[("out", t)], slot=obk)

        nxt = prep(0)
        for t in range(NTT):
            cur = nxt
            if t + 1 < NTT:
                nxt = prep(t + 1)
            mm(t, cur)
        if cx is None or final:
            P.final_wait("sp", [("out", t) for t in range(NTT)])
        if cx is None:
            P.emit()
    return nc


def build_C(S, cx=None, sfx="", io=None):
    NT, NG = S // 128, S // 512
    NCMP = (S - 32) // 16 + 1
    NCT = max(1, S // 2048)
    nc = cx.nc if cx else bass.Bass("TRN2", target_bir_lowering=False)
    din = lambda n, sh, dt=F32: _din(nc, cx, sfx, n, sh, dt)
    x = io["x"] if io else din("x", [S, 1024])
    gcol = din("gcol", [128, 8])
    wC = din("wC", [1024, 1816])
    bg = din("bg", [128, 24])
    ident = None if cx else din("ident", [128, 128])
    peT = din("peT", [2, 64, 32])
    w1 = din("w1", [2, 2048, 256])
    w2 = din("w2", [2, 256, 64])
    AQ = din("AQ", [8, 9, S], BF16)
    AK = din("AK", [9, S], BF16)
    AKc = din("AKc", [9, 128 * NCT], BF16)
    cmask = din("cmask", [128, 4, 512], BF16)
    wmask = din("wmask", [128, 8, 512], BF16)
    cmk = din("cmk", [128, 5, 512], BF16)
    Mc = din("Mc", [128, NCT, 128], BF16)
    ADD = din("ADD", [S, 128])
    Ew = din("Ew", [128, S], BF16)
    Y = io["Y"] if io else nc.dram_tensor("Y", [S, 512], BF16, kind="ExternalOutput").ap()
    QKT = _dsc(nc, cx, "QKT", [1024, S], BF16)
    VT = _dsc(nc, cx, "VT2", [S, 256], BF16)
    ZS = _dsc(nc, cx, "ZS", [S, 512], F32)
    GL = _dsc(nc, cx, "GL", [S, 24], F32)
    OC = _dsc(nc, cx, "OC", [S, 512], F32)

    with ExitStack() as st:
        P = cx.P if cx else Prog(nc, st)
        idb = cx.idb if cx else load_consts(P, ident)
        if cx:
            P.stack = st
            P.slot_prefix = sfx[:2]
            P.barrier()
        KcT = [P.sb(f"KcT{g}", [73, 128 * NCT], BF16) for g in range(2)]
        VcA = [P.sb(f"VcA{g}", [128, NCT, 65], BF16) for g in range(2)]
        selT = [P.sb(f"selT{g}", [128, S], BF16) for g in range(2)]
        with ExitStack() as st1:
            P.stack = st1
            wb = P.sb("wb", [128, 8, 1816], BF16)
            wkeys = load_weight_bf16(P, wb, wC, 1816, "wb")
            nt = NormT(P, idb, gcol)
            hT = Rot(P, "hT", 2, [128, 8, 512], BF16)
            psA = Rot(P, "psA", 3, [128, 512], F32, psum=True)
            fmst = Rot(P, "fmst", 3, [128, 512], BF16)
            tmst = Rot(P, "tmst", 2, [128, 256], BF16)
            zst = Rot(P, "zst", 2, [128, 512], F32)
            gst = Rot(P, "gst", 2, [128, 24], F32)
            bgs = P.sb("bgs", [128, 24], F32)
            P.dma("sp", bgs[:], bg, writes=["bgs"], slot="bgs")
            def prep(g):
                h, hk = hT.next()
                xs = [nt.load(x[g * 512 + tt * 128:g * 512 + (tt + 1) * 128, :]) for tt in range(4)]
                nt.run4(xs, h, hk)
                return h, hk

            nxt = prep(0)
            for g in range(NG):
                h, hk = nxt
                if g + 1 < NG:
                    nxt = prep(g + 1)
                for ct in range(8):
                    ps, pk = psA.next()
                    for c in range(8):
                        P.op("pe", lambda e, ps=ps, c=c, ct=ct, h=h: e.matmul(ps[:], lhsT=wb[:, c, ct * 128:(ct + 1) * 128], rhs=h[:, c, :],
                                                                             start=(c == 0), stop=(c == 7)), reads=[hk] + wkeys, writes=[pk])
                    sg, sgk = fmst.next()
                    scl = 0.125 if ct < 4 else 1.0
                    if ct % 2 == 0:
                        P.op("act", lambda e, ps=ps, sg=sg, scl=scl: e.activation(out=sg[:], in_=ps[:], func=AF.Copy, scale=scl), reads=[pk], writes=[sgk])
                    else:
                        P.op("dve", lambda e, ps=ps, sg=sg, scl=scl: e.tensor_scalar(out=sg[:], in0=ps[:], scalar1=scl, scalar2=None, op0=ALU.mult),
                             reads=[pk], writes=[sgk])
                    P.dma("pool", QKT[ct * 128:(ct + 1) * 128, g * 512:(g + 1) * 512], sg[:], reads=[sgk], writes=[("QKT", ct, g)], slot=sgk)
                for tt in range(4):
                    r0 = g * 512 + tt * 128
                    tm, tmk = tmst.next()
                    zs, zk = zst.next()
                    gs, gk = gst.next()
                    ps, pk = psA.next()
                    for c in range(8):
                        P.op("pe", lambda e, ps=ps, c=c, h=h, tt=tt: e.matmul(ps[:], lhsT=h[:, c, tt * 128:(tt + 1) * 128], rhs=wb[:, c, 1024:1536],
                                                                             start=(c == 0), stop=(c == 7)), reads=[hk] + wkeys, writes=[pk])
                    P.op("dve", lambda e, ps=ps, tm=tm: e.tensor_copy(out=tm[:], in_=ps[:, 0:256]), reads=[pk], writes=[tmk])
                    P.op("act", lambda e, ps=ps, zs=zs: e.activation(out=zs[:, 0:256], in_=ps[:, 256:512], func=AF.Silu), reads=[pk], writes=[zk, pk])
                    ps, pk = psA.next()
                    for c in range(8):
                        P.op("pe", lambda e, ps=ps, c=c, h=h, tt=tt: e.matmul(ps[:, 0:280], lhsT=h[:, c, tt * 128:(tt + 1) * 128], rhs=wb[:, c, 1536:1816],
                                                                             start=(c == 0), stop=(c == 7)), reads=[hk] + wkeys, writes=[pk])
                    P.op("act", lambda e, ps=ps, zs=zs: e.activation(out=zs[:, 256:512], in_=ps[:, 0:256], func=AF.Silu), reads=[pk], writes=[zk])
                    P.op("dve", lambda e, ps=ps, gs=gs: e.tensor_tensor(out=gs[:], in0=ps[:, 256:280], in1=bgs[:], op=ALU.add), reads=[pk, "bgs"], writes=[gk, pk])
                    P.op("act", lambda e, gs=gs: e.activation(out=gs[:], in_=gs[:], func=AF.Sigmoid), reads=[gk], writes=[gk])
                    P.dma("pool", VT[r0:r0 + 128, :], tm[:], reads=[tmk], writes=[("VT", g)], slot=tmk)
                    P.dma("pool", ZS[r0:r0 + 128, :], zs[:], reads=[zk], writes=[("ZS", g)], slot=zk)
                    P.dma("pool", GL[r0:r0 + 128, :], gs[:], reads=[gk], writes=[("GL", g)], slot=gk)
        P.stack = st
        P.barrier()
        allQKT = [("QKT", ct, g) for ct in range(8) for g in range(NG)]
        allVT = [("VT", g) for g in range(NG)]
        allZS = [("ZS", g) for g in range(NG)]
        allGL = [("GL", g) for g in range(NG)]
        with ExitStack() as st2:
            P.stack = st2
            ATr = Rot(P, "AT", 2, [64, S], BF16)
            w1s = Rot(P, "w1s", 2, [64, 16, 256], F32)
            w1b = [P.sb(f"w1b{k}", [64, 32, 256], BF16) for k in range(2)]
            w2s = P.sb("w2s", [128, 2, 2, 64], F32)
            w2b = P.sb("w2b", [128, 2, 2, 64], BF16)
            pes = P.sb("pes", [64, 2, 32], F32)
            peb = P.sb("peb", [64, 2, 32], BF16)
            bias = P.sb("cbias", [128, 2, 2], F32)
            hidT = Rot(P, "hidT", 2, [128, 2, 512], BF16)
            psH = Rot(P, "psH", 2, [128, 512], F32, psum=True)
            psK = P.ps("psK", [128, 512], F32)
            psV = P.ps("psV", [128, 512], F32)
            psB = P.ps("psB", [128, 512], F32)
            for k in range(2):
                for half in range(2):
                    t, tk = w1s.next()
                    P.dma("sp", t[:], w1[k, half * 1024:(half + 1) * 1024, :].rearrange("(l d) h -> d l h", d=64), writes=[tk], slot=tk)
                    P.op("dve", lambda e, t=t, k=k, half=half: e.tensor_copy(out=w1b[k][:, half * 16:(half + 1) * 16, :], in_=t[:]), reads=[tk], writes=[f"w1b{k}"])
            P.dma("sp", w2s[:], w2.rearrange("k (c p) d -> p k c d", p=128), writes=["w2s"], slot="w2s")
            P.op("dve", lambda e: e.tensor_copy(out=w2b[:], in_=w2s[:]), reads=["w2s"], writes=["w2b"])
            P.dma("sp", pes[:], peT.rearrange("k d l -> d k l"), writes=["pes"], slot="pes")
            P.op("dve", lambda e: e.tensor_copy(out=peb[:], in_=pes[:]), reads=["pes"], writes=["peb"])
            for (t, tk) in hidT.items:
                P.op("pool", lambda e, t=t: e.memset(t[:], 0.0), writes=[tk])
            for g in range(2):
                P.op("pool", lambda e, g=g: e.memset(VcA[g][:, :, 64:65], 1.0), writes=[f"VcA{g}"])
            for k in range(2):
                for hh2 in range(2):
                    for l in range(32):
                        P.op("pe", lambda e, k=k, hh2=hh2, l=l: e.matmul(psB[:, (k * 2 + hh2):(k * 2 + hh2) + 1], lhsT=w1b[k][:, l, hh2 * 128:(hh2 + 1) * 128],
                                                                        rhs=peb[:, k, l:l + 1], start=(l == 0), stop=(l == 31)),
                             reads=[f"w1b{k}", "peb"], writes=["psB"])
            P.op("dve", lambda e: e.tensor_copy(out=bias[:].rearrange("p a b -> p (a b)"), in_=psB[:, 0:4]), reads=["psB"], writes=["cbias"])
            for g in range(2):
                for k in range(2):
                    AT, atk = ATr.next()
                    row0 = (512 if k == 0 else 640) + g * 64
                    P.dma("sp", AT[:], QKT[row0:row0 + 64, :], reads=allQKT, writes=[atk], slot=atk)
                    hd, hdk = hidT.next()
                    for hh2 in range(2):
                        ps, pk = psH.next()
                        for l in range(32):
                            P.op("pe", lambda e, ps=ps, k=k, hh2=hh2, l=l, AT=AT: e.matmul(ps[:, 0:NCMP], lhsT=w1b[k][:, l, hh2 * 128:(hh2 + 1) * 128],
                                                                                        rhs=AT[:, l:l + 16 * (NCMP - 1) + 1:16], start=(l == 0), stop=(l == 31)),
                                 reads=[f"w1b{k}", atk], writes=[pk])
                        P.op("act", lambda e, ps=ps, hd=hd, hh2=hh2, k=k: e.activation(out=hd[:, hh2, 0:NCMP], in_=ps[:, 0:NCMP], func=AF.Silu, bias=bias[:, k, hh2:hh2 + 1]),
                             reads=[pk, "cbias"], writes=[hdk])
                    if k == 0:
                        for hh2 in range(2):
                            P.op("pe", lambda e, hd=hd, hh2=hh2: e.matmul(psK[0:64, 0:128 * NCT], lhsT=w2b[:, 0, hh2, :], rhs=hd[:, hh2, 0:128 * NCT], start=(hh2 == 0), stop=(hh2 == 1)),
                                 reads=[hdk, "w2b"], writes=["psK"])
                        P.op("dve", lambda e, g=g: e.tensor_copy(out=KcT[g][0:64, :], in_=psK[0:64, 0:128 * NCT]), reads=["psK"], writes=[f"KcT{g}"])
                        P.dma("sp", KcT[g][64:73, :], AKc, writes=[f"KcT{g}"], slot=f"KcT{g}")
                    else:
                        for ct in range(NCT):
                            for hh2 in range(2):
                                P.op("pe", lambda e, hd=hd, hh2=hh2, ct=ct: e.matmul(psV[:, ct * 64:(ct + 1) * 64], lhsT=hd[:, hh2, ct * 128:(ct + 1) * 128], rhs=w2b[:, 1, hh2, :],
                                                                                    start=(hh2 == 0), stop=(hh2 == 1)), reads=[hdk, "w2b"], writes=["psV"])
                        P.op("dve", lambda e, g=g: e.tensor_copy(out=VcA[g][:, :, 0:64], in_=psV[:, 0:64 * NCT].rearrange("p (c d) -> p c d", d=64)),
                             reads=["psV"], writes=[f"VcA{g}"])
        P.stack = st
        P.barrier()
        with ExitStack() as st3:
            P.stack = st3
            ac = AttnCore(P, idb)
            psI = Rot(P, "psI", 2, [128, 4, 128], F32, psum=True)
            psT = P.ps("psT", [128, 8, 128], BF16)
            cmks = P.sb("cmks", [128, 5, 512], BF16)
            Ms = P.sb("Ms", [128, NCT, 128], BF16)
            P.dma("sp", cmks[:], cmk, writes=["cmks"], slot="cmks")
            P.dma("sp", Ms[:], Mc, writes=["Ms"], slot="Ms")
            QTb = Rot(P, "QT", 4, [73, 512], BF16)
            glt = Rot(P, "glt", 2, [128, 4, 24], F32)
            adt = Rot(P, "adt", 2, [128, 4, 128], F32)
            oct_ = Rot(P, "oct", 2, [128, 4, 64], F32)
            rv = Rot(P, "rv", 2, [128, 4], F32)
            imp = P.sb("imp", [128, 4, 128], F32)
            wk = P.sb("wk", [128, 4, 128], F32)
            m8 = P.sb("m8", [128, 4, 16], F32)
            sb16 = P.sb("sb16", [128, 4, 128], BF16)
            for g in range(2):
                for i in range(NG):
                    gl_, glk = glt.next()
                    ad, adk = adt.next()
                    P.dma("sp", gl_[:], GL[i * 512:(i + 1) * 512, :].rearrange("(q p) c -> p q c", p=128), reads=allGL, writes=[glk], slot=glk)
                    P.dma("sp", ad[:], ADD[i * 512:(i + 1) * 512, :].rearrange("(q p) c -> p q c", p=128), writes=[adk], slot=adk)
                    jcs = list(range(0, min(i // 4, NCT - 1) + 1))
                    for hl in range(4):
                        hq = g * 4 + hl
                        QT, qk = QTb.next()
                        P.dma("sp", QT[0:64, :], QKT[hq * 64:(hq + 1) * 64, i * 512:(i + 1) * 512], reads=allQKT, writes=[qk], slot=qk)
                        P.dma("sp", QT[64:73, :], AQ[hq, :, i * 512:(i + 1) * 512], writes=[qk], slot=qk)
                        tiles = []
                        for jc in jcs:
                            dd = (512 * i - 2048 * jc) // 512
                            ex = [(idb[:], cmks[:, dd, :], ["idb", "cmks"])] if dd <= 4 else []
                            tiles.append(dict(j=jc, extras=ex, qs_min=0))
                        pI, pik = psI.next()

                        def pv_extra(ti, t, pT, pk, pI=pI, pik=pik, ntl=len(tiles)):
                            for qs in range(4):
                                P.op("pe", lambda e, pI=pI, pT=pT, qs=qs, jc=t["j"], st_=(ti == 0 and qs == 0), sp_=(ti == ntl - 1):
                                     e.matmul(pI[:, qs, :], lhsT=pT[:, qs * 128:(qs + 1) * 128], rhs=Ms[:, jc, :], start=st_, stop=sp_, skip_group_check=True),
                                     reads=[pk, "Ms"], writes=[pik])
                        psO, ok = ac.group(QT, [qk], KcT[g], [f"KcT{g}"], 73, 0, tiles, lambda j, g=g: (VcA[g][:, j, :], [f"VcA{g}"]), pv_extra=pv_extra)
                        r, rk = rv.next()
                        oc, ock = oct_.next()
                        P.op("dve", lambda e, psO=psO, r=r: e.tensor_scalar(out=r[:], in0=psO[:, 64:260:65], scalar1=1e-30, scalar2=None, op0=ALU.max), reads=[ok], writes=[rk])
                        P.op("dve", lambda e, r=r: e.reciprocal(out=r[:], in_=r[:]), reads=[rk], writes=[rk])
                        for qs in range(4):
                            P.op("dve", lambda e, psO=psO, r=r, oc=oc, gl_=gl_, qs=qs, hq=hq: e.tensor_scalar(
                                out=oc[:, qs, :], in0=psO[:, qs * 65:qs * 65 + 64], scalar1=r[:, qs:qs + 1], scalar2=gl_[:, qs, hq * 3:hq * 3 + 1], op0=ALU.mult, op1=ALU.mult),
                                reads=[ok, rk, glk], writes=[sub(ock, qs)])
                        P.dma("pool", OC[i * 512:(i + 1) * 512, hq * 64:(hq + 1) * 64].rearrange("(q p) d -> p q d", p=128), oc[:], reads=[ock], writes=[("OC", hq, i)], slot=ock)
                        for qs in range(4):
                            if hl == 0:
                                P.op("dve", lambda e, pI=pI, r=r, qs=qs: e.tensor_scalar(out=imp[:, qs, :], in0=pI[:, qs, :], scalar1=r[:, qs:qs + 1], scalar2=None, op0=ALU.mult),
                                     reads=[pik, rk], writes=[sub("imp", qs)])
                            else:
                                P.op("dve", lambda e, pI=pI, r=r, qs=qs: e.scalar_tensor_tensor(out=imp[:, qs, :], in0=pI[:, qs, :], scalar=r[:, qs:qs + 1], in1=imp[:, qs, :],
                                                                                              op0=ALU.mult, op1=ALU.add), reads=[pik, rk, sub("imp", qs)], writes=[sub("imp", qs)])
                    P.op("dve", lambda e, ad=ad: e.tensor_tensor(out=imp[:], in0=imp[:], in1=ad[:], op=ALU.add), reads=["imp", adk], writes=["imp"])
                    for qs in range(4):
                        P.op("dve", lambda e, qs=qs: e.max(out=m8[:, qs, 0:8], in_=imp[:, qs, :]), reads=[sub("imp", qs)], writes=[sub("m8", qs)])
                    for qs in range(4):
                        P.op("dve", lambda e, qs=qs: e.match_replace(out=wk[:, qs, :], in_to_replace=m8[:, qs, 0:8], in_values=imp[:, qs, :], imm_value=-3.0e38),
                             reads=[sub("imp", qs), sub("m8", qs)], writes=[sub("wk", qs)])
                    for qs in range(4):
                        P.op("dve", lambda e, qs=qs: e.max(out=m8[:, qs, 8:16], in_=wk[:, qs, :]), reads=[sub("wk", qs)], writes=[sub("m8", qs)])
                    for qs in range(4):
                        P.op("dve", lambda e, qs=qs: e.tensor_scalar(out=wk[:, qs, :], in0=imp[:, qs, :], scalar1=m8[:, qs, 15:16], scalar2=None, op0=ALU.is_ge),
                             reads=[sub("imp", qs), sub("m8", qs)], writes=[sub("wk", qs)])
                    P.op("dve", lambda e: e.tensor_scalar(out=sb16[:], in0=wk[:], scalar1=-1.0, scalar2=-NEGM, op0=ALU.add, op1=ALU.mult), reads=["wk"], writes=["sb16"])
                    for qs in range(4):
                        P.op("pe", lambda e, qs=qs: e.transpose(out=psT[:, qs, :], in_=sb16[:, qs, :], identity=idb[:]), reads=["sb16", "idb"], writes=["psT"])
                    P.op("act", lambda e, g=g, i=i: e.copy(out=selT[g][:, i * 512:(i + 1) * 512], in_=psT[:, 0:4, :].rearrange("p a b -> p (a b)")),
                         reads=["psT"], writes=[("selT", g, i)])
        P.stack = st
        P.barrier()
        allOC = [("OC", hq, i) for hq in range(8) for i in range(NG)]
        with ExitStack() as st4:
            P.stack = st4
            ac = AttnCore(P, idb, nO=3, skew=4)
            cm = P.sb("cm", [128, 4, 512], BF16)
            wm = P.sb("wm", [128, 8, 512], BF16)
            Ews = P.sb("Ews", [128, S], BF16)
            P.dma("sp", cm[:], cmask, writes=["cm"], slot="cm")
            P.dma("sp", wm[:], wmask, writes=["wm"], slot="wm")
            P.dma("sp", Ews[:], Ew, writes=["Ews"], slot="Ews")
            QTb = Rot(P, "QT", 2, [73, S], BF16)
            KsT = P.sb("KsT", [73, S], BF16)
            KwT = P.sb("KwT", [73, S], BF16)
            VsA = P.sb("VsA", [128, NT, 65], BF16)
            VwA = P.sb("VwA", [128, NT, 65], BF16)
            P.op("pool", lambda e: e.memset(VsA[:, :, 64:65], 1.0), writes=["VsA"])
            P.op("pool", lambda e: e.memset(VwA[:, :, 64:65], 1.0), writes=["VwA"])
            glt = Rot(P, "glt", 2, [128, 4, 24], F32)
            zt = Rot(P, "zt", 2, [128, 4, 64], F32)
            oct_ = Rot(P, "oc4", 2, [128, 4, 64], F32)
            acc = Rot(P, "acc", 2, [128, 4, 64], F32)
            yt = Rot(P, "yt", 2, [128, 4, 64], BF16)
            rv = Rot(P, "rv", 2, [128, 8], F32)
            for g in range(2):
                P.dma("sp", KsT[0:64, :], QKT[768 + g * 64:768 + (g + 1) * 64, :], reads=allQKT, writes=["KsT"], slot="KsT")
                P.dma("sp", KsT[64:73, :], AK, writes=["KsT"], slot="KsT")
                P.dma("sp", KwT[0:64, :], QKT[896 + g * 64:896 + (g + 1) * 64, :], reads=allQKT, writes=["KwT"], slot="KwT")
                P.dma("sp", KwT[64:73, :], AK, writes=["KwT"], slot="KwT")
                P.dma("sp", VsA[:, :, 0:64], VT[:, g * 64:(g + 1) * 64].rearrange("(n p) d -> p n d", p=128), reads=allVT, writes=["VsA"], slot="VsA")
                P.dma("sp", VwA[:, :, 0:64], VT[:, 128 + g * 64:128 + (g + 1) * 64].rearrange("(n p) d -> p n d", p=128), reads=allVT, writes=["VwA"], slot="VwA")
                selk = [("selT", g, i) for i in range(NG)]
                for hl in range(4):
                    hq = g * 4 + hl
                    QT, qk = QTb.next()
                    P.dma("sp", QT[0:64, :], QKT[hq * 64:(hq + 1) * 64, :], reads=allQKT, writes=[qk], slot=qk)
                    P.dma("sp", QT[64:73, :], AQ[hq], writes=[qk], slot=qk)
                    for i in range(NG):
                        tiles = causal_tiles(i, lambda m: (idb[:], cm[:, m, :], ["idb", "cm"]),
                                             extra_fn=lambda j, i=i, g=g: [(Ews[:, j * 128:(j + 1) * 128], selT[g][:, i * 512:(i + 1) * 512], ["Ews", ("selT", g, i)])])
                        psOs, oks = ac.group(QT, [qk], KsT, ["KsT"], 73, i * 512, tiles, lambda j: (VsA[:, j, :], ["VsA"]))
                        wt = []
                        for m in range(-4, 4):
                            j = 4 * i + m
                            if j < 0:
                                continue
                            wt.append(dict(j=j, extras=[(idb[:], wm[:, m + 4, :], ["idb", "wm"])], qs_min=max(m, 0), qs_max=min(4, m + 5)))
                        psOw, okw = ac.group(QT, [qk], KwT, ["KwT"], 73, i * 512, wt, lambda j: (VwA[:, j, :], ["VwA"]))
                        gl_, glk = glt.next()
                        z, zk = zt.next()
                        oc, ock = oct_.next()
                        a_, ak_ = acc.next()
                        y, yk = yt.next()
                        r, rk = rv.next()
                        P.dma("sp", gl_[:], GL[i * 512:(i + 1) * 512, :].rearrange("(q p) c -> p q c", p=128), reads=allGL, writes=[glk], slot=glk)
                        P.dma("sp", z[:], ZS[i * 512:(i + 1) * 512, hq * 64:(hq + 1) * 64].rearrange("(q p) d -> p q d", p=128), reads=allZS, writes=[zk], slot=zk)
                        P.dma("sp", oc[:], OC[i * 512:(i + 1) * 512, hq * 64:(hq + 1) * 64].rearrange("(q p) d -> p q d", p=128), reads=allOC, writes=[ock], slot=ock)
                        P.op("dve", lambda e, psOs=psOs, r=r: e.reciprocal(out=r[:, 0:4], in_=psOs[:, 64:260:65]), reads=[oks], writes=[rk])
                        P.op("dve", lambda e, psOw=psOw, r=r: e.reciprocal(out=r[:, 4:8], in_=psOw[:, 64:260:65]), reads=[okw, rk], writes=[rk])
                        P.op("dve", lambda e, r=r, gl_=gl_, hq=hq: e.tensor_tensor(out=r[:, 0:4], in0=r[:, 0:4], in1=gl_[:, :, hq * 3 + 1], op=ALU.mult), reads=[rk, glk], writes=[rk])
                        P.op("dve", lambda e, r=r, gl_=gl_, hq=hq: e.tensor_tensor(out=r[:, 4:8], in0=r[:, 4:8], in1=gl_[:, :, hq * 3 + 2], op=ALU.mult), reads=[rk, glk], writes=[rk])
                        for qs in range(4):
                            P.op("dve", lambda e, psOs=psOs, r=r, a_=a_, oc=oc, qs=qs: e.scalar_tensor_tensor(
                                out=a_[:, qs, :], in0=psOs[:, qs * 65:qs * 65 + 64], scalar=r[:, qs:qs + 1], in1=oc[:, qs, :], op0=ALU.mult, op1=ALU.add),
                                reads=[oks, rk, ock], writes=[sub(ak_, qs)])
                        for qs in range(4):
                            P.op("dve", lambda e, psOw=psOw, r=r, a_=a_, qs=qs: e.scalar_tensor_tensor(
                                out=a_[:, qs, :], in0=psOw[:, qs * 65:qs * 65 + 64], scalar=r[:, 4 + qs:5 + qs], in1=a_[:, qs, :], op0=ALU.mult, op1=ALU.add),
                                reads=[okw, rk, sub(ak_, qs)], writes=[sub(ak_, qs)])
                        P.op("pool", lambda e, a_=a_, z=z, y=y: e.tensor_tensor(out=y[:], in0=a_[:], in1=z[:], op=ALU.mult), reads=[ak_, zk], writes=[yk])
                        P.dma("pool", Y[i * 512:(i + 1) * 512, hq * 64:(hq + 1) * 64].rearrange("(q p) d -> p q d", p=128), y[:], reads=[yk], writes=[("Y", hq, i)], slot=yk)
        P.stack = st
        if cx is None:
            P.final_wait("sp", [("Y", hq, i) for hq in range(8) for i in range(NG)])
            P.emit()
    return nc


def _split3(v):
    v = np.asarray(v, np.float64)
    hi = v.astype(NPBF)
    r1 = v - hi.astype(np.float64)
    mid = r1.astype(NPBF)
    r2 = r1 - mid.astype(np.float64)
    lo = r2.astype(NPBF)
    return hi, mid, lo


def consts_C(S, hh):
    NCMP = (S - 32) // 16 + 1
    NCT = max(1, S // 2048)
    k = np.arange(128)[:, None]
    q = np.arange(512)[None, :]
    cm = np.stack([np.where(128 * m + k <= q, 0.0, NEGM) for m in range(4)], axis=1).astype(NPBF)
    wm = np.stack([np.where((128 * m + k <= q) & (128 * m + k > q - 512), 0.0, NEGM) for m in range(-4, 4)], axis=1).astype(NPBF)
    cmk = np.stack([np.where(16 * k + 31 <= 512 * dd + q, 0.0, NEGM) for dd in range(5)], axis=1).astype(NPBF)
    t = np.arange(S, dtype=np.float64)
    AQ = np.zeros((8, 9, S), NPBF)
    for hl in range(8):
        h = 8 * hh + hl
        slope = np.float64(np.float32(2.0 ** (-8.0 * (h + 1) / 16)))
        a, b, c = _split3(-slope * t)
        s1, s2, s3 = _split3(np.full(S, slope))
        AQ[hl] = np.stack([a, b, c, s1, s1, s2, s2, s3, s3])
    def krows(pos):
        pos = np.asarray(pos, np.float64)
        pa = (np.floor(pos / 128) * 128).astype(NPBF)
        pb = (pos % 128).astype(NPBF)
        one = np.ones(len(pos), NPBF)
        return np.stack([one, one, one, pa, pb, pa, pb, pa, pb])
    AK = krows(np.arange(S))
    AKc = krows(16 * np.arange(128 * NCT) + 31)
    ns = S // 64
    c0 = np.arange(128 * NCT)[:, None] * 16
    s0 = np.arange(128)[None, :] * 64
    ov = np.clip(np.minimum(c0 + 32, s0 + 64) - np.maximum(c0, s0), 0, None) / 16.0
    ov[NCMP:, :] = 0
    ov[:, ns:] = 0
    Mc = np.ascontiguousarray(ov.reshape(NCT, 128, 128).transpose(1, 0, 2)).astype(NPBF)
    tt = np.arange(S)[:, None]
    blk = np.arange(128)[None, :]
    cur = tt // 64
    valid = blk * 64 <= tt
    forced = (blk == 0) | (blk == cur) | (blk == cur - 1)
    ADD = np.where(valid, np.where(forced, 1e6, 0.0), -1e30).astype(np.float32)
    Ew = (np.arange(128)[:, None] == (np.arange(S)[None, :] // 64)).astype(NPBF)
    return dict(AQ=AQ, AK=AK, AKc=AKc, cmask=cm, wmask=wm, cmk=cmk, Mc=Mc, ADD=ADD, Ew=Ew, ident=np.eye(128, dtype=np.float32))


def inputs_C(x_b, norm_g, w_in, b_gate, pe_k, pe_v, wk1, wk2, wv1, wv2, hh, S):
    o_q, o_kc, o_vc, o_ks, o_vs, o_kw, o_vw, o_gl, o_z = 0, 1024, 1280, 1536, 1792, 2048, 2304, 2560, 2608
    g0 = 2 * hh
    gsl = lambda o: np.arange(o + g0 * 64, o + (g0 + 2) * 64)
    cols = np.concatenate([np.arange(o_q + hh * 512, o_q + (hh + 1) * 512), gsl(o_kc), gsl(o_vc), gsl(o_ks), gsl(o_kw),
                           gsl(o_vs), gsl(o_vw), np.arange(o_z + hh * 512, o_z + (hh + 1) * 512),
                           np.arange(o_gl + hh * 24, o_gl + (hh + 1) * 24)])
    d = dict(x=np.ascontiguousarray(x_b), gcol=np.ascontiguousarray(norm_g.reshape(8, 128).T), wC=np.ascontiguousarray(w_in[:, cols]),
             bg=np.ascontiguousarray(np.broadcast_to(b_gate[hh * 24:(hh + 1) * 24][None, :], (128, 24))),
             peT=np.ascontiguousarray(np.stack([pe_k.T, pe_v.T])), w1=np.ascontiguousarray(np.stack([wk1, wv1])), w2=np.ascontiguousarray(np.stack([wk2, wv2])))
    d.update(consts_C(S, hh))
    return d


def build_F(S):
    nc = bass.Bass("TRN2", target_bir_lowering=False)
    x = nc.dram_tensor("x", [S, 1024], F32, kind="ExternalInput").ap()
    ident = nc.dram_tensor("ident", [128, 128], F32, kind="ExternalInput").ap()
    out = nc.dram_tensor("out", [S, 1024], F32, kind="ExternalOutput").ap()
    Y1 = nc.dram_tensor("Y1", [S, 1024], BF16, kind="Internal").ap()
    X1 = nc.dram_tensor("X1", [S, 1024], F32, kind="Internal").ap()
    Y2 = nc.dram_tensor("Y2", [S, 1024], BF16, kind="Internal").ap()
    with ExitStack() as st:
        P = Prog(nc, st)
        idb = load_consts(P, ident)
        cx = Ctx(nc, P, idb)
        for hh in range(2):
            build_A(S, cx=cx, sfx="_a%d" % hh, io=dict(x=x, Yf=Y1[:, hh * 256:(hh + 1) * 256], Yr=Y1[:, 512 + hh * 256:512 + (hh + 1) * 256]))
        build_O(S, False, cx=cx, sfx="_b", io=dict(y=Y1, x=x, out=X1))
        for hh in range(2):
            build_C(S, cx=cx, sfx="_c%d" % hh, io=dict(x=X1, Y=Y2[:, hh * 512:(hh + 1) * 512]))
        build_O(S, True, cx=cx, sfx="_d", io=dict(y=Y2, x=X1, out=out))
        P.stack = st
        P.emit()
    return nc


def inputs_F(xb, p, S):
    f32 = lambda a: np.ascontiguousarray(np.asarray(a, dtype=np.float32))
    m = dict(x=np.ascontiguousarray(xb), ident=np.eye(128, dtype=np.float32))
    gfin = np.ascontiguousarray(np.broadcast_to(f32(p["final_g"])[None, :], (128, 1024)))
    for hh in range(2):
        d = inputs_A(xb, f32(p["even_norm_g"])[0], f32(p["even_w_in"])[0], f32(p["even_b_f"])[0], f32(p["even_gn_g"])[0], hh)
        for k, v in d.items():
            if k not in ("x", "ident"):
                m[k + "_a%d" % hh] = v
        d = inputs_C(xb, f32(p["odd_norm_g"])[0], f32(p["odd_w_in"])[0], f32(p["odd_b_gate"])[0], f32(p["odd_pe_k"])[0], f32(p["odd_pe_v"])[0],
                     f32(p["odd_wk1"])[0], f32(p["odd_wk2"])[0], f32(p["odd_wv1"])[0], f32(p["odd_wv2"])[0], hh, S)
        for k, v in d.items():
            if k not in ("x", "ident"):
                m[k + "_c%d" % hh] = v
    m["w_b"] = f32(p["even_w_out"])[0]
    m["gf_b"] = gfin
    m["w_d"] = f32(p["odd_w_out"])[0]
    m["gf_d"] = gfin
    return m


_NC_CACHE = {}


def _get_nc(name, fn):
    if name not in _NC_CACHE:
        _NC_CACHE[name] = fn()
    return _NC_CACHE[name]


def _run(nc, maps):
    return run_bass_kernel_spmd(nc, maps, core_ids=list(range(len(maps)))).results


def _assemble_y(res, S):
    shards = []
    for b in range(BATCH):
        y0 = np.asarray(res[2 * b]["Y"])
        y1 = np.asarray(res[2 * b + 1]["Y"])
        shards.append((y0, y1))
    return shards


def kernel(x, even_norm_g, even_w_in, even_b_f, even_gn_g, even_w_out,
           odd_norm_g, odd_w_in, odd_b_gate, odd_pe_k, odd_pe_v, odd_wk1, odd_wk2, odd_wv1, odd_wv2, odd_w_out, final_g):
    f32 = lambda a: np.ascontiguousarray(np.asarray(a, dtype=np.float32))
    x = f32(x)
    S = x.shape[1]
    p = dict(even_norm_g=even_norm_g, even_w_in=even_w_in, even_b_f=even_b_f, even_gn_g=even_gn_g, even_w_out=even_w_out,
             odd_norm_g=odd_norm_g, odd_w_in=odd_w_in, odd_b_gate=odd_b_gate, odd_pe_k=odd_pe_k, odd_pe_v=odd_pe_v,
             odd_wk1=odd_wk1, odd_wk2=odd_wk2, odd_wv1=odd_wv1, odd_wv2=odd_wv2, odd_w_out=odd_w_out, final_g=final_g)
    ncF = _get_nc(("F", S), lambda: build_F(S))
    per_b = [inputs_F(x[b], p, S) for b in range(BATCH)]
    maps = [per_b[c // 2] for c in range(8)]
    res = _run(ncF, maps)
    out = np.stack([np.asarray(res[2 * b]["out"]) for b in range(BATCH)])
    return out.astype(np.float32)
```
